# Optimizing a Trainium2 kernel written in Bass

```python
import jax, jax.numpy as jnp
from jax import lax
import numpy as np

D_MODEL = 2048
BATCH = 4
SEQ = 2048
DEPTH = 1
DEC_BATCH = 1
DEC_SEQ = 16384
PAST_LEN = 128

N_META = 16
GRID_W = 64
CHUNK = 128
Q_BLOCK = 128
MIX_W = D_MODEL
RET_HEADS = 4
RET_V = MIX_W // 2
RET_DV = RET_V // RET_HEADS
RET_DK = RET_DV // 2
RET_QK = RET_HEADS * RET_DK
ATT_DH = 128
ATT_HEADS = (MIX_W - RET_V) // ATT_DH
ATT_KV_HEADS = ATT_HEADS // 4
ATT_Q = ATT_HEADS * ATT_DH
ATT_KV = ATT_KV_HEADS * ATT_DH
D_FF = 4 * D_MODEL
ROPE_THETA = 10000.0
EPS = 1e-6
IN_SPLITS = (RET_QK, RET_QK, RET_V, RET_V, ATT_Q, ATT_KV, ATT_KV)
IN_W = sum(IN_SPLITS)
SPLIT_IDX = [int(i) for i in np.cumsum(IN_SPLITS)[:-1]]

kernel_name = "hymba_retention_axial_gqa_encoder"


def rms_norm(x, g=None):
    xf = x.astype(jnp.float32)
    y = xf * lax.rsqrt(jnp.mean(xf * xf, axis=-1, keepdims=True) + EPS)
    if g is not None:
        y = y * g.astype(jnp.float32)
    return y.astype(x.dtype)


def rope_pairs(x, ang):
    xf = x.astype(jnp.float32)
    x1, x2 = xf[..., 0::2], xf[..., 1::2]
    c, s = jnp.cos(ang), jnp.sin(ang)
    y = jnp.stack([x1 * c - x2 * s, x1 * s + x2 * c], axis=-1).reshape(x.shape)
    return y.astype(x.dtype)


def axial_angles(n_tokens):
    rows_n = n_tokens // GRID_W
    row = jnp.repeat(jnp.arange(rows_n), GRID_W).astype(jnp.float32)
    col = jnp.tile(jnp.arange(GRID_W), rows_n).astype(jnp.float32)
    n_pair = ATT_DH // 4
    freq = ROPE_THETA ** (-jnp.arange(n_pair, dtype=jnp.float32) / n_pair)
    ang = jnp.concatenate([row[:, None] * freq[None], col[:, None] * freq[None]], axis=-1)
    return jnp.concatenate([jnp.zeros((N_META, ATT_DH // 2), jnp.float32), ang], axis=0)


def retention_dir(q, k, v, log_g, inclusive):
    B, H, Lp, dk = q.shape
    dv = v.shape[-1]
    n = Lp // CHUNK
    f32 = jnp.float32
    qc = q.reshape(B, H, n, CHUNK, dk).astype(f32)
    kc = k.reshape(B, H, n, CHUNK, dk).astype(f32)
    vc = v.reshape(B, H, n, CHUNK, dv).astype(f32)
    lg = log_g.astype(f32)
    idx = jnp.arange(CHUNK, dtype=f32)
    diff = idx[:, None] - idx[None, :]
    mask = (diff >= 0) if inclusive else (diff > 0)
    dmat = jnp.where(mask[None], jnp.exp(lg[:, None, None] * jnp.maximum(diff, 0.0)[None]), 0.0)
    scores = jnp.einsum('bhncd,bhnsd->bhncs', qc, kc) * dmat[:, None]
    inner = jnp.einsum('bhncs,bhnsv->bhncv', scores, vc)
    k_dec = jnp.exp(lg[:, None] * (CHUNK - 1 - idx)[None])
    kv = jnp.einsum('bhncd,bhncv,hc->nbhdv', kc, vc, k_dec)
    chunk_decay = jnp.exp(lg * CHUNK)[None, :, None, None]

    def step(s, kv_n):
        return s * chunk_decay + kv_n, s

    _, s_prev = lax.scan(step, jnp.zeros((B, H, dk, dv), f32), kv)
    q_dec = jnp.exp(lg[:, None] * (idx + 1.0)[None])
    cross = jnp.einsum('bhncd,hc,nbhdv->bhncv', qc, q_dec, s_prev)
    return (inner + cross).reshape(B, H, Lp, dv)


def retention_group(q, k, v, g, log_g_fwd, log_g_bwd):
    B, L, _ = q.shape
    pad = CHUNK - N_META

    def heads(t, d):
        t = t.reshape(B, L, RET_HEADS, d).transpose(0, 2, 1, 3)
        return jnp.pad(t, ((0, 0), (0, 0), (pad, 0), (0, 0)))

    qh = heads(q, RET_DK)
    kh = heads(k, RET_DK) * (RET_DK ** -0.5)
    vh = heads(v, RET_DV)
    lp = L + pad
    pos = jnp.arange(lp, dtype=jnp.float32)
    freq = ROPE_THETA ** (-jnp.linspace(0.0, 1.0, RET_DK // 2, dtype=jnp.float32))
    ang = pos[:, None] * freq[None]
    qh = rope_pairs(qh, ang)
    kh = rope_pairs(kh, ang)
    fwd = retention_dir(qh, kh, vh, log_g_fwd, True)
    rev = lambda t: jnp.flip(t, axis=2)
    bwd = rev(retention_dir(rev(qh), rev(kh), rev(vh), log_g_bwd, False))
    o = rms_norm((fwd + bwd)[:, :, pad:])
    o = o.transpose(0, 2, 1, 3).reshape(B, L, RET_V)
    return (o * jax.nn.silu(g.astype(jnp.float32))).astype(q.dtype)


def attention_group(q, k, v, q_g, k_g, ang):
    B, L, _ = q.shape
    G = ATT_HEADS // ATT_KV_HEADS
    qh = q.reshape(B, L, ATT_KV_HEADS, G, ATT_DH).transpose(0, 2, 3, 1, 4)
    kh = k.reshape(B, L, ATT_KV_HEADS, ATT_DH).transpose(0, 2, 1, 3)
    vh = v.reshape(B, L, ATT_KV_HEADS, ATT_DH).transpose(0, 2, 1, 3)
    qh = rope_pairs(rms_norm(qh, q_g), ang) * (ATT_DH ** -0.5)
    kh = rope_pairs(rms_norm(kh, k_g), ang)

    def attend(qb):
        s = jnp.einsum('bkgqd,bksd->bkgqs', qb, kh).astype(jnp.float32)
        p = jax.nn.softmax(s, axis=-1)
        return jnp.einsum('bkgqs,bksd->bkgqd', p.astype(vh.dtype), vh)

    o_meta = attend(qh[:, :, :, :N_META])
    S = L - N_META
    nb = S // Q_BLOCK
    qb = qh[:, :, :, N_META:].reshape(B, ATT_KV_HEADS, G, nb, Q_BLOCK, ATT_DH).transpose(3, 0, 1, 2, 4, 5)
    o_real = lax.map(attend, qb)
    o_real = o_real.transpose(1, 2, 3, 0, 4, 5).reshape(B, ATT_KV_HEADS, G, S, ATT_DH)
    o = jnp.concatenate([o_meta, o_real], axis=3)
    return o.transpose(0, 3, 1, 2, 4).reshape(B, L, ATT_Q)


def encoder_layer(h, ang, ln1, w_in, q_g, k_g, dec_f, dec_b, w_out, ln2, w_up, w_down):
    u = rms_norm(h, ln1)
    proj = jnp.einsum('bld,de->ble', u, w_in)
    rq, rk, rv, rg, aq, ak, av = jnp.split(proj, SPLIT_IDX, axis=-1)
    mix = jnp.concatenate([retention_group(rq, rk, rv, rg, dec_f, dec_b),
                           attention_group(aq, ak, av, q_g, k_g, ang)], axis=-1)
    h = h + jnp.einsum('ble,ed->bld', mix, w_out)
    u = rms_norm(h, ln2)
    a = jnp.square(jax.nn.relu(jnp.einsum('bld,df->blf', u, w_up)))
    return h + jnp.einsum('blf,fd->bld', a, w_down)


def encode(x, meta_tokens, ln1_g, w_in, q_norm_g, k_norm_g, ret_log_decay_fwd, ret_log_decay_bwd,
           w_out, ln2_g, w_up, w_down, final_norm_g):
    B, S, _ = x.shape
    h = jnp.concatenate([jnp.broadcast_to(meta_tokens[None].astype(x.dtype), (B, N_META, D_MODEL)), x], axis=1)
    ang = axial_angles(S)
    for l in range(DEPTH):
        h = encoder_layer(h, ang, ln1_g[l], w_in[l], q_norm_g[l], k_norm_g[l],
                          ret_log_decay_fwd[l], ret_log_decay_bwd[l], w_out[l], ln2_g[l], w_up[l], w_down[l])
    return rms_norm(h[:, N_META:], final_norm_g)


def setup_inputs(seed: int = 0) -> dict:
    key = jax.random.key(seed)
    ks = jax.random.split(key, 16)
    f32 = jnp.float32

    def gain(k, shape):
        return 1.0 + 0.02 * jax.random.normal(k, shape, f32)

    def log_decay(k):
        u = jax.random.uniform(k, (DEPTH, RET_HEADS), f32, 0.0, 0.25)
        e = 5.0 + jnp.arange(RET_HEADS, dtype=f32)[None] + u
        return jnp.log(1.0 - 2.0 ** (-e))

    return {
        "x_prompt": jax.random.normal(ks[0], (BATCH, SEQ, D_MODEL), f32),
        "x_sample": jax.random.normal(ks[1], (DEC_BATCH, DEC_SEQ, D_MODEL), f32),
        "meta_tokens": jax.random.normal(ks[2], (N_META, D_MODEL), f32),
        "ln1_g": gain(ks[3], (DEPTH, D_MODEL)),
        "w_in": jax.random.normal(ks[4], (DEPTH, D_MODEL, IN_W), f32) * D_MODEL ** -0.5,
        "q_norm_g": gain(ks[5], (DEPTH, ATT_DH)),
        "k_norm_g": gain(ks[6], (DEPTH, ATT_DH)),
        "ret_log_decay_fwd": log_decay(ks[7]),
        "ret_log_decay_bwd": log_decay(ks[8]),
        "w_out": jax.random.normal(ks[9], (DEPTH, MIX_W, D_MODEL), f32) * MIX_W ** -0.5,
        "ln2_g": gain(ks[10], (DEPTH, D_MODEL)),
        "w_up": jax.random.normal(ks[11], (DEPTH, D_MODEL, D_FF), f32) * D_MODEL ** -0.5,
        "w_down": jax.random.normal(ks[12], (DEPTH, D_FF, D_MODEL), f32) * D_FF ** -0.5,
        "final_norm_g": gain(ks[13], (D_MODEL,)),
    }


def reference(x_prompt, x_sample, meta_tokens, ln1_g, w_in, q_norm_g, k_norm_g, ret_log_decay_fwd,
              ret_log_decay_bwd, w_out, ln2_g, w_up, w_down, final_norm_g):
    y_prompt = encode(x_prompt, meta_tokens, ln1_g, w_in, q_norm_g, k_norm_g, ret_log_decay_fwd,
                      ret_log_decay_bwd, w_out, ln2_g, w_up, w_down, final_norm_g)
    y_sample = encode(x_sample, meta_tokens, ln1_g, w_in, q_norm_g, k_norm_g, ret_log_decay_fwd,
                      ret_log_decay_bwd, w_out, ln2_g, w_up, w_down, final_norm_g)
    return (y_prompt, y_sample)
```

```python
import contextlib
import math
import numpy as np
import ml_dtypes
import concourse.bass as bass
import concourse.mybir as mybir
from concourse.bass_utils import run_bass_kernel_spmd

F32 = mybir.dt.float32
BF16 = mybir.dt.bfloat16
I32 = mybir.dt.int32
AF = mybir.ActivationFunctionType
ALU = mybir.AluOpType
AX = mybir.AxisListType

D = 2048
KC = 16
DFF = 8192
EPS = 1e-6
TWO_PI = 2.0 * math.pi


class Buf:
    def __init__(self, t, name):
        self.t = t
        self.name = name
        self.w = None
        self.r = {}
        self.dsem = None
        self.dval = 0

    def __getitem__(self, key):
        return self.t[key]


class Eng:
    def __init__(self, name, eng, sem):
        self.name = name
        self.e = eng
        self.sem = sem
        self.count = 0
        self.waited = {}

    def wait(self, dep):
        sem, val = dep
        if self.waited.get(id(sem), 0) >= val:
            return
        self.waited[id(sem)] = val
        self.e.wait_ge(sem, val)


class Kern:
    def __init__(self, nc, n_dma_sems=96):
        self.nc = nc
        self.es = contextlib.ExitStack()
        self.uid = 0
        self.PE = self._mk("pe", nc.tensor)
        self.DVE = self._mk("dve", nc.vector)
        self.ACT = self._mk("act", nc.scalar)
        self.POOL = self._mk("pool", nc.gpsimd)
        self.SP = self._mk("sp", nc.sync)
        self.engs = [self.PE, self.DVE, self.ACT, self.POOL, self.SP]
        self.pool = [[self.es.enter_context(nc.semaphore(f"dq{i}")), 0] for i in range(n_dma_sems)]
        self.live = []

    def _mk(self, name, eng):
        return Eng(name, eng, self.es.enter_context(self.nc.semaphore("s_" + name)))

    def _give_sem(self, b):
        ent = self.pool.pop()
        b.dsem, b.dval = ent[0], ent[1]
        b._ent = ent
        self.live.append(b)

    def release(self, bufs):
        for b in bufs:
            if b.dsem is not None and b in self.live:
                self.live.remove(b)
                b._ent[1] = b.dval
                self.pool.append(b._ent)

    def sb(self, stack, name, shape, dt, dma=False):
        self.uid += 1
        t = stack.enter_context(self.nc.sbuf_tensor(f"{name}_{self.uid}", list(shape), dt))
        b = Buf(t, name)
        if dma:
            self._give_sem(b)
        stack.callback(self.release, [b])
        return b

    def ps(self, stack, name, shape, dt=F32):
        self.uid += 1
        t = stack.enter_context(self.nc.psum_tensor(f"{name}_{self.uid}", list(shape), dt))
        return Buf(t, name)

    def dram(self, name, shape, dt, kind="Internal"):
        t = self.nc.dram_tensor(name, list(shape), dt, kind=kind)
        b = Buf(t.ap(), name)
        self._give_sem(b)
        return b

    def op(self, E, fn, reads=(), writes=(), selfsync=True):
        deps = []
        for b in reads:
            if b.w is not None:
                deps.append(b.w)
        for b in writes:
            if b.w is not None:
                deps.append(b.w)
            deps.extend(b.r.values())
        for d in deps:
            if d[0] is E.sem and not selfsync:
                continue
            E.wait(d)
        ins = fn(E.e)
        E.count += 1
        ins.then_inc(E.sem, 1)
        me = (E.sem, E.count)
        for b in reads:
            b.r[id(E.sem)] = me
        for b in writes:
            b.w = me
            b.r = {}
        return ins

    def dma(self, Q, dst, dst_ap, src, src_ap, chk_dst=True, **kw):
        deps = []
        if src.w is not None:
            deps.append(src.w)
        if chk_dst:
            if dst.w is not None:
                deps.append(dst.w)
            deps.extend(dst.r.values())
        for d in deps:
            Q.wait(d)
        ins = Q.e.dma_start(out=dst_ap, in_=src_ap, **kw)
        dst.dval += 16
        ins.then_inc(dst.dsem, 16)
        me = (dst.dsem, dst.dval)
        src.r[id(dst.dsem)] = me
        dst.w = me
        if chk_dst:
            dst.r = {}
        return ins

    def barrier(self):
        marks = [(E.sem, E.count) for E in self.engs if E.count > 0]
        marks += [(b.dsem, b.dval) for b in self.live if b.dval > 0]
        marks += [(ent[0], ent[1]) for ent in self.pool if ent[1] > 0]
        for E in self.engs:
            for m in marks:
                if m[0] is E.sem:
                    continue
                E.wait(m)


C_OFF = {}
_o = 0
for _n, _w in [("ident", 128), ("dp", 128), ("dn", 128), ("cp1", 128), ("c128m", 128), ("rfreq", 64),
               ("afreq", 32), ("g1c", 16), ("g2c", 16), ("gfc", 16), ("qg", 128), ("kg", 128), ("lgf", 4),
               ("lgb", 4), ("colc", 1), ("col127", 1)]:
    C_OFF[_n] = (_o, _w)
    _o += _w
NCST = _o


class Cfg:
    def __init__(self, nbp=16, nbs=128):
        self.NBP = nbp
        self.NBS = nbs
        self.OWN_P = nbp // 2
        self.OWN_S = nbs // 8
        self.NOWN = self.OWN_P + self.OWN_S
        self.NB1 = nbp + 1 + nbs + 1
        assert self.NOWN % 4 == 0
        self.seqs = [dict(name="p", nb=nbp + 1, b0=0, own0=0, nown=self.OWN_P),
                     dict(name="s", nb=nbs + 1, b0=nbp + 1, own0=self.OWN_P, nown=self.OWN_S)]


def build(cfg, debug=False):
    nc = bass.Bass("TRN2", target_bir_lowering=False)
    K = Kern(nc)
    NB1, NOWN = cfg.NB1, cfg.NOWN
    PE, DVE, ACT, POOL, SP = K.PE, K.DVE, K.ACT, K.POOL, K.SP
    skind = "ExternalOutput" if debug else "Internal"

    xb1 = K.dram("xb1", [NB1, 128, KC, 128], F32, "ExternalInput")
    xown = K.dram("xown", [NOWN, 128, KC, 128], F32, "ExternalInput")
    meta1 = K.dram("meta1", [128, NB1, 7], F32, "ExternalInput")
    metao = K.dram("metao", [128, NOWN, 3], F32, "ExternalInput")
    cst_d = K.dram("cst", [128, NCST], F32, "ExternalInput")
    w_in = K.dram("w_in", [D, 4608], F32, "ExternalInput")
    w_out = K.dram("w_out", [D, D], F32, "ExternalInput")
    w_up = K.dram("w_up", [D, DFF], F32, "ExternalInput")
    w_down = K.dram("w_down", [DFF, D], F32, "ExternalInput")
    yT = K.dram("yT", [128, KC, NOWN * 128], F32, "ExternalOutput")

    W1s = K.dram("W1s", [128, KC, 2048], BF16)
    Wown = K.dram("Wown", [8, 128, KC, 512], BF16)
    Wouts = K.dram("Wouts", [4, 128, KC, 512], BF16)
    Wups = K.dram("Wups", [16, 128, KC, 512], BF16)
    Wdowns = K.dram("Wdowns", [16, 128, 64, 128], BF16)
    KTs = [K.dram(f"KT_{s['name']}", [128, 2, s["nb"] * 128], BF16, skind) for s in cfg.seqs]
    Vs = [K.dram(f"V_{s['name']}", [128, s["nb"], 256], BF16, skind) for s in cfg.seqs]
    PROJ = K.dram("PROJ", [NOWN, 128, 4096], F32, skind)
    QT = K.dram("QT", [NOWN, 128, 8, 128], BF16, skind)
    RO = K.dram("RO", [NOWN, 128, 1024], F32, skind)
    RQB = K.dram("RQB", [NOWN, 128, 512], BF16, skind)
    KVB = K.dram("KVB", [NOWN, 128, 1024], F32, skind)
    MIXT = K.dram("MIXT", [128, KC, NOWN * 128], BF16, skind)
    SDBG = K.dram("SDBG", [4, 128, 1024], F32, skind) if debug else None

    with contextlib.ExitStack() as G:
        cst = K.sb(G, "cst", [128, NCST], F32, dma=True)
        K.dma(SP, cst, cst[:], cst_d, cst_d[:])

        def C(name, rows=128):
            o, w = C_OFF[name]
            return cst[0:rows, o:o + w]

        ident = K.sb(G, "ident", [128, 128], BF16)
        ones = K.sb(G, "ones", [128, 128], BF16)
        epsb = K.sb(G, "epsb", [128, 1], F32)
        negB = K.sb(G, "negB", [128, 1], F32)
        MT = K.sb(G, "MT", [128, 4, 128], F32)
        qdf = K.sb(G, "qdf", [128, 4, 128], F32)
        qdb = K.sb(G, "qdb", [128, 4, 128], F32)
        kdf = K.sb(G, "kdf", [128, 4], F32)
        kdb = K.sb(G, "kdb", [128, 4], F32)
        cdf = K.sb(G, "cdf", [128, 4], F32)
        cdb = K.sb(G, "cdb", [128, 4], F32)
        wfb = K.sb(G, "wfb", [128, NB1, 8], F32)
        m1 = K.sb(G, "m1", [128, NB1, 7], F32, dma=True)
        mo = K.sb(G, "mo", [128, NOWN, 3], F32, dma=True)
        Sst = [[K.sb(G, f"S{d}{s['name']}", [128, 4, 256], F32) for d in "fb"] for s in cfg.seqs]
        tmpc = K.sb(G, "tmpc", [128, max(NB1, 128)], F32)
        tmpd = K.sb(G, "tmpd", [128, max(NB1, 128)], F32)

        K.dma(SP, m1, m1[:], meta1, meta1[:])
        K.dma(SP, mo, mo[:], metao, metao[:])

        K.op(DVE, lambda e: e.tensor_copy(out=ident[:], in_=C("ident")), [cst], [ident])
        K.op(DVE, lambda e: e.memset(ones[:], 1.0), [], [ones])
        K.op(DVE, lambda e: e.memset(epsb[:], EPS), [], [epsb])
        for s in range(2):
            for d in range(2):
                K.op(POOL, lambda e: e.memset(Sst[s][d][:], 0.0), [], [Sst[s][d]])
        K.op(DVE, lambda e: e.tensor_reduce(out=tmpc[:, 0:1], in_=C("qg"), axis=AX.X, op=ALU.max,
                                            apply_absolute_value=True), [cst], [tmpc])
        K.op(DVE, lambda e: e.tensor_reduce(out=tmpc[:, 1:2], in_=C("kg"), axis=AX.X, op=ALU.max,
                                            apply_absolute_value=True), [cst], [tmpc])
        K.op(DVE, lambda e: e.scalar_tensor_tensor(out=negB[:], in0=tmpc[:, 0:1], scalar=-math.sqrt(128.0),
                                                   in1=tmpc[:, 1:2], op0=ALU.mult, op1=ALU.mult), [tmpc], [negB])
        lgf, lgb = C("lgf"), C("lgb")
        for h in range(4):
            K.op(DVE, lambda e: e.tensor_scalar(out=tmpc[:, 0:128], in0=C("dp"), scalar1=lgf[:, h:h + 1],
                                                scalar2=None, op0=ALU.mult), [cst], [tmpc])
            K.op(DVE, lambda e: e.scalar_tensor_tensor(out=tmpd[:, 0:128], in0=C("dn"), scalar=lgb[:, h:h + 1],
                                                       in1=tmpc[:, 0:128], op0=ALU.mult, op1=ALU.add),
                 [cst, tmpc], [tmpd])
            K.op(ACT, lambda e: e.activation(out=MT[:, h, :], in_=tmpd[:, 0:128], func=AF.Exp), [tmpd], [MT])
            K.op(ACT, lambda e: e.activation(out=qdf[:, h, :], in_=C("cp1"), func=AF.Exp, scale=lgf[:, h:h + 1]),
                 [cst], [qdf])
            K.op(ACT, lambda e: e.activation(out=qdb[:, h, :], in_=C("c128m"), func=AF.Exp, scale=lgb[:, h:h + 1]),
                 [cst], [qdb])
            K.op(ACT, lambda e: e.activation(out=kdf[:, h:h + 1], in_=C("col127"), func=AF.Exp,
                                             scale=lgf[:, h:h + 1]), [cst], [kdf])
            K.op(ACT, lambda e: e.activation(out=kdb[:, h:h + 1], in_=C("colc"), func=AF.Exp,
                                             scale=lgb[:, h:h + 1]), [cst], [kdb])
            for d, (lg, dcol, mcol) in enumerate([(lgf, 3, 4), (lgb, 5, 6)]):
                K.op(ACT, lambda e: e.activation(out=tmpc[:, 0:NB1], in_=m1[:, :, dcol], func=AF.Exp,
                                                 scale=lg[:, h:h + 1]), [m1, cst], [tmpc])
                K.op(DVE, lambda e: e.tensor_tensor(out=wfb[:, :, 4 * d + h], in0=tmpc[:, 0:NB1],
                                                    in1=m1[:, :, mcol], op=ALU.mult), [tmpc, m1], [wfb])
        K.op(ACT, lambda e: e.activation(out=cdf[:], in_=lgf, func=AF.Exp, scale=128.0), [cst], [cdf])
        K.op(ACT, lambda e: e.activation(out=cdb[:], in_=lgb, func=AF.Exp, scale=128.0), [cst], [cdb])

        def cs_tmps(st, Gn):
            return (K.sb(st, "cs_a", [128, Gn, 64], F32), K.sb(st, "cs_k", [128, Gn, 64], I32),
                    K.sb(st, "cs_f", [128, Gn, 64], F32))

        def build_cs(st, name, tmps, Gn, scale):
            ang, ki, kf = tmps
            cs = K.sb(st, name + "_cs", [128, Gn, 128], F32)

            def fill(specs_now, Gc):
                for (fr, pos, off) in specs_now:
                    nf = fr.shape[-1]
                    K.op(DVE, lambda e: e.tensor_tensor(
                        out=ang[:, 0:Gc, off:off + nf], in0=fr.unsqueeze(1).to_broadcast([128, Gc, nf]),
                        in1=pos.unsqueeze(2).to_broadcast([128, Gc, nf]), op=ALU.mult), [cst, m1, mo], [ang])
                K.op(DVE, lambda e: e.tensor_scalar(out=ki[:, 0:Gc, :], in0=ang[:, 0:Gc, :], scalar1=1.0 / TWO_PI,
                                                    scalar2=None, op0=ALU.mult), [ang], [ki])
                K.op(DVE, lambda e: e.tensor_copy(out=kf[:, 0:Gc, :], in_=ki[:, 0:Gc, :]), [ki], [kf])
                K.op(DVE, lambda e: e.scalar_tensor_tensor(out=ang[:, 0:Gc, :], in0=kf[:, 0:Gc, :], scalar=-TWO_PI,
                                                           in1=ang[:, 0:Gc, :], op0=ALU.mult, op1=ALU.add),
                     [kf, ang], [ang])
                K.op(ACT, lambda e: e.activation(out=cs[:, 0:Gc, 0:64], in_=ang[:, 0:Gc, :], func=AF.Sin,
                                                 scale=1.0 - 1e-6), [ang], [cs])
                K.op(ACT, lambda e: e.activation(out=kf[:, 0:Gc, :], in_=ang[:, 0:Gc, :], func=AF.Sin,
                                                 scale=0.5), [ang], [kf])
                K.op(DVE, lambda e: e.scalar_tensor_tensor(out=kf[:, 0:Gc, :], in0=kf[:, 0:Gc, :], scalar=-2.0,
                                                           in1=kf[:, 0:Gc, :], op0=ALU.mult, op1=ALU.mult),
                     [kf], [kf])
                if scale == 1.0:
                    K.op(DVE, lambda e: e.tensor_scalar(out=cs[:, 0:Gc, 64:128], in0=kf[:, 0:Gc, :], scalar1=1.0,
                                                        scalar2=None, op0=ALU.add), [kf], [cs])
                else:
                    K.op(DVE, lambda e: e.tensor_scalar(out=cs[:, 0:Gc, 64:128], in0=kf[:, 0:Gc, :], scalar1=1.0,
                                                        scalar2=scale, op0=ALU.add, op1=ALU.mult), [kf], [cs])
                    K.op(DVE, lambda e: e.tensor_scalar(out=cs[:, 0:Gc, 0:64], in0=cs[:, 0:Gc, 0:64],
                                                        scalar1=scale, scalar2=None, op0=ALU.mult), [cs], [cs])
            return cs, fill

        def rope(E1, E2, xb, x_ap, nh, cs, g, ob, o_ap, tm):
            xv = x_ap.rearrange("p (h i t) -> p h i t", h=nh, t=2)
            ov = o_ap.rearrange("p (h i t) -> p h i t", h=nh, t=2)
            x1, x2 = xv[:, :, :, 0], xv[:, :, :, 1]
            sb_ = cs[:, g, 0:64].unsqueeze(1).to_broadcast([128, nh, 64])
            cb_ = cs[:, g, 64:128].unsqueeze(1).to_broadcast([128, nh, 64])
            t = [tm[i][:, 0:nh * 64].rearrange("p (h i) -> p h i", h=nh) for i in range(4)]
            K.op(E1, lambda e: e.tensor_tensor(out=t[0], in0=x1, in1=cb_, op=ALU.mult), [xb, cs], [tm[0]])
            K.op(E1, lambda e: e.tensor_tensor(out=t[1], in0=x2, in1=sb_, op=ALU.mult), [xb, cs], [tm[1]])
            K.op(E1, lambda e: e.tensor_tensor(out=ov[:, :, :, 0], in0=t[0], in1=t[1], op=ALU.subtract),
                 [tm[0], tm[1]], [ob])
            K.op(E2, lambda e: e.tensor_tensor(out=t[2], in0=x1, in1=sb_, op=ALU.mult), [xb, cs], [tm[2]])
            K.op(E2, lambda e: e.tensor_tensor(out=t[3], in0=x2, in1=cb_, op=ALU.mult), [xb, cs], [tm[3]])
            K.op(E2, lambda e: e.tensor_tensor(out=ov[:, :, :, 1], in0=t[2], in1=t[3], op=ALU.add),
                 [tm[2], tm[3]], [ob])

        def rstd_from(E_sq, ssq_b, ssq_ap, inv_n, tmp_b, tmp_ap, out_b, out_ap):
            rows = ssq_ap.shape[0]
            K.op(ACT, lambda e: e.activation(out=tmp_ap, in_=ssq_ap, func=AF.Sqrt, scale=inv_n,
                                             bias=epsb[0:rows, 0:1]), [ssq_b, epsb], [tmp_b])
            K.op(DVE, lambda e: e.reciprocal(out=out_ap, in_=tmp_ap), [tmp_b], [out_b])

        def norm_block(xTb, sqb, uTb, ps_stat, rt, rstd, ntok):
            K.op(ACT, lambda e: e.activation(out=sqb[:, :, 0:ntok], in_=xTb[:, :, 0:ntok], func=AF.Square),
                 [xTb], [sqb])
            for kc in range(KC):
                K.op(PE, lambda e: e.matmul(ps_stat[:, 0:ntok], lhsT=ones[:], rhs=sqb[:, kc, 0:ntok],
                                            start=(kc == 0), stop=(kc == KC - 1)), [ones, sqb], [ps_stat],
                     selfsync=False)
            rstd_from(None, ps_stat, ps_stat[:, 0:ntok], 1.0 / D, rt, rt[:, 0:ntok], rstd, rstd[:, 0:ntok])
            h = KC // 2
            for (E, a, b) in ((DVE, 0, h), (POOL, h, KC)):
                K.op(E, lambda e: e.tensor_tensor(
                    out=uTb[:, a:b, 0:ntok], in0=xTb[:, a:b, 0:ntok],
                    in1=rstd[:, 0:ntok].unsqueeze(1).to_broadcast([128, b - a, ntok]), op=ALU.mult),
                    [xTb, rstd], [uTb])

        cast_rr = [0]

        def cast(srcb, src_ap, dstb, dst_ap, gcol=None):
            i = cast_rr[0] % 3
            cast_rr[0] += 1
            if i == 0:
                if gcol is None:
                    K.op(DVE, lambda e: e.tensor_copy(out=dst_ap, in_=src_ap), [srcb], [dstb])
                else:
                    K.op(DVE, lambda e: e.tensor_scalar(out=dst_ap, in0=src_ap, scalar1=gcol, scalar2=None,
                                                        op0=ALU.mult), [srcb, cst], [dstb])
            elif i == 1:
                K.op(ACT, lambda e: e.activation(out=dst_ap, in_=src_ap, func=AF.Copy,
                                                 scale=(1.0 if gcol is None else gcol)), [srcb, cst], [dstb])
            else:
                if gcol is None:
                    K.op(POOL, lambda e: e.tensor_copy(out=dst_ap, in_=src_ap), [srcb], [dstb])
                else:
                    K.op(POOL, lambda e: e.tensor_scalar(out=dst_ap, in0=src_ap, scalar1=gcol, scalar2=None,
                                                         op0=ALU.mult), [srcb, cst], [dstb])

        with contextlib.ExitStack() as st:
            NBUF = 3
            stg = [K.sb(st, f"stg{i}", [128, 2048], F32, dma=True) for i in range(NBUF)]
            wbf = [K.sb(st, f"wbf{i}", [128, 2048], BF16) for i in range(NBUF)]
            g1c, g2c = C("g1c"), C("g2c")
            units = []
            for kc in range(KC):
                rows = slice(kc * 128, (kc + 1) * 128)
                units.append((w_in, w_in[rows, 0:2048], 2048, g1c[:, kc:kc + 1],
                              [(Wown, Wown[0:4, :, kc, :].rearrange("g p j -> p g j"), 0, 2048, 4),
                               (W1s, W1s[:, kc, 0:1536], 512, 1536, 0)]))
                units.append((w_in, w_in[rows, 2048:4096], 2048, g1c[:, kc:kc + 1],
                              [(Wown, Wown[4:8, :, kc, :].rearrange("g p j -> p g j"), 0, 2048, 4)]))
                units.append((w_in, w_in[rows, 4096:4608], 512, g1c[:, kc:kc + 1],
                              [(W1s, W1s[:, kc, 1536:2048], 0, 512, 0)]))
            for kc in range(KC):
                rows = slice(kc * 128, (kc + 1) * 128)
                units.append((w_out, w_out[rows, :], 2048, None,
                              [(Wouts, Wouts[:, :, kc, :].rearrange("g p j -> p g j"), 0, 2048, 4)]))
            for kc in range(KC):
                rows = slice(kc * 128, (kc + 1) * 128)
                for q in range(4):
                    units.append((w_up, w_up[rows, q * 2048:(q + 1) * 2048], 2048, g2c[:, kc:kc + 1],
                                  [(Wups, Wups[4 * q:4 * q + 4, :, kc, :].rearrange("g p j -> p g j"), 0, 2048, 4)]))
            for fc in range(64):
                rows = slice(fc * 128, (fc + 1) * 128)
                units.append((w_down, w_down[rows, :], 2048, None,
                              [(Wdowns, Wdowns[:, :, fc, :].rearrange("g p j -> p g j"), 0, 2048, 16)]))

            def ld(i):
                src, sap, n, _, _ = units[i]
                K.dma(SP, stg[i % NBUF], stg[i % NBUF][:, 0:n], src, sap)

            for i in range(min(2, len(units))):
                ld(i)
            for i, (src, sap, n, gcol, stores) in enumerate(units):
                if i + 2 < len(units):
                    ld(i + 2)
                s_, w_ = stg[i % NBUF], wbf[i % NBUF]
                cast(s_, s_[:, 0:n], w_, w_[:, 0:n], gcol)
                for (dstb, dap, off, nn, ng) in stores:
                    sap2 = w_[:, off:off + nn]
                    if ng:
                        sap2 = sap2.rearrange("p (g j) -> p g j", g=ng)
                    K.dma(ACT, dstb, dap, w_, sap2, chk_dst=False)
        K.barrier()

        GT = 8
        with contextlib.ExitStack() as st:
            W1 = K.sb(st, "W1", [128, KC, 2048], BF16, dma=True)
            for q in range(4):
                K.dma(SP, W1, W1[:, 4 * q:4 * q + 4, :], W1s, W1s[:, 4 * q:4 * q + 4, :], chk_dst=(q == 0))
            xTs = [K.sb(st, f"xT{i}", [128, KC, 128], F32, dma=True) for i in range(2)]
            sqb = K.sb(st, "sq", [128, KC, 128], BF16)
            uTs = [K.sb(st, f"uT{i}", [128, KC, 128], BF16) for i in range(2)]
            rt = K.sb(st, "rt", [128, 128], F32)
            rstd = K.sb(st, "rstd", [128, 128], F32)
            pp = [K.ps(st, f"pp{j}", [128, 512]) for j in range(4)]
            ps_stat = K.ps(st, "pstat", [128, 512])
            ps_tr = K.ps(st, "ptr", [128, 1024], BF16)
            ps_kv = [K.ps(st, f"pkv{j}", [128, 512]) for j in range(2)]
            cst_ = cs_tmps(st, GT)
            cs_r = [build_cs(st, f"csr{i}", cst_, GT, 128.0 ** -0.5) for i in range(2)]
            cs_a = [build_cs(st, f"csa{i}", cst_, GT, 1.0) for i in range(2)]
            rk32 = K.sb(st, "rk32", [128, 512], F32)
            ak32 = K.sb(st, "ak32", [128, 256], F32)
            akn = K.sb(st, "akn", [128, 256], F32)
            tm = [K.sb(st, f"tm{i}", [128, 256], F32) for i in range(4)]
            tm2 = [K.sb(st, f"tn{i}", [128, 256], F32) for i in range(4)]
            rk_r = K.sb(st, "rk_r", [128, 512], BF16)
            kw = [K.sb(st, f"kw{i}", [128, 512], BF16) for i in range(2)]
            rv_bf = K.sb(st, "rv_bf", [128, 1024], BF16)
            ak_r = K.sb(st, "ak_r", [128, 256], BF16)
            av_bf = [K.sb(st, f"av_bf{i}", [128, 256], BF16) for i in range(2)]
            ktb = [K.sb(st, f"ktb{i}", [128, 2, 128], BF16) for i in range(2)]
            sk = K.sb(st, "sk", [128, 8], F32)
            junk = K.sb(st, "junk", [128, 128], F32)

            def load_x(i):
                K.dma(SP, xTs[i % 2], xTs[i % 2][:], xb1, xb1[i])

            load_x(0)
            for si, s in enumerate(cfg.seqs):
                Sf, Sb = Sst[si]
                for bl in range(s["nb"]):
                    i = s["b0"] + bl
                    if i + 1 < NB1:
                        load_x(i + 1)
                    g = i % GT
                    gi = (i // GT) % 2
                    if g == 0:
                        Gc = min(GT, NB1 - i)
                        cs_r[gi][1]([(C("rfreq"), m1[:, i:i + Gc, 0], 0)], Gc)
                        cs_a[gi][1]([(C("afreq"), m1[:, i:i + Gc, 1], 0), (C("afreq"), m1[:, i:i + Gc, 2], 32)], Gc)
                    csr, csa = cs_r[gi][0], cs_a[gi][0]
                    xTb, uTb = xTs[i % 2], uTs[i % 2]
                    norm_block(xTb, sqb, uTb, ps_stat, rt, rstd, 128)
                    for j in range(4):
                        for kc in range(KC):
                            K.op(PE, lambda e: e.matmul(pp[j][:], lhsT=uTb[:, kc, :], rhs=W1[:, kc, 512 * j:512 * j + 512],
                                                        start=(kc == 0), stop=(kc == KC - 1)), [uTb, W1], [pp[j]],
                                 selfsync=False)
                    K.op(ACT, lambda e: e.activation(out=rk32[:], in_=pp[0][:], func=AF.Copy), [pp[0]], [rk32])
                    K.op(ACT, lambda e: e.activation(out=rv_bf[:, 0:512], in_=pp[1][:], func=AF.Copy), [pp[1]], [rv_bf])
                    K.op(ACT, lambda e: e.activation(out=rv_bf[:, 512:1024], in_=pp[2][:], func=AF.Copy), [pp[2]], [rv_bf])
                    K.op(ACT, lambda e: e.activation(out=ak32[:], in_=pp[3][:, 0:256], func=AF.Copy), [pp[3]], [ak32])
                    avb = av_bf[i % 2]
                    K.op(ACT, lambda e: e.activation(out=avb[:], in_=pp[3][:, 256:512], func=AF.Copy), [pp[3]], [avb])
                    K.dma(SP, Vs[si], Vs[si][:, bl, :], avb, avb[:], chk_dst=False)
                    rope(DVE, POOL, rk32, rk32[:], 4, csr, g, rk_r, rk_r[:], tm)
                    for d in range(2):
                        K.op(DVE if d == 0 else POOL, lambda e: e.tensor_tensor(
                            out=kw[d][:].rearrange("p (h x) -> p h x", h=4),
                            in0=rk_r[:].rearrange("p (h x) -> p h x", h=4),
                            in1=wfb[:, i, 4 * d:4 * d + 4].unsqueeze(2).to_broadcast([128, 4, 128]), op=ALU.mult),
                            [rk_r, wfb], [kw[d]])
                    for d, Sd in enumerate((Sf, Sb)):
                        for hp in range(2):
                            pk = ps_kv[hp]
                            for hh in range(2):
                                h = 2 * hp + hh
                                K.op(PE, lambda e: e.matmul(pk[:, 256 * hh:256 * hh + 256],
                                                            lhsT=kw[d][:, 128 * h:128 * h + 128],
                                                            rhs=rv_bf[:, 256 * h:256 * h + 256], start=True, stop=True),
                                     [kw[d], rv_bf], [pk], selfsync=False)
                            sv = Sd[:, 2 * hp:2 * hp + 2, :].rearrange("p h v -> p (h v)")
                            K.op(DVE, lambda e: e.tensor_tensor(out=sv, in0=sv, in1=pk[:], op=ALU.add), [Sd, pk], [Sd])
                    for h in range(2):
                        K.op(ACT, lambda e: e.activation(out=junk[:], in_=ak32[:, 128 * h:128 * h + 128], func=AF.Square,
                                                         accum_out=sk[:, h:h + 1]), [ak32], [junk, sk])
                    rstd_from(None, sk, sk[:, 0:2], 1.0 / 128, sk, sk[:, 2:4], sk, sk[:, 4:6])
                    for h in range(2):
                        K.op(DVE, lambda e: e.scalar_tensor_tensor(out=akn[:, 128 * h:128 * h + 128],
                                                                   in0=ak32[:, 128 * h:128 * h + 128],
                                                                   scalar=sk[:, 4 + h:5 + h], in1=C("kg"),
                                                                   op0=ALU.mult, op1=ALU.mult), [ak32, sk, cst], [akn])
                    rope(POOL, DVE, akn, akn[:], 2, csa, g, ak_r, ak_r[:], tm2)
                    for h in range(2):
                        K.op(PE, lambda e: e.transpose(out=ps_tr[:, 128 * h:128 * h + 128], in_=ak_r[:, 128 * h:128 * h + 128],
                                                       identity=ident[:]), [ak_r, ident], [ps_tr], selfsync=False)
                    kt = ktb[i % 2]
                    K.op(DVE, lambda e: e.tensor_copy(out=kt[:].rearrange("p h t -> p (h t)"), in_=ps_tr[:, 0:256]),
                         [ps_tr], [kt])
                    K.dma(SP, KTs[si], KTs[si][:, :, bl * 128:(bl + 1) * 128], kt, kt[:], chk_dst=False)
            if debug:
                for si in range(2):
                    for d in range(2):
                        K.dma(SP, SDBG, SDBG[2 * si + d], Sst[si][d], Sst[si][d][:].rearrange("p h v -> p (h v)"),
                              chk_dst=False)
        K.barrier()

        with contextlib.ExitStack() as st:
            uTo = [K.sb(st, f"uTo{b}", [128, KC, 128], BF16) for b in range(NOWN)]
            xTs = [K.sb(st, f"xT{i}", [128, KC, 128], F32, dma=True) for i in range(2)]
            sqb = K.sb(st, "sq", [128, KC, 128], BF16)
            rt = K.sb(st, "rt", [128, 128], F32)
            rstd = K.sb(st, "rstd", [128, 128], F32)
            ps_stat = K.ps(st, "pstat", [128, 512])
            pp = [K.ps(st, f"pp{j}", [128, 512]) for j in range(4)]
            wg = [K.sb(st, f"wg{i}", [128, KC, 512], BF16, dma=True) for i in range(2)]
            stg = [K.sb(st, f"pstg{i}", [128, 512], F32) for i in range(4)]
            K.dma(SP, wg[0], wg[0][:], Wown, Wown[0])
            K.dma(SP, xTs[0], xTs[0][:], xown, xown[0])
            for b in range(NOWN):
                if b + 1 < NOWN:
                    K.dma(SP, xTs[(b + 1) % 2], xTs[(b + 1) % 2][:], xown, xown[b + 1])
                norm_block(xTs[b % 2], sqb, uTo[b], ps_stat, rt, rstd, 128)
            n = 0
            for cg in range(8):
                if cg + 1 < 8:
                    K.dma(SP, wg[(cg + 1) % 2], wg[(cg + 1) % 2][:], Wown, Wown[cg + 1])
                w = wg[cg % 2]
                for b in range(NOWN):
                    p_, s_ = pp[n % 4], stg[n % 4]
                    for kc in range(KC):
                        K.op(PE, lambda e: e.matmul(p_[:], lhsT=uTo[b][:, kc, :], rhs=w[:, kc, :], start=(kc == 0),
                                                    stop=(kc == KC - 1)), [uTo[b], w], [p_], selfsync=False)
                    if n % 2 == 0:
                        K.op(ACT, lambda e: e.activation(out=s_[:], in_=p_[:], func=AF.Copy), [p_], [s_])
                    else:
                        K.op(DVE, lambda e: e.tensor_copy(out=s_[:], in_=p_[:]), [p_], [s_])
                    K.dma(SP, PROJ, PROJ[b, :, 512 * cg:512 * cg + 512], s_, s_[:], chk_dst=False)
                    n += 1
        K.barrier()

        for si, s in enumerate(cfg.seqs):
            Sf, Sb = Sst[si]
            own = list(range(s["own0"], s["own0"] + s["nown"]))
            no = len(own)
            with contextlib.ExitStack() as st:
                pr = [K.sb(st, f"pr{i}", [128, 4096], F32, dma=True) for i in range(2)]
                cst_ = cs_tmps(st, no)
                csr1 = build_cs(st, "csr1", cst_, no, 1.0)
                csrs = build_cs(st, "csrs", cst_, no, 128.0 ** -0.5)
                csa = build_cs(st, "csa", cst_, no, 128.0 ** -0.5)
                o0 = own[0]
                csr1[1]([(C("rfreq"), mo[:, o0:o0 + no, 0], 0)], no)
                csrs[1]([(C("rfreq"), mo[:, o0:o0 + no, 0], 0)], no)
                csa[1]([(C("afreq"), mo[:, o0:o0 + no, 1], 0), (C("afreq"), mo[:, o0:o0 + no, 2], 32)], no)
                tm = [K.sb(st, f"tm{i}", [128, 256], F32) for i in range(4)]
                tm2 = [K.sb(st, f"tn{i}", [128, 256], F32) for i in range(4)]
                tm3 = [K.sb(st, f"to{i}", [128, 512], F32) for i in range(4)]
                rq_r = K.sb(st, "rq_r", [128, 512], BF16)
                rk_r = K.sb(st, "rk_r", [128, 512], BF16)
                rv_bf = K.sb(st, "rv_bf", [128, 1024], BF16)
                aqsq = K.sb(st, "aqsq", [128, 1024], F32)
                aqn = K.sb(st, "aqn", [128, 1024], F32)
                aq_r = K.sb(st, "aq_r", [128, 1024], BF16)
                sk = K.sb(st, "sk", [128, 24], F32)
                ps_t1 = K.ps(st, "pt1", [128, 1024], BF16)
                ps_t2 = K.ps(st, "pt2", [128, 1024], BF16)
                ps_p = K.ps(st, "ppt", [128, 512])
                ps_o = [K.ps(st, f"po{j}", [128, 512]) for j in range(2)]
                ps_kv = [K.ps(st, f"pkv{j}", [128, 512]) for j in range(2)]
                rqT = K.sb(st, "rqT", [128, 512], BF16)
                rqfT = K.sb(st, "rqfT", [128, 512], BF16)
                rqbT = [K.sb(st, f"rqbT{i}", [128, 512], BF16) for i in range(2)]
                rkT = K.sb(st, "rkT", [128, 512], BF16)
                aqT = [K.sb(st, f"aqT{i}", [128, 1024], BF16) for i in range(2)]
                PT = K.sb(st, "PT", [128, 512], BF16)
                kw = [K.sb(st, f"kw{i}", [128, 512], BF16) for i in range(2)]
                Sf_bf = K.sb(st, "Sf_bf", [128, 1024], BF16)
                ro_s = [K.sb(st, f"ro_s{i}", [128, 1024], F32) for i in range(2)]
                kvb_s = [K.sb(st, f"kvb_s{i}", [128, 1024], F32) for i in range(2)]
                K.op(DVE, lambda e: e.tensor_copy(out=Sf_bf[:], in_=Sf[:].rearrange("p h v -> p (h v)")), [Sf], [Sf_bf])
                K.dma(SP, pr[0], pr[0][:], PROJ, PROJ[own[0]])
                for n, b in enumerate(own):
                    if n + 1 < no:
                        K.dma(SP, pr[(n + 1) % 2], pr[(n + 1) % 2][:], PROJ, PROJ[own[n + 1]])
                    p_ = pr[n % 2]
                    rope(DVE, POOL, p_, p_[:, 0:512], 4, csr1[0], n, rq_r, rq_r[:], tm)
                    rope(DVE, POOL, p_, p_[:, 512:1024], 4, csrs[0], n, rk_r, rk_r[:], tm2)
                    K.op(ACT, lambda e: e.activation(out=rv_bf[:], in_=p_[:, 1024:2048], func=AF.Copy), [p_], [rv_bf])
                    aq = p_[:, 3072:4096]
                    K.op(POOL, lambda e: e.tensor_tensor(out=aqsq[:], in0=aq, in1=aq, op=ALU.mult), [p_], [aqsq])
                    K.op(DVE, lambda e: e.tensor_reduce(out=sk[:, 0:8], in_=aqsq[:].rearrange("p (h x) -> p h x", h=8),
                                                        axis=AX.X, op=ALU.add), [aqsq], [sk])
                    rstd_from(None, sk, sk[:, 0:8], 1.0 / 128, sk, sk[:, 8:16], sk, sk[:, 16:24])
                    K.op(DVE, lambda e: e.tensor_tensor(
                        out=aqn[:].rearrange("p (h x) -> p h x", h=8), in0=aq.rearrange("p (h x) -> p h x", h=8),
                        in1=sk[:, 16:24].unsqueeze(2).to_broadcast([128, 8, 128]), op=ALU.mult), [p_, sk], [aqn])
                    K.op(POOL, lambda e: e.tensor_tensor(
                        out=aqn[:].rearrange("p (h x) -> p h x", h=8), in0=aqn[:].rearrange("p (h x) -> p h x", h=8),
                        in1=C("qg").unsqueeze(1).to_broadcast([128, 8, 128]), op=ALU.mult), [aqn, cst], [aqn])
                    rope(DVE, POOL, aqn, aqn[:], 8, csa[0], n, aq_r, aq_r[:], tm3)
                    for h in range(4):
                        K.op(PE, lambda e: e.transpose(out=ps_t1[:, 128 * h:128 * h + 128], in_=rq_r[:, 128 * h:128 * h + 128],
                                                       identity=ident[:]), [rq_r, ident], [ps_t1], selfsync=False)
                    for h in range(4):
                        K.op(PE, lambda e: e.transpose(out=ps_t1[:, 512 + 128 * h:640 + 128 * h],
                                                       in_=rk_r[:, 128 * h:128 * h + 128], identity=ident[:]),
                             [rk_r, ident], [ps_t1], selfsync=False)
                    for h in range(8):
                        K.op(PE, lambda e: e.transpose(out=ps_t2[:, 128 * h:128 * h + 128], in_=aq_r[:, 128 * h:128 * h + 128],
                                                       identity=ident[:]), [aq_r, ident], [ps_t2], selfsync=False)
                    rb = rqbT[n % 2]
                    K.op(ACT, lambda e: e.activation(out=rqT[:], in_=ps_t1[:, 0:512], func=AF.Copy), [ps_t1], [rqT])
                    K.op(DVE, lambda e: e.tensor_tensor(out=rqfT[:], in0=ps_t1[:, 0:512],
                                                        in1=qdf[:].rearrange("p h c -> p (h c)"), op=ALU.mult),
                         [ps_t1, qdf], [rqfT])
                    K.op(DVE, lambda e: e.tensor_tensor(out=rb[:], in0=ps_t1[:, 0:512],
                                                        in1=qdb[:].rearrange("p h c -> p (h c)"), op=ALU.mult),
                         [ps_t1, qdb], [rb])
                    K.op(ACT, lambda e: e.activation(out=rkT[:], in_=ps_t1[:, 512:1024], func=AF.Copy), [ps_t1], [rkT])
                    at = aqT[n % 2]
                    K.op(ACT, lambda e: e.activation(out=at[:], in_=ps_t2[:], func=AF.Copy), [ps_t2], [at])
                    K.dma(SP, QT, QT[b], at, at[:].rearrange("p (h t) -> p h t", h=8), chk_dst=False)
                    K.dma(SP, RQB, RQB[b], rb, rb[:], chk_dst=False)
                    for h in range(4):
                        K.op(PE, lambda e: e.matmul(ps_p[:, 128 * h:128 * h + 128], lhsT=rkT[:, 128 * h:128 * h + 128],
                                                    rhs=rqT[:, 128 * h:128 * h + 128], start=True, stop=True),
                             [rkT, rqT], [ps_p], selfsync=False)
                    K.op(DVE, lambda e: e.tensor_tensor(out=PT[:], in0=ps_p[:], in1=MT[:].rearrange("p h c -> p (h c)"),
                                                        op=ALU.mult), [ps_p, MT], [PT])
                    for h in range(4):
                        po = ps_o[h // 2]
                        osl = po[:, 256 * (h % 2):256 * (h % 2) + 256]
                        K.op(PE, lambda e: e.matmul(osl, lhsT=PT[:, 128 * h:128 * h + 128], rhs=rv_bf[:, 256 * h:256 * h + 256],
                                                    start=True, stop=False), [PT, rv_bf], [po], selfsync=False)
                        K.op(PE, lambda e: e.matmul(osl, lhsT=rqfT[:, 128 * h:128 * h + 128], rhs=Sf_bf[:, 256 * h:256 * h + 256],
                                                    start=False, stop=True), [rqfT, Sf_bf], [po], selfsync=False)
                    ros = ro_s[n % 2]
                    K.op(ACT, lambda e: e.activation(out=ros[:, 0:512], in_=ps_o[0][:], func=AF.Copy), [ps_o[0]], [ros])
                    K.op(ACT, lambda e: e.activation(out=ros[:, 512:1024], in_=ps_o[1][:], func=AF.Copy), [ps_o[1]], [ros])
                    K.dma(SP, RO, RO[b], ros, ros[:], chk_dst=False)
                    for d, kd in enumerate((kdf, kdb)):
                        K.op(POOL, lambda e: e.tensor_tensor(
                            out=kw[d][:].rearrange("p (h x) -> p h x", h=4), in0=rk_r[:].rearrange("p (h x) -> p h x", h=4),
                            in1=kd[:].unsqueeze(2).to_broadcast([128, 4, 128]), op=ALU.mult), [rk_r, kd], [kw[d]])
                    for hp in range(2):
                        pk = ps_kv[hp]
                        for hh in range(2):
                            h = 2 * hp + hh
                            K.op(PE, lambda e: e.matmul(pk[:, 256 * hh:256 * hh + 256], lhsT=kw[0][:, 128 * h:128 * h + 128],
                                                        rhs=rv_bf[:, 256 * h:256 * h + 256], start=True, stop=True),
                                 [kw[0], rv_bf], [pk], selfsync=False)
                        for hh in range(2):
                            h = 2 * hp + hh
                            K.op(DVE, lambda e: e.scalar_tensor_tensor(out=Sf[:, h, :], in0=Sf[:, h, :], scalar=cdf[:, h:h + 1],
                                                                       in1=pk[:, 256 * hh:256 * hh + 256], op0=ALU.mult,
                                                                       op1=ALU.add), [Sf, cdf, pk], [Sf])
                    K.op(POOL, lambda e: e.tensor_copy(out=Sf_bf[:], in_=Sf[:].rearrange("p h v -> p (h v)")), [Sf], [Sf_bf])
                    kvs = kvb_s[n % 2]
                    for hp in range(2):
                        pk = ps_kv[hp]
                        for hh in range(2):
                            h = 2 * hp + hh
                            K.op(PE, lambda e: e.matmul(pk[:, 256 * hh:256 * hh + 256], lhsT=kw[1][:, 128 * h:128 * h + 128],
                                                        rhs=rv_bf[:, 256 * h:256 * h + 256], start=True, stop=True),
                                 [kw[1], rv_bf], [pk], selfsync=False)
                        K.op(ACT, lambda e: e.activation(out=kvs[:, 512 * hp:512 * hp + 512], in_=pk[:], func=AF.Copy),
                             [pk], [kvs])
                    K.dma(SP, KVB, KVB[b], kvs, kvs[:], chk_dst=False)
            K.barrier()
            with contextlib.ExitStack() as st:
                ro_l = [K.sb(st, f"ro_l{i}", [128, 1024], F32, dma=True) for i in range(2)]
                rg_l = [K.sb(st, f"rg_l{i}", [128, 1024], F32, dma=True) for i in range(2)]
                rqb_l = [K.sb(st, f"rqb_l{i}", [128, 512], BF16, dma=True) for i in range(2)]
                kvb_l = [K.sb(st, f"kvb_l{i}", [128, 1024], F32, dma=True) for i in range(2)]
                Sb_bf = K.sb(st, "Sb_bf", [128, 1024], BF16)
                o32 = K.sb(st, "o32", [128, 1024], F32)
                osq = K.sb(st, "osq", [128, 1024], F32)
                sg = K.sb(st, "sg", [128, 1024], F32)
                mixr = K.sb(st, "mixr", [128, 1024], BF16)
                mT = [K.sb(st, f"mT{i}", [128, 1024], BF16) for i in range(2)]
                sk = K.sb(st, "sk", [128, 12], F32)
                ps_o = [K.ps(st, f"po{j}", [128, 512]) for j in range(2)]
                ps_t = K.ps(st, "pt", [128, 1024], BF16)

                def loadB(n):
                    b = own[n]
                    K.dma(SP, ro_l[n % 2], ro_l[n % 2][:], RO, RO[b])
                    K.dma(SP, rg_l[n % 2], rg_l[n % 2][:], PROJ, PROJ[b, :, 2048:3072])
                    K.dma(SP, rqb_l[n % 2], rqb_l[n % 2][:], RQB, RQB[b])
                    if n + 1 < no:
                        K.dma(SP, kvb_l[n % 2], kvb_l[n % 2][:], KVB, KVB[own[n + 1]])

                loadB(no - 1)
                for n in range(no - 1, -1, -1):
                    b = own[n]
                    if n - 1 >= 0:
                        loadB(n - 1)
                    if n + 1 < no:
                        kv = kvb_l[n % 2]
                        for h in range(4):
                            K.op(DVE, lambda e: e.scalar_tensor_tensor(out=Sb[:, h, :], in0=Sb[:, h, :], scalar=cdb[:, h:h + 1],
                                                                       in1=kv[:, 256 * h:256 * h + 256], op0=ALU.mult,
                                                                       op1=ALU.add), [Sb, cdb, kv], [Sb])
                    K.op(POOL, lambda e: e.tensor_copy(out=Sb_bf[:], in_=Sb[:].rearrange("p h v -> p (h v)")), [Sb], [Sb_bf])
                    rb, ro, rg = rqb_l[n % 2], ro_l[n % 2], rg_l[n % 2]
                    for h in range(4):
                        po = ps_o[h // 2]
                        K.op(PE, lambda e: e.matmul(po[:, 256 * (h % 2):256 * (h % 2) + 256], lhsT=rb[:, 128 * h:128 * h + 128],
                                                    rhs=Sb_bf[:, 256 * h:256 * h + 256], start=True, stop=True),
                             [rb, Sb_bf], [po], selfsync=False)
                    for hp in range(2):
                        K.op(DVE, lambda e: e.tensor_tensor(out=o32[:, 512 * hp:512 * hp + 512], in0=ps_o[hp][:],
                                                            in1=ro[:, 512 * hp:512 * hp + 512], op=ALU.add),
                             [ps_o[hp], ro], [o32])
                    K.op(POOL, lambda e: e.tensor_tensor(out=osq[:], in0=o32[:], in1=o32[:], op=ALU.mult), [o32], [osq])
                    K.op(DVE, lambda e: e.tensor_reduce(out=sk[:, 0:4], in_=osq[:].rearrange("p (h x) -> p h x", h=4),
                                                        axis=AX.X, op=ALU.add), [osq], [sk])
                    rstd_from(None, sk, sk[:, 0:4], 1.0 / 256, sk, sk[:, 4:8], sk, sk[:, 8:12])
                    K.op(ACT, lambda e: e.activation(out=sg[:], in_=rg[:], func=AF.Silu), [rg], [sg])
                    for h in range(4):
                        K.op(DVE, lambda e: e.scalar_tensor_tensor(
                            out=mixr[:, 256 * h:256 * h + 256], in0=o32[:, 256 * h:256 * h + 256], scalar=sk[:, 8 + h:9 + h],
                            in1=sg[:, 256 * h:256 * h + 256], op0=ALU.mult, op1=ALU.mult), [o32, sk, sg], [mixr])
                    for c in range(8):
                        K.op(PE, lambda e: e.transpose(out=ps_t[:, 128 * c:128 * c + 128], in_=mixr[:, 128 * c:128 * c + 128],
                                                       identity=ident[:]), [mixr, ident], [ps_t], selfsync=False)
                    mt = mT[n % 2]
                    K.op(ACT, lambda e: e.activation(out=mt[:], in_=ps_t[:], func=AF.Copy), [ps_t], [mt])
                    K.dma(SP, MIXT, MIXT[:, 0:8, b * 128:(b + 1) * 128], mt, mt[:].rearrange("p (c t) -> p c t", c=8),
                          chk_dst=False)
            K.barrier()

        for si, s in enumerate(cfg.seqs):
            own = list(range(s["own0"], s["own0"] + s["nown"]))
            nb = s["nb"]
            with contextlib.ExitStack() as st:
                KT = K.sb(st, "KT", [128, 2, nb * 128], BF16, dma=True)
                V = K.sb(st, "V", [128, nb, 256], BF16, dma=True)
                npc = 4 if nb >= 8 else 1
                bounds = [nb * q // npc for q in range(npc + 1)]
                for q in range(npc):
                    a, b_ = bounds[q], bounds[q + 1]
                    K.dma(SP, KT, KT[:, :, a * 128:b_ * 128], KTs[si], KTs[si][:, :, a * 128:b_ * 128], chk_dst=(q == 0))
                    K.dma(SP, V, V[:, a:b_, :], Vs[si], Vs[si][:, a:b_, :], chk_dst=(q == 0))
                qT = [K.sb(st, f"qT{i}", [128, 8, 128], BF16, dma=True) for i in range(2)]
                ps_s = [K.ps(st, f"pss{j}", [128, 512]) for j in range(2)]
                ps_ot = [K.ps(st, f"pso{j}", [128, 512]) for j in range(2)]
                ps_dn = [K.ps(st, f"psd{j}", [128, 512]) for j in range(2)]
                PTs = [K.sb(st, f"PTa{i}", [128, 512], BF16) for i in range(3)]
                rden = K.sb(st, "rden", [128, 512], F32)
                ma = [K.sb(st, f"ma{i}", [128, 512], BF16) for i in range(2)]
                K.dma(SP, qT[0], qT[0][:], QT, QT[own[0]])
                cnt = 0
                for n, b in enumerate(own):
                    if n + 1 < len(own):
                        K.dma(SP, qT[(n + 1) % 2], qT[(n + 1) % 2][:], QT, QT[own[n + 1]])
                    q_ = qT[n % 2]
                    for kvh in range(2):
                        pot, pdn = ps_ot[kvh], ps_dn[kvh]
                        qr = q_[:, 4 * kvh:4 * kvh + 4, :].rearrange("p h t -> p (h t)")
                        for kb in range(nb):
                            nk = 16 if kb == 0 else 128
                            pss, pt = ps_s[cnt % 2], PTs[cnt % 3]
                            cnt += 1
                            K.op(PE, lambda e: e.matmul(pss[0:nk, :], lhsT=KT[:, kvh, kb * 128:kb * 128 + nk], rhs=qr,
                                                        start=True, stop=True), [KT, q_], [pss], selfsync=False)
                            K.op(ACT, lambda e: e.activation(out=pt[0:nk, :], in_=pss[0:nk, :], func=AF.Exp,
                                                             bias=negB[0:nk, 0:1]), [pss, negB], [pt])
                            K.op(PE, lambda e: e.matmul(pot[:], lhsT=V[0:nk, kb, 128 * kvh:128 * kvh + 128], rhs=pt[0:nk, :],
                                                        start=(kb == 0), stop=(kb == nb - 1)), [V, pt], [pot], selfsync=False)
                            K.op(PE, lambda e: e.matmul(pdn[:], lhsT=ones[0:nk, :], rhs=pt[0:nk, :],
                                                        start=(kb == 0), stop=(kb == nb - 1)), [ones, pt], [pdn], selfsync=False)
                        K.op(DVE, lambda e: e.reciprocal(out=rden[:], in_=pdn[:]), [pdn], [rden])
                        m_ = ma[kvh]
                        K.op(DVE, lambda e: e.tensor_tensor(out=m_[:], in0=pot[:], in1=rden[:], op=ALU.mult), [pot, rden], [m_])
                        K.dma(SP, MIXT, MIXT[:, 8 + 4 * kvh:12 + 4 * kvh, b * 128:(b + 1) * 128], m_,
                              m_[:].rearrange("p (h t) -> p h t", h=4), chk_dst=False)
            K.barrier()

        NT = NOWN // 4
        with contextlib.ExitStack() as st:
            hT = K.sb(st, "hT", [128, KC, 512], F32, dma=True)
            actT = K.sb(st, "actT", [128, KC, 512], BF16, dma=True)
            aT = K.sb(st, "aT", [128, 64, 512], BF16)
            wb = [K.sb(st, f"wb{i}", [128, 8192], BF16, dma=True) for i in range(2)]
            rt = K.sb(st, "rt", [128, 512], F32)
            rstd = K.sb(st, "rstd", [128, 512], F32)
            rl = [K.sb(st, f"rl{i}", [128, 512], F32) for i in range(2)]
            pm = [K.ps(st, f"pm{j}", [128, 512]) for j in range(4)]
            ps_stat = K.ps(st, "pstat", [128, 512])
            slabs = []
            for t in range(NT):
                for g in range(4):
                    slabs.append((Wouts, Wouts[g].rearrange("p k j -> p (k j)")))
                for g in range(16):
                    slabs.append((Wups, Wups[g].rearrange("p k j -> p (k j)")))
                for g in range(16):
                    slabs.append((Wdowns, Wdowns[g].rearrange("p f j -> p (f j)")))
            wi = [0]

            def issue_slab():
                i = wi[0]
                if i < len(slabs):
                    K.dma(SP, wb[i % 2], wb[i % 2][:], slabs[i][0], slabs[i][1])
                wi[0] += 1

            used = [0]

            def next_slab():
                i = used[0]
                used[0] += 1
                return wb[i % 2]

            issue_slab()
            cnt = 0
            for t in range(NT):
                tok = slice(t * 512, (t + 1) * 512)
                K.dma(SP, hT, hT[:].rearrange("p k (b t) -> p k b t", b=4), xown,
                      xown[4 * t:4 * t + 4].rearrange("b p k t -> p k b t"))
                K.dma(SP, actT, actT[:], MIXT, MIXT[:, :, tok])
                for g in range(4):
                    issue_slab()
                    w = next_slab()
                    wv = w[:].rearrange("p (k j) -> p k j", k=KC)
                    for dd in range(4):
                        dc = 4 * g + dd
                        p_ = pm[cnt % 4]
                        cnt += 1
                        for kc in range(KC):
                            K.op(PE, lambda e: e.matmul(p_[:], lhsT=wv[:, kc, 128 * dd:128 * dd + 128], rhs=actT[:, kc, :],
                                                        start=(kc == 0), stop=(kc == KC - 1)), [w, actT], [p_], selfsync=False)
                        K.op(DVE, lambda e: e.tensor_tensor(out=hT[:, dc, :], in0=p_[:], in1=hT[:, dc, :], op=ALU.add),
                             [p_, hT], [hT])
                K.op(ACT, lambda e: e.activation(out=actT[:], in_=hT[:], func=AF.Square), [hT], [actT])
                for kc in range(KC):
                    K.op(PE, lambda e: e.matmul(ps_stat[:], lhsT=ones[:], rhs=actT[:, kc, :], start=(kc == 0),
                                                stop=(kc == KC - 1)), [ones, actT], [ps_stat], selfsync=False)
                rstd_from(None, ps_stat, ps_stat[:], 1.0 / D, rt, rt[:], rstd, rstd[:])
                for (E, a, b_) in ((DVE, 0, 8), (POOL, 8, 16)):
                    K.op(E, lambda e: e.tensor_tensor(out=actT[:, a:b_, :], in0=hT[:, a:b_, :],
                                                      in1=rstd[:].unsqueeze(1).to_broadcast([128, b_ - a, 512]),
                                                      op=ALU.mult), [hT, rstd], [actT])
                for g in range(16):
                    issue_slab()
                    w = next_slab()
                    wv = w[:].rearrange("p (k j) -> p k j", k=KC)
                    for ff in range(4):
                        fc = 4 * g + ff
                        p_ = pm[cnt % 4]
                        r_ = rl[cnt % 2]
                        cnt += 1
                        for kc in range(KC):
                            K.op(PE, lambda e: e.matmul(p_[:], lhsT=wv[:, kc, 128 * ff:128 * ff + 128], rhs=actT[:, kc, :],
                                                        start=(kc == 0), stop=(kc == KC - 1)), [w, actT], [p_], selfsync=False)
                        K.op(ACT, lambda e: e.activation(out=r_[:], in_=p_[:], func=AF.Relu), [p_], [r_])
                        K.op(POOL if fc % 2 == 0 else DVE, lambda e: e.tensor_tensor(out=aT[:, fc, :], in0=r_[:], in1=r_[:],
                                                                                      op=ALU.mult), [r_], [aT])
                for dc in range(16):
                    issue_slab()
                    w = next_slab()
                    wv = w[:].rearrange("p (f j) -> p f j", f=64)
                    p_ = pm[cnt % 4]
                    cnt += 1
                    for fc in range(64):
                        K.op(PE, lambda e: e.matmul(p_[:], lhsT=wv[:, fc, :], rhs=aT[:, fc, :], start=(fc == 0),
                                                    stop=(fc == 63)), [w, aT], [p_], selfsync=False)
                    K.op(DVE, lambda e: e.tensor_tensor(out=hT[:, dc, :], in0=p_[:], in1=hT[:, dc, :], op=ALU.add),
                         [p_, hT], [hT])
                K.op(ACT, lambda e: e.activation(out=actT[:], in_=hT[:], func=AF.Square), [hT], [actT])
                for kc in range(KC):
                    K.op(PE, lambda e: e.matmul(ps_stat[:], lhsT=ones[:], rhs=actT[:, kc, :], start=(kc == 0),
                                                stop=(kc == KC - 1)), [ones, actT], [ps_stat], selfsync=False)
                rstd_from(None, ps_stat, ps_stat[:], 1.0 / D, rt, rt[:], rstd, rstd[:])
                gfc = C("gfc")
                for dc in range(16):
                    K.op(DVE, lambda e: e.scalar_tensor_tensor(
                        out=hT[:, dc, :], in0=hT[:, dc, :], scalar=gfc[:, dc:dc + 1], in1=rstd[:], op0=ALU.mult,
                        op1=ALU.mult), [hT, cst, rstd], [hT])
                K.dma(SP, yT, yT[:, :, tok], hT, hT[:], chk_dst=False)
        SP.waited.pop(id(yT.dsem), None)
        SP.wait((yT.dsem, yT.dval))
    K.es.close()
    return nc


N_META = 16
GRID_W = 64
ROPE_THETA = 10000.0


def _blocks_T(h):
    nb = h.shape[0] // 128
    return np.ascontiguousarray(h.reshape(nb, 128, KC, 128).transpose(0, 3, 2, 1))


def _consts(q_g, k_g, lgf, lgb, g1, g2, gf):
    c = np.zeros((128, NCST), np.float32)

    def put(name, arr):
        o, w = C_OFF[name]
        c[:, o:o + w] = arr

    idx = np.arange(128, dtype=np.float32)
    dmat = idx[None, :] - idx[:, None]
    put("ident", np.eye(128, dtype=np.float32))
    put("dp", np.maximum(dmat, 0.0))
    put("dn", np.maximum(-dmat, 0.0))
    put("cp1", np.broadcast_to(idx[None, :] + 1.0, (128, 128)))
    put("c128m", np.broadcast_to(128.0 - idx[None, :], (128, 128)))
    rfreq = (np.float32(ROPE_THETA) ** (-np.linspace(0.0, 1.0, 64, dtype=np.float32))).astype(np.float32)
    afreq = (np.float32(ROPE_THETA) ** (-np.arange(32, dtype=np.float32) / np.float32(32))).astype(np.float32)
    put("rfreq", np.broadcast_to(rfreq[None], (128, 64)))
    put("afreq", np.broadcast_to(afreq[None], (128, 32)))
    put("g1c", g1.reshape(KC, 128).T)
    put("g2c", g2.reshape(KC, 128).T)
    put("gfc", gf.reshape(KC, 128).T)
    put("qg", np.broadcast_to(q_g[None], (128, 128)))
    put("kg", np.broadcast_to(k_g[None], (128, 128)))
    put("lgf", np.broadcast_to(lgf[None], (128, 4)))
    put("lgb", np.broadcast_to(lgb[None], (128, 4)))
    put("colc", idx[:, None])
    put("col127", 127.0 - idx[:, None])
    return c


def _seq_meta(nb_real):
    L = (nb_real + 1) * 128
    pos = np.zeros(L, np.float32)
    row = np.zeros(L, np.float32)
    col = np.zeros(L, np.float32)
    valid = np.zeros(L, np.float32)
    pos[0:16] = 112 + np.arange(16)
    valid[0:16] = 1
    j = np.arange(nb_real * 128)
    pos[128:] = 128 + j
    row[128:] = j // GRID_W
    col[128:] = j % GRID_W
    valid[128:] = 1
    return pos, row, col, valid


def prepare(cfg, x_prompt, x_sample, meta_tokens, ln1_g, w_in, q_norm_g, k_norm_g, ret_log_decay_fwd,
            ret_log_decay_bwd, w_out, ln2_g, w_up, w_down, final_norm_g):
    f32 = np.float32
    cst = _consts(np.asarray(q_norm_g[0], f32), np.asarray(k_norm_g[0], f32), np.asarray(ret_log_decay_fwd[0], f32),
                  np.asarray(ret_log_decay_bwd[0], f32), np.asarray(ln1_g[0], f32), np.asarray(ln2_g[0], f32),
                  np.asarray(final_norm_g, f32))
    mblk = np.zeros((128, D), f32)
    mblk[0:16] = meta_tokens
    mblkT = _blocks_T(mblk)
    xs_T = _blocks_T(np.asarray(x_sample[0], f32))
    xp_T = [_blocks_T(np.asarray(x_prompt[b], f32)) for b in range(x_prompt.shape[0])]
    shared = dict(cst=cst, w_in=np.ascontiguousarray(w_in[0], f32), w_out=np.ascontiguousarray(w_out[0], f32),
                  w_up=np.ascontiguousarray(w_up[0], f32), w_down=np.ascontiguousarray(w_down[0], f32))
    in_maps = []
    for c in range(8):
        pb, ph = c // 2, c % 2
        xb1 = np.concatenate([mblkT, xp_T[pb], mblkT, xs_T], axis=0)
        own_p = list(range(ph * cfg.OWN_P, (ph + 1) * cfg.OWN_P))
        own_s = list(range(c * cfg.OWN_S, (c + 1) * cfg.OWN_S))
        xown = np.concatenate([xp_T[pb][own_p], xs_T[own_s]], axis=0)
        metas, metao = [], []
        for (nbr, ownl) in ((cfg.NBP, own_p), (cfg.NBS, own_s)):
            pos, row, col, valid = _seq_meta(nbr)
            start = 128.0 + ownl[0] * 128
            end = 128.0 + (ownl[-1] + 1) * 128
            mf = ((pos < start) & (valid > 0)).astype(f32)
            mb = ((pos >= end) & (valid > 0)).astype(f32)
            df = np.where(mf > 0, start - 1.0 - pos, 0.0).astype(f32)
            db = np.where(mb > 0, pos - end, 0.0).astype(f32)
            m = np.stack([pos, row, col, df, mf, db, mb], axis=-1).reshape(nbr + 1, 128, 7)
            metas.append(m)
            osl = [1 + o for o in ownl]
            metao.append(m[osl][:, :, 0:3])
        meta1 = np.ascontiguousarray(np.concatenate(metas, axis=0).transpose(1, 0, 2))
        metao = np.ascontiguousarray(np.concatenate(metao, axis=0).transpose(1, 0, 2))
        d = dict(xb1=np.ascontiguousarray(xb1), xown=np.ascontiguousarray(xown), meta1=meta1, metao=metao)
        d.update(shared)
        in_maps.append(d)
    return in_maps


def assemble(cfg, results, nbatch):
    yp = np.zeros((nbatch, cfg.NBP * 128, D), np.float32)
    ys = np.zeros((1, cfg.NBS * 128, D), np.float32)
    for c in range(8):
        y = results[c]["yT"]
        y = y.transpose(2, 1, 0).reshape(cfg.NOWN * 128, D)
        pb, ph = c // 2, c % 2
        np_ = cfg.OWN_P * 128
        yp[pb, ph * np_:(ph + 1) * np_] = y[0:np_]
        ns_ = cfg.OWN_S * 128
        ys[0, c * ns_:(c + 1) * ns_] = y[np_:np_ + ns_]
    return yp, ys


_NC_CACHE = {}


def kernel(x_prompt, x_sample, meta_tokens, ln1_g, w_in, q_norm_g, k_norm_g, ret_log_decay_fwd,
           ret_log_decay_bwd, w_out, ln2_g, w_up, w_down, final_norm_g):
    cfg = Cfg(16, 128)
    args = [np.asarray(a) for a in (x_prompt, x_sample, meta_tokens, ln1_g, w_in, q_norm_g, k_norm_g,
                                    ret_log_decay_fwd, ret_log_decay_bwd, w_out, ln2_g, w_up, w_down, final_norm_g)]
    in_maps = prepare(cfg, *args)
    nc = build(cfg)
    res = run_bass_kernel_spmd(nc, in_maps, core_ids=list(range(8)))
    yp, ys = assemble(cfg, res.results, args[0].shape[0])
    return (yp, ys)
```

```python
import contextlib
import math
import numpy as np
import ml_dtypes
import concourse.bass as bass
import concourse.mybir as mybir
from concourse.bass_utils import run_bass_kernel_spmd

F32 = mybir.dt.float32
BF16 = mybir.dt.bfloat16
I32 = mybir.dt.int32
AF = mybir.ActivationFunctionType
ALU = mybir.AluOpType
AX = mybir.AxisListType

D = 2048
KC = 16
DFF = 8192
EPS = 1e-6
TWO_PI = 2.0 * math.pi


class Buf:
    def __init__(self, t, name):
        self.t = t
        self.name = name
        self.w = None
        self.r = {}
        self.dsem = None
        self.dval = 0

    def __getitem__(self, key):
        return self.t[key]


class Eng:
    def __init__(self, name, eng, sem):
        self.name = name
        self.e = eng
        self.sem = sem
        self.count = 0
        self.waited = {}

    def wait(self, dep):
        sem, val = dep
        if self.waited.get(id(sem), 0) >= val:
            return
        self.waited[id(sem)] = val
        self.e.wait_ge(sem, val)


class Kern:
    def __init__(self, nc, n_dma_sems=96):
        self.nc = nc
        self.es = contextlib.ExitStack()
        self.uid = 0
        self.PE = self._mk("pe", nc.tensor)
        self.DVE = self._mk("dve", nc.vector)
        self.ACT = self._mk("act", nc.scalar)
        self.POOL = self._mk("pool", nc.gpsimd)
        self.SP = self._mk("sp", nc.sync)
        self.engs = [self.PE, self.DVE, self.ACT, self.POOL, self.SP]
        self.pool = [[self.es.enter_context(nc.semaphore(f"dq{i}")), 0] for i in range(n_dma_sems)]
        self.live = []

    def _mk(self, name, eng):
        return Eng(name, eng, self.es.enter_context(self.nc.semaphore("s_" + name)))

    def _give_sem(self, b):
        ent = self.pool.pop()
        b.dsem, b.dval = ent[0], ent[1]
        b._ent = ent
        self.live.append(b)

    def release(self, bufs):
        for b in bufs:
            if b.dsem is not None and b in self.live:
                self.live.remove(b)
                b._ent[1] = b.dval
                self.pool.append(b._ent)

    def sb(self, stack, name, shape, dt, dma=False):
        self.uid += 1
        t = stack.enter_context(self.nc.sbuf_tensor(f"{name}_{self.uid}", list(shape), dt))
        b = Buf(t, name)
        if dma:
            self._give_sem(b)
        stack.callback(self.release, [b])
        return b

    def ps(self, stack, name, shape, dt=F32):
        self.uid += 1
        t = stack.enter_context(self.nc.psum_tensor(f"{name}_{self.uid}", list(shape), dt))
        return Buf(t, name)

    def dram(self, name, shape, dt, kind="Internal"):
        t = self.nc.dram_tensor(name, list(shape), dt, kind=kind)
        b = Buf(t.ap(), name)
        self._give_sem(b)
        return b

    def op(self, E, fn, reads=(), writes=(), selfsync=True):
        deps = []
        for b in reads:
            if b.w is not None:
                deps.append(b.w)
        for b in writes:
            if b.w is not None:
                deps.append(b.w)
            deps.extend(b.r.values())
        for d in deps:
            if d[0] is E.sem and not selfsync:
                continue
            E.wait(d)
        ins = fn(E.e)
        E.count += 1
        ins.then_inc(E.sem, 1)
        me = (E.sem, E.count)
        for b in reads:
            b.r[id(E.sem)] = me
        for b in writes:
            b.w = me
            b.r = {}
        return ins

    def dma(self, Q, dst, dst_ap, src, src_ap, chk_dst=True, **kw):
        deps = []
        if src.w is not None:
            deps.append(src.w)
        if chk_dst:
            if dst.w is not None:
                deps.append(dst.w)
            deps.extend(dst.r.values())
        for d in deps:
            Q.wait(d)
        ins = Q.e.dma_start(out=dst_ap, in_=src_ap, **kw)
        dst.dval += 16
        ins.then_inc(dst.dsem, 16)
        me = (dst.dsem, dst.dval)
        src.r[id(dst.dsem)] = me
        dst.w = me
        if chk_dst:
            dst.r = {}
        return ins

    def barrier(self):
        marks = [(E.sem, E.count) for E in self.engs if E.count > 0]
        marks += [(b.dsem, b.dval) for b in self.live if b.dval > 0]
        marks += [(ent[0], ent[1]) for ent in self.pool if ent[1] > 0]
        for E in self.engs:
            for m in marks:
                if m[0] is E.sem:
                    continue
                E.wait(m)


C_OFF = {}
_o = 0
for _n, _w in [("ident", 128), ("dp", 128), ("dn", 128), ("cp1", 128), ("c128m", 128), ("rfreq", 64),
               ("afreq", 32), ("g1c", 16), ("g2c", 16), ("gfc", 16), ("qg", 128), ("kg", 128), ("lgf", 4),
               ("lgb", 4), ("colc", 1), ("col127", 1)]:
    C_OFF[_n] = (_o, _w)
    _o += _w
NCST = _o


class Cfg:
    def __init__(self, nbp=16, nbs=128):
        self.NBP = nbp
        self.NBS = nbs
        self.OWN_P = nbp // 2
        self.OWN_S = nbs // 8
        self.NOWN = self.OWN_P + self.OWN_S
        self.NB1 = nbp + 1 + nbs + 1
        assert self.NOWN % 4 == 0
        self.seqs = [dict(name="p", nb=nbp + 1, b0=0, own0=0, nown=self.OWN_P),
                     dict(name="s", nb=nbs + 1, b0=nbp + 1, own0=self.OWN_P, nown=self.OWN_S)]


def build(cfg, debug=False):
    nc = bass.Bass("TRN2", target_bir_lowering=False)
    K = Kern(nc)
    NB1, NOWN = cfg.NB1, cfg.NOWN
    PE, DVE, ACT, POOL, SP = K.PE, K.DVE, K.ACT, K.POOL, K.SP
    skind = "ExternalOutput" if debug else "Internal"

    xb1 = K.dram("xb1", [NB1, 128, KC, 128], F32, "ExternalInput")
    xown = K.dram("xown", [NOWN, 128, KC, 128], F32, "ExternalInput")
    meta1 = K.dram("meta1", [128, NB1, 7], F32, "ExternalInput")
    metao = K.dram("metao", [128, NOWN, 3], F32, "ExternalInput")
    cst_d = K.dram("cst", [128, NCST], F32, "ExternalInput")
    w_in = K.dram("w_in", [D, 4608], F32, "ExternalInput")
    w_out = K.dram("w_out", [D, D], F32, "ExternalInput")
    w_up = K.dram("w_up", [D, DFF], F32, "ExternalInput")
    w_down = K.dram("w_down", [DFF, D], F32, "ExternalInput")
    yT = K.dram("yT", [128, KC, NOWN * 128], F32, "ExternalOutput")

    W1s = K.dram("W1s", [128, KC, 2048], BF16)
    Wown = K.dram("Wown", [8, 128, KC, 512], BF16)
    Wouts = K.dram("Wouts", [4, 128, KC, 512], BF16)
    Wups = K.dram("Wups", [16, 128, KC, 512], BF16)
    Wdowns = K.dram("Wdowns", [16, 128, 64, 128], BF16)
    KTs = [K.dram(f"KT_{s['name']}", [128, 2, s["nb"] * 128], BF16, skind) for s in cfg.seqs]
    Vs = [K.dram(f"V_{s['name']}", [128, s["nb"], 256], BF16, skind) for s in cfg.seqs]
    PROJ = K.dram("PROJ", [NOWN, 128, 4096], F32, skind)
    QT = K.dram("QT", [NOWN, 128, 8, 128], BF16, skind)
    RO = K.dram("RO", [NOWN, 128, 1024], F32, skind)
    RQB = K.dram("RQB", [NOWN, 128, 512], BF16, skind)
    KVB = K.dram("KVB", [NOWN, 128, 1024], F32, skind)
    MIXT = K.dram("MIXT", [128, KC, NOWN * 128], BF16, skind)
    SDBG = K.dram("SDBG", [4, 128, 1024], F32, skind) if debug else None

    with contextlib.ExitStack() as G:
        cst = K.sb(G, "cst", [128, NCST], F32, dma=True)
        K.dma(SP, cst, cst[:], cst_d, cst_d[:])

        def C(name, rows=128):
            o, w = C_OFF[name]
            return cst[0:rows, o:o + w]

        ident = K.sb(G, "ident", [128, 128], BF16)
        ones = K.sb(G, "ones", [128, 128], BF16)
        epsb = K.sb(G, "epsb", [128, 1], F32)
        negB = K.sb(G, "negB", [128, 1], F32)
        kdf = K.sb(G, "kdf", [128, 4], F32)
        kdb = K.sb(G, "kdb", [128, 4], F32)
        cdf = K.sb(G, "cdf", [128, 4], F32)
        cdb = K.sb(G, "cdb", [128, 4], F32)
        wfb = K.sb(G, "wfb", [128, NB1, 8], F32)
        m1 = K.sb(G, "m1", [128, NB1, 7], F32, dma=True)
        mo = K.sb(G, "mo", [128, NOWN, 3], F32, dma=True)
        Sst = [[K.sb(G, f"S{d}{s['name']}", [128, 4, 256], F32) for d in "fb"] for s in cfg.seqs]
        tmpc = K.sb(G, "tmpc", [128, max(NB1, 128)], F32)
        tmpd = K.sb(G, "tmpd", [128, max(NB1, 128)], F32)

        K.dma(SP, m1, m1[:], meta1, meta1[:])
        K.dma(SP, mo, mo[:], metao, metao[:])

        K.op(DVE, lambda e: e.tensor_copy(out=ident[:], in_=C("ident")), [cst], [ident])
        K.op(DVE, lambda e: e.memset(ones[:], 1.0), [], [ones])
        K.op(DVE, lambda e: e.memset(epsb[:], EPS), [], [epsb])
        for s in range(2):
            for d in range(2):
                K.op(POOL, lambda e: e.memset(Sst[s][d][:], 0.0), [], [Sst[s][d]])
        K.op(DVE, lambda e: e.tensor_reduce(out=tmpc[:, 0:1], in_=C("qg"), axis=AX.X, op=ALU.max,
                                            apply_absolute_value=True), [cst], [tmpc])
        K.op(DVE, lambda e: e.tensor_reduce(out=tmpc[:, 1:2], in_=C("kg"), axis=AX.X, op=ALU.max,
                                            apply_absolute_value=True), [cst], [tmpc])
        K.op(DVE, lambda e: e.scalar_tensor_tensor(out=negB[:], in0=tmpc[:, 0:1], scalar=-math.sqrt(128.0),
                                                   in1=tmpc[:, 1:2], op0=ALU.mult, op1=ALU.mult), [tmpc], [negB])
        lgf, lgb = C("lgf"), C("lgb")
        def make_decay_tables(st):
            MT = K.sb(st, "MT", [128, 4, 128], F32)
            qdf = K.sb(st, "qdf", [128, 4, 128], F32)
            qdb = K.sb(st, "qdb", [128, 4, 128], F32)
            for h in range(4):
                K.op(DVE, lambda e: e.tensor_scalar(out=tmpc[:, 0:128], in0=C("dp"), scalar1=lgf[:, h:h + 1],
                                                    scalar2=None, op0=ALU.mult), [cst], [tmpc])
                K.op(DVE, lambda e: e.scalar_tensor_tensor(out=tmpd[:, 0:128], in0=C("dn"), scalar=lgb[:, h:h + 1],
                                                           in1=tmpc[:, 0:128], op0=ALU.mult, op1=ALU.add),
                     [cst, tmpc], [tmpd])
                K.op(ACT, lambda e: e.activation(out=MT[:, h, :], in_=tmpd[:, 0:128], func=AF.Exp), [tmpd], [MT])
                K.op(ACT, lambda e: e.activation(out=qdf[:, h, :], in_=C("cp1"), func=AF.Exp, scale=lgf[:, h:h + 1]),
                     [cst], [qdf])
                K.op(ACT, lambda e: e.activation(out=qdb[:, h, :], in_=C("c128m"), func=AF.Exp, scale=lgb[:, h:h + 1]),
                     [cst], [qdb])
            return MT, qdf, qdb

        for h in range(4):
            K.op(ACT, lambda e: e.activation(out=kdf[:, h:h + 1], in_=C("col127"), func=AF.Exp,
                                             scale=lgf[:, h:h + 1]), [cst], [kdf])
            K.op(ACT, lambda e: e.activation(out=kdb[:, h:h + 1], in_=C("colc"), func=AF.Exp,
                                             scale=lgb[:, h:h + 1]), [cst], [kdb])
            for d, (lg, dcol, mcol) in enumerate([(lgf, 3, 4), (lgb, 5, 6)]):
                K.op(ACT, lambda e: e.activation(out=tmpc[:, 0:NB1], in_=m1[:, :, dcol], func=AF.Exp,
                                                 scale=lg[:, h:h + 1]), [m1, cst], [tmpc])
                K.op(DVE, lambda e: e.tensor_tensor(out=wfb[:, :, 4 * d + h], in0=tmpc[:, 0:NB1],
                                                    in1=m1[:, :, mcol], op=ALU.mult), [tmpc, m1], [wfb])
        K.op(ACT, lambda e: e.activation(out=cdf[:], in_=lgf, func=AF.Exp, scale=128.0), [cst], [cdf])
        K.op(ACT, lambda e: e.activation(out=cdb[:], in_=lgb, func=AF.Exp, scale=128.0), [cst], [cdb])

        def cs_tmps(st, Gn):
            return (K.sb(st, "cs_a", [128, Gn, 64], F32), K.sb(st, "cs_k", [128, Gn, 64], I32),
                    K.sb(st, "cs_f", [128, Gn, 64], F32))

        def build_cs(st, name, tmps, Gn, scale):
            ang, ki, kf = tmps
            cs = K.sb(st, name + "_cs", [128, Gn, 128], F32)

            def fill(specs_now, Gc):
                for (fr, pos, off) in specs_now:
                    nf = fr.shape[-1]
                    K.op(DVE, lambda e: e.tensor_tensor(
                        out=ang[:, 0:Gc, off:off + nf], in0=fr.unsqueeze(1).to_broadcast([128, Gc, nf]),
                        in1=pos.unsqueeze(2).to_broadcast([128, Gc, nf]), op=ALU.mult), [cst, m1, mo], [ang])
                K.op(DVE, lambda e: e.tensor_scalar(out=ki[:, 0:Gc, :], in0=ang[:, 0:Gc, :], scalar1=1.0 / TWO_PI,
                                                    scalar2=None, op0=ALU.mult), [ang], [ki])
                K.op(DVE, lambda e: e.tensor_copy(out=kf[:, 0:Gc, :], in_=ki[:, 0:Gc, :]), [ki], [kf])
                K.op(DVE, lambda e: e.scalar_tensor_tensor(out=ang[:, 0:Gc, :], in0=kf[:, 0:Gc, :], scalar=-TWO_PI,
                                                           in1=ang[:, 0:Gc, :], op0=ALU.mult, op1=ALU.add),
                     [kf, ang], [ang])
                K.op(ACT, lambda e: e.activation(out=cs[:, 0:Gc, 0:64], in_=ang[:, 0:Gc, :], func=AF.Sin,
                                                 scale=1.0 - 1e-6), [ang], [cs])
                K.op(ACT, lambda e: e.activation(out=kf[:, 0:Gc, :], in_=ang[:, 0:Gc, :], func=AF.Sin,
                                                 scale=0.5), [ang], [kf])
                K.op(DVE, lambda e: e.scalar_tensor_tensor(out=kf[:, 0:Gc, :], in0=kf[:, 0:Gc, :], scalar=-2.0,
                                                           in1=kf[:, 0:Gc, :], op0=ALU.mult, op1=ALU.mult),
                     [kf], [kf])
                if scale == 1.0:
                    K.op(DVE, lambda e: e.tensor_scalar(out=cs[:, 0:Gc, 64:128], in0=kf[:, 0:Gc, :], scalar1=1.0,
                                                        scalar2=None, op0=ALU.add), [kf], [cs])
                else:
                    K.op(DVE, lambda e: e.tensor_scalar(out=cs[:, 0:Gc, 64:128], in0=kf[:, 0:Gc, :], scalar1=1.0,
                                                        scalar2=scale, op0=ALU.add, op1=ALU.mult), [kf], [cs])
                    K.op(DVE, lambda e: e.tensor_scalar(out=cs[:, 0:Gc, 0:64], in0=cs[:, 0:Gc, 0:64],
                                                        scalar1=scale, scalar2=None, op0=ALU.mult), [cs], [cs])
            return cs, fill

        def rope(E1, E2, xb, x_ap, nh, cs, g, ob, o_ap, tm):
            xv = x_ap.rearrange("p (h i t) -> p h i t", h=nh, t=2)
            ov = o_ap.rearrange("p (h i t) -> p h i t", h=nh, t=2)
            x1, x2 = xv[:, :, :, 0], xv[:, :, :, 1]
            sb_ = cs[:, g, 0:64].unsqueeze(1).to_broadcast([128, nh, 64])
            cb_ = cs[:, g, 64:128].unsqueeze(1).to_broadcast([128, nh, 64])
            t = [tm[i][:, 0:nh * 64].rearrange("p (h i) -> p h i", h=nh) for i in range(4)]
            K.op(E1, lambda e: e.tensor_tensor(out=t[0], in0=x1, in1=cb_, op=ALU.mult), [xb, cs], [tm[0]])
            K.op(E1, lambda e: e.tensor_tensor(out=t[1], in0=x2, in1=sb_, op=ALU.mult), [xb, cs], [tm[1]])
            K.op(E1, lambda e: e.tensor_tensor(out=ov[:, :, :, 0], in0=t[0], in1=t[1], op=ALU.subtract),
                 [tm[0], tm[1]], [ob])
            K.op(E2, lambda e: e.tensor_tensor(out=t[2], in0=x1, in1=sb_, op=ALU.mult), [xb, cs], [tm[2]])
            K.op(E2, lambda e: e.tensor_tensor(out=t[3], in0=x2, in1=cb_, op=ALU.mult), [xb, cs], [tm[3]])
            K.op(E2, lambda e: e.tensor_tensor(out=ov[:, :, :, 1], in0=t[2], in1=t[3], op=ALU.add),
                 [tm[2], tm[3]], [ob])

        def rstd_from(E_sq, ssq_b, ssq_ap, inv_n, tmp_b, tmp_ap, out_b, out_ap):
            rows = ssq_ap.shape[0]
            K.op(ACT, lambda e: e.activation(out=tmp_ap, in_=ssq_ap, func=AF.Sqrt, scale=inv_n,
                                             bias=epsb[0:rows, 0:1]), [ssq_b, epsb], [tmp_b])
            K.op(DVE, lambda e: e.reciprocal(out=out_ap, in_=tmp_ap), [tmp_b], [out_b])

        def norm_block(xTb, sqb, uTb, ps_stat, rt, rstd, ntok):
            K.op(ACT, lambda e: e.activation(out=sqb[:, :, 0:ntok], in_=xTb[:, :, 0:ntok], func=AF.Square),
                 [xTb], [sqb])
            for kc in range(KC):
                K.op(PE, lambda e: e.matmul(ps_stat[:, 0:ntok], lhsT=ones[:], rhs=sqb[:, kc, 0:ntok],
                                            start=(kc == 0), stop=(kc == KC - 1)), [ones, sqb], [ps_stat],
                     selfsync=False)
            rstd_from(None, ps_stat, ps_stat[:, 0:ntok], 1.0 / D, rt, rt[:, 0:ntok], rstd, rstd[:, 0:ntok])
            h = KC // 2
            for (E, a, b) in ((DVE, 0, h), (POOL, h, KC)):
                K.op(E, lambda e: e.tensor_tensor(
                    out=uTb[:, a:b, 0:ntok], in0=xTb[:, a:b, 0:ntok],
                    in1=rstd[:, 0:ntok].unsqueeze(1).to_broadcast([128, b - a, ntok]), op=ALU.mult),
                    [xTb, rstd], [uTb])

        cast_rr = [0]

        cast_engs = [[0, 1, 2]]

        def cast(srcb, src_ap, dstb, dst_ap, gcol=None):
            i = cast_engs[0][cast_rr[0] % len(cast_engs[0])]
            cast_rr[0] += 1
            if i == 0:
                if gcol is None:
                    K.op(DVE, lambda e: e.tensor_copy(out=dst_ap, in_=src_ap), [srcb], [dstb])
                else:
                    K.op(DVE, lambda e: e.tensor_scalar(out=dst_ap, in0=src_ap, scalar1=gcol, scalar2=None,
                                                        op0=ALU.mult), [srcb, cst], [dstb])
            elif i == 1:
                K.op(ACT, lambda e: e.activation(out=dst_ap, in_=src_ap, func=AF.Copy,
                                                 scale=(1.0 if gcol is None else gcol)), [srcb, cst], [dstb])
            else:
                if gcol is None:
                    K.op(POOL, lambda e: e.tensor_copy(out=dst_ap, in_=src_ap), [srcb], [dstb])
                else:
                    K.op(POOL, lambda e: e.tensor_scalar(out=dst_ap, in0=src_ap, scalar1=gcol, scalar2=None,
                                                         op0=ALU.mult), [srcb, cst], [dstb])

        with contextlib.ExitStack() as st:
            NBUF = 3
            stg = [K.sb(st, f"stg{i}", [128, 2048], F32, dma=True) for i in range(NBUF)]
            wbf = [K.sb(st, f"wbf{i}", [128, 2048], BF16) for i in range(NBUF)]
            g1c, g2c = C("g1c"), C("g2c")
            units = []
            for kc in range(KC):
                rows = slice(kc * 128, (kc + 1) * 128)
                units.append((w_in, w_in[rows, 0:2048], 2048, g1c[:, kc:kc + 1],
                              [(Wown, Wown[0:4, :, kc, :].rearrange("g p j -> p g j"), 0, 2048, 4),
                               (W1s, W1s[:, kc, 512:2048], 512, 1536, 0)]))
                units.append((w_in, w_in[rows, 2048:4096], 2048, g1c[:, kc:kc + 1],
                              [(Wown, Wown[4:8, :, kc, :].rearrange("g p j -> p g j"), 0, 2048, 4)]))
                units.append((w_in, w_in[rows, 4096:4608], 512, g1c[:, kc:kc + 1],
                              [(W1s, W1s[:, kc, 0:512], 0, 512, 0)]))
            for kc in range(KC):
                rows = slice(kc * 128, (kc + 1) * 128)
                units.append((w_out, w_out[rows, :], 2048, None,
                              [(Wouts, Wouts[:, :, kc, :].rearrange("g p j -> p g j"), 0, 2048, 4)]))
            for kc in range(KC):
                rows = slice(kc * 128, (kc + 1) * 128)
                for q in range(4):
                    units.append((w_up, w_up[rows, q * 2048:(q + 1) * 2048], 2048, g2c[:, kc:kc + 1],
                                  [(Wups, Wups[4 * q:4 * q + 4, :, kc, :].rearrange("g p j -> p g j"), 0, 2048, 4)]))
            for fc in range(64):
                rows = slice(fc * 128, (fc + 1) * 128)
                units.append((w_down, w_down[rows, :], 2048, None,
                              [(Wdowns, Wdowns[:, :, fc, :].rearrange("g p j -> p g j"), 0, 2048, 16)]))

            def run_units(ulist, stg, wbf, Q):
                nb_ = len(stg)

                def ld(i):
                    src, sap, n, _, _ = ulist[i]
                    K.dma(SP, stg[i % nb_], stg[i % nb_][:, 0:n], src, sap)

                for i in range(min(nb_ - 1, len(ulist))):
                    ld(i)
                for i, (src, sap, n, gcol, stores) in enumerate(ulist):
                    if i + nb_ - 1 < len(ulist):
                        ld(i + nb_ - 1)
                    s_, w_ = stg[i % nb_], wbf[i % nb_]
                    cast(s_, s_[:, 0:n], w_, w_[:, 0:n], gcol)
                    for (dstb, dap, off, nn, ng) in stores:
                        sap2 = w_[:, off:off + nn]
                        if ng:
                            sap2 = sap2.rearrange("p (g j) -> p g j", g=ng)
                        K.dma(Q, dstb, dap, w_, sap2, chk_dst=False)
                    yield i

            n_in = 3 * KC
            for _ in run_units(units[:n_in], stg, wbf, ACT):
                pass
            late_units = units[n_in:]
        K.barrier()

        GT = 4
        with contextlib.ExitStack() as st:
            W1 = K.sb(st, "W1", [128, KC, 2048], BF16, dma=True)
            stg1 = [K.sb(st, f"stgl{i}", [128, 2048], F32, dma=True) for i in range(2)]
            wbf1 = [K.sb(st, f"wbfl{i}", [128, 2048], BF16) for i in range(2)]
            late = run_units(late_units[:KC], stg1, wbf1, SP)
            for q in range(4):
                K.dma(SP, W1, W1[:, 4 * q:4 * q + 4, :], W1s, W1s[:, 4 * q:4 * q + 4, :], chk_dst=(q == 0))
            NX = 3
            xTs = [K.sb(st, f"xT{i}", [128, KC, 128], F32, dma=True) for i in range(NX)]
            sqbs = [K.sb(st, f"sq{i}", [128, KC, 128], BF16) for i in range(2)]
            uTs = [K.sb(st, f"uT{i}", [128, KC, 128], BF16) for i in range(2)]
            rt = K.sb(st, "rt", [128, 128], F32)
            rstd = K.sb(st, "rstd", [128, 128], F32)
            pp = [K.ps(st, f"pp{j}", [128, 512]) for j in range(4)]
            ps_stat = K.ps(st, "pstat", [128, 512])
            ps_tr = K.ps(st, "ptr", [128, 1024], BF16)
            ps_kv = [K.ps(st, f"pkv{j}", [128, 512]) for j in range(2)]
            cst_ = cs_tmps(st, GT)
            cs_r = [build_cs(st, f"csr{i}", cst_, GT, 128.0 ** -0.5) for i in range(2)]
            cs_a = [build_cs(st, f"csa{i}", cst_, GT, 1.0) for i in range(2)]
            rk32 = K.sb(st, "rk32", [128, 512], F32)
            ak32 = K.sb(st, "ak32", [128, 256], F32)
            akn = K.sb(st, "akn", [128, 256], F32)
            tm = [K.sb(st, f"tm{i}", [128, 256], F32) for i in range(4)]
            tm2 = [K.sb(st, f"tn{i}", [128, 256], F32) for i in range(4)]
            rk_r = K.sb(st, "rk_r", [128, 512], BF16)
            kws = [[K.sb(st, f"kw{j}{i}", [128, 512], BF16) for i in range(2)] for j in range(2)]
            rv_bfs = [K.sb(st, f"rv_bf{j}", [128, 1024], BF16) for j in range(2)]
            ak_rs = [K.sb(st, f"ak_r{j}", [128, 256], BF16) for j in range(2)]
            av_bf = [K.sb(st, f"av_bf{i}", [128, 256], BF16) for i in range(2)]
            ktb = [K.sb(st, f"ktb{i}", [128, 2, 128], BF16) for i in range(2)]
            sk = K.sb(st, "sk", [128, 8], F32)
            junk = K.sb(st, "junk", [128, 128], F32)

            def load_x(i):
                K.dma(SP, xTs[i % NX], xTs[i % NX][:], xb1, xb1[i])

            blocks = []
            for si, s in enumerate(cfg.seqs):
                for bl in range(s["nb"]):
                    blocks.append((s["b0"] + bl, si, bl))

            def stage_sq(i):
                xTb, sqb = xTs[i % NX], sqbs[i % 2]
                K.op(ACT, lambda e: e.activation(out=sqb[:], in_=xTb[:], func=AF.Square), [xTb], [sqb])

            def stage_stat(i):
                sqb = sqbs[i % 2]
                for kc in range(KC):
                    K.op(PE, lambda e: e.matmul(ps_stat[:, 0:128], lhsT=ones[:], rhs=sqb[:, kc, :],
                                                start=(kc == 0), stop=(kc == KC - 1)), [ones, sqb], [ps_stat],
                         selfsync=False)
                K.op(ACT, lambda e: e.activation(out=rt[:], in_=ps_stat[:, 0:128], func=AF.Sqrt, scale=1.0 / D,
                                                 bias=epsb[:, 0:1]), [ps_stat, epsb], [rt])

            def stage_uT(i):
                xTb, uTb = xTs[i % NX], uTs[i % 2]
                K.op(DVE, lambda e: e.reciprocal(out=rstd[:], in_=rt[:]), [rt], [rstd])
                h = KC // 2
                for (E, a_, b_) in ((DVE, 0, h), (POOL, h, KC)):
                    K.op(E, lambda e: e.tensor_tensor(
                        out=uTb[:, a_:b_, :], in0=xTb[:, a_:b_, :],
                        in1=rstd[:].unsqueeze(1).to_broadcast([128, b_ - a_, 128]), op=ALU.mult),
                        [xTb, rstd], [uTb])

            def stage_tables(i):
                g, gi = i % GT, (i // GT) % 2
                if g == 0:
                    Gc = min(GT, NB1 - i)
                    cs_r[gi][1]([(C("rfreq"), m1[:, i:i + Gc, 0], 0)], Gc)
                    cs_a[gi][1]([(C("afreq"), m1[:, i:i + Gc, 1], 0), (C("afreq"), m1[:, i:i + Gc, 2], 32)], Gc)

            def stage_proj(i, banks):
                uTb = uTs[i % 2]
                for j in banks:
                    for kc in range(KC):
                        K.op(PE, lambda e: e.matmul(pp[j][:], lhsT=uTb[:, kc, :], rhs=W1[:, kc, 512 * j:512 * j + 512],
                                                    start=(kc == 0), stop=(kc == KC - 1)), [uTb, W1], [pp[j]],
                             selfsync=False)

            def stage_post(i, si, bl):
                g, gi = i % GT, (i // GT) % 2
                csr, csa = cs_r[gi][0], cs_a[gi][0]
                kw, rv_bf, ak_r = kws[i % 2], rv_bfs[i % 2], ak_rs[i % 2]
                K.op(ACT, lambda e: e.activation(out=ak32[:], in_=pp[0][:, 0:256], func=AF.Copy), [pp[0]], [ak32])
                avb = av_bf[i % 2]
                K.op(ACT, lambda e: e.activation(out=avb[:], in_=pp[0][:, 256:512], func=AF.Copy), [pp[0]], [avb])
                K.dma(SP, Vs[si], Vs[si][:, bl, :], avb, avb[:], chk_dst=False)
                for h in range(2):
                    K.op(ACT, lambda e: e.activation(out=junk[:], in_=ak32[:, 128 * h:128 * h + 128], func=AF.Square,
                                                     accum_out=sk[:, h:h + 1]), [ak32], [junk, sk])
                rstd_from(None, sk, sk[:, 0:2], 1.0 / 128, sk, sk[:, 2:4], sk, sk[:, 4:6])
                for h in range(2):
                    K.op(DVE, lambda e: e.scalar_tensor_tensor(out=akn[:, 128 * h:128 * h + 128],
                                                               in0=ak32[:, 128 * h:128 * h + 128],
                                                               scalar=sk[:, 4 + h:5 + h], in1=C("kg"),
                                                               op0=ALU.mult, op1=ALU.mult), [ak32, sk, cst], [akn])
                rope(POOL, DVE, akn, akn[:], 2, csa, g, ak_r, ak_r[:], tm2)
                K.op(ACT, lambda e: e.activation(out=rk32[:], in_=pp[1][:], func=AF.Copy), [pp[1]], [rk32])
                K.op(ACT, lambda e: e.activation(out=rv_bf[:, 0:512], in_=pp[2][:], func=AF.Copy), [pp[2]], [rv_bf])
                K.op(ACT, lambda e: e.activation(out=rv_bf[:, 512:1024], in_=pp[3][:], func=AF.Copy), [pp[3]], [rv_bf])
                rope(DVE, POOL, rk32, rk32[:], 4, csr, g, rk_r, rk_r[:], tm)
                for d in range(2):
                    K.op(DVE if d == 0 else POOL, lambda e: e.tensor_tensor(
                        out=kw[d][:].rearrange("p (h x) -> p h x", h=4),
                        in0=rk_r[:].rearrange("p (h x) -> p h x", h=4),
                        in1=wfb[:, i, 4 * d:4 * d + 4].unsqueeze(2).to_broadcast([128, 4, 128]), op=ALU.mult),
                        [rk_r, wfb], [kw[d]])

            def stage_def_tr(i, si, bl):
                ak_r = ak_rs[i % 2]
                for h in range(2):
                    K.op(PE, lambda e: e.transpose(out=ps_tr[:, 128 * h:128 * h + 128], in_=ak_r[:, 128 * h:128 * h + 128],
                                                   identity=ident[:]), [ak_r, ident], [ps_tr], selfsync=False)
                kt = ktb[i % 2]
                K.op(DVE, lambda e: e.tensor_copy(out=kt[:].rearrange("p h t -> p (h t)"), in_=ps_tr[:, 0:256]),
                     [ps_tr], [kt])
                K.dma(SP, KTs[si], KTs[si][:, :, bl * 128:(bl + 1) * 128], kt, kt[:], chk_dst=False)

            def stage_def_state(i, si, bl, d):
                Sd = Sst[si][d]
                kw, rv_bf = kws[i % 2], rv_bfs[i % 2]
                for hp in range(2):
                    pk = ps_kv[hp]
                    for hh in range(2):
                        h = 2 * hp + hh
                        K.op(PE, lambda e: e.matmul(pk[:, 256 * hh:256 * hh + 256],
                                                    lhsT=kw[d][:, 128 * h:128 * h + 128],
                                                    rhs=rv_bf[:, 256 * h:256 * h + 256], start=True, stop=True),
                             [kw[d], rv_bf], [pk], selfsync=False)
                for hp in range(2):
                    pk = ps_kv[hp]
                    sv = Sd[:, 2 * hp:2 * hp + 2, :].rearrange("p h v -> p (h v)")
                    K.op(DVE, lambda e: e.tensor_tensor(out=sv, in0=sv, in1=pk[:], op=ALU.add), [Sd, pk], [Sd])

            for i0 in range(min(NX, NB1)):
                load_x(i0)
            stage_sq(0)
            if NB1 > 1:
                stage_sq(1)
            stage_stat(0)
            stage_uT(0)
            for idx, (i, si, bl) in enumerate(blocks):
                prev = blocks[idx - 1] if idx > 0 else None
                if i + 1 < NB1:
                    stage_stat(i + 1)
                if prev:
                    stage_def_tr(*prev)
                    stage_def_state(*prev, 0)
                if i + 2 < NB1:
                    stage_sq(i + 2)
                stage_tables(i)
                stage_proj(i, [0])
                if prev:
                    stage_def_state(*prev, 1)
                if i + 1 < NB1:
                    stage_uT(i + 1)
                if i + 3 < NB1:
                    load_x(i + 3)
                stage_proj(i, [1, 2, 3])
                stage_post(i, si, bl)
                next(late, None)
            stage_def_tr(*blocks[-1])
            stage_def_state(*blocks[-1], 0)
            stage_def_state(*blocks[-1], 1)
            for _ in late:
                pass
            if debug:
                for si in range(2):
                    for d in range(2):
                        K.dma(SP, SDBG, SDBG[2 * si + d], Sst[si][d], Sst[si][d][:].rearrange("p h v -> p (h v)"),
                              chk_dst=False)
        K.barrier()

        with contextlib.ExitStack() as st:
            uTo = [K.sb(st, f"uTo{b}", [128, KC, 128], BF16) for b in range(NOWN)]
            xTs = [K.sb(st, f"xT{i}", [128, KC, 128], F32, dma=True) for i in range(2)]
            sqb = K.sb(st, "sq", [128, KC, 128], BF16)
            rt = K.sb(st, "rt", [128, 128], F32)
            rstd = K.sb(st, "rstd", [128, 128], F32)
            ps_stat = K.ps(st, "pstat", [128, 512])
            pp = [K.ps(st, f"pp{j}", [128, 512]) for j in range(4)]
            wg = [K.sb(st, f"wg{i}", [128, KC, 512], BF16, dma=True) for i in range(2)]
            stg = [K.sb(st, f"pstg{i}", [128, 512], F32) for i in range(4)]
            K.dma(SP, wg[0], wg[0][:], Wown, Wown[0])
            K.dma(SP, xTs[0], xTs[0][:], xown, xown[0])
            for b in range(NOWN):
                if b + 1 < NOWN:
                    K.dma(SP, xTs[(b + 1) % 2], xTs[(b + 1) % 2][:], xown, xown[b + 1])
                norm_block(xTs[b % 2], sqb, uTo[b], ps_stat, rt, rstd, 128)
            n = 0
            for cg in range(8):
                if cg + 1 < 8:
                    K.dma(SP, wg[(cg + 1) % 2], wg[(cg + 1) % 2][:], Wown, Wown[cg + 1])
                w = wg[cg % 2]
                for b in range(NOWN):
                    p_, s_ = pp[n % 4], stg[n % 4]
                    for kc in range(KC):
                        K.op(PE, lambda e: e.matmul(p_[:], lhsT=uTo[b][:, kc, :], rhs=w[:, kc, :], start=(kc == 0),
                                                    stop=(kc == KC - 1)), [uTo[b], w], [p_], selfsync=False)
                    if n % 2 == 0:
                        K.op(ACT, lambda e: e.activation(out=s_[:], in_=p_[:], func=AF.Copy), [p_], [s_])
                    else:
                        K.op(DVE, lambda e: e.tensor_copy(out=s_[:], in_=p_[:]), [p_], [s_])
                    K.dma(SP, PROJ, PROJ[b, :, 512 * cg:512 * cg + 512], s_, s_[:], chk_dst=False)
                    n += 1
        K.barrier()

        for si, s in enumerate(cfg.seqs):
            Sf, Sb = Sst[si]
            own = list(range(s["own0"], s["own0"] + s["nown"]))
            no = len(own)
            with contextlib.ExitStack() as st:
                pr = [K.sb(st, f"pr{i}", [128, 4096], F32, dma=True) for i in range(2)]
                MT, qdf, qdb = make_decay_tables(st)
                cst_ = cs_tmps(st, no)
                csr1 = build_cs(st, "csr1", cst_, no, 1.0)
                csrs = build_cs(st, "csrs", cst_, no, 128.0 ** -0.5)
                csa = build_cs(st, "csa", cst_, no, 128.0 ** -0.5)
                o0 = own[0]
                csr1[1]([(C("rfreq"), mo[:, o0:o0 + no, 0], 0)], no)
                csrs[1]([(C("rfreq"), mo[:, o0:o0 + no, 0], 0)], no)
                csa[1]([(C("afreq"), mo[:, o0:o0 + no, 1], 0), (C("afreq"), mo[:, o0:o0 + no, 2], 32)], no)
                tm = [K.sb(st, f"tm{i}", [128, 256], F32) for i in range(4)]
                tm2 = [K.sb(st, f"tn{i}", [128, 256], F32) for i in range(4)]
                tm3 = [K.sb(st, f"to{i}", [128, 512], F32) for i in range(4)]
                rq_r = K.sb(st, "rq_r", [128, 512], BF16)
                rk_r = K.sb(st, "rk_r", [128, 512], BF16)
                rv_bf = K.sb(st, "rv_bf", [128, 1024], BF16)
                aqsq = K.sb(st, "aqsq", [128, 1024], F32)
                aqn = K.sb(st, "aqn", [128, 1024], F32)
                aq_r = K.sb(st, "aq_r", [128, 1024], BF16)
                sk = K.sb(st, "sk", [128, 24], F32)
                ps_t1 = K.ps(st, "pt1", [128, 1024], BF16)
                ps_t2 = K.ps(st, "pt2", [128, 1024], BF16)
                ps_p = K.ps(st, "ppt", [128, 512])
                ps_o = [K.ps(st, f"po{j}", [128, 512]) for j in range(2)]
                ps_kv = [K.ps(st, f"pkv{j}", [128, 512]) for j in range(2)]
                rqT = K.sb(st, "rqT", [128, 512], BF16)
                rqfT = K.sb(st, "rqfT", [128, 512], BF16)
                rqbT = [K.sb(st, f"rqbT{i}", [128, 512], BF16) for i in range(2)]
                rkT = K.sb(st, "rkT", [128, 512], BF16)
                aqT = [K.sb(st, f"aqT{i}", [128, 1024], BF16) for i in range(2)]
                PT = K.sb(st, "PT", [128, 512], BF16)
                kw = [K.sb(st, f"kw{i}", [128, 512], BF16) for i in range(2)]
                Sf_bf = K.sb(st, "Sf_bf", [128, 1024], BF16)
                ro_s = [K.sb(st, f"ro_s{i}", [128, 1024], F32) for i in range(2)]
                kvb_s = [K.sb(st, f"kvb_s{i}", [128, 1024], F32) for i in range(2)]
                K.op(DVE, lambda e: e.tensor_copy(out=Sf_bf[:], in_=Sf[:].rearrange("p h v -> p (h v)")), [Sf], [Sf_bf])
                K.dma(SP, pr[0], pr[0][:], PROJ, PROJ[own[0]])
                for n, b in enumerate(own):
                    if n + 1 < no:
                        K.dma(SP, pr[(n + 1) % 2], pr[(n + 1) % 2][:], PROJ, PROJ[own[n + 1]])
                    p_ = pr[n % 2]
                    rope(DVE, POOL, p_, p_[:, 0:512], 4, csr1[0], n, rq_r, rq_r[:], tm)
                    rope(DVE, POOL, p_, p_[:, 512:1024], 4, csrs[0], n, rk_r, rk_r[:], tm2)
                    K.op(ACT, lambda e: e.activation(out=rv_bf[:], in_=p_[:, 1024:2048], func=AF.Copy), [p_], [rv_bf])
                    aq = p_[:, 3072:4096]
                    K.op(POOL, lambda e: e.tensor_tensor(out=aqsq[:], in0=aq, in1=aq, op=ALU.mult), [p_], [aqsq])
                    K.op(DVE, lambda e: e.tensor_reduce(out=sk[:, 0:8], in_=aqsq[:].rearrange("p (h x) -> p h x", h=8),
                                                        axis=AX.X, op=ALU.add), [aqsq], [sk])
                    rstd_from(None, sk, sk[:, 0:8], 1.0 / 128, sk, sk[:, 8:16], sk, sk[:, 16:24])
                    K.op(DVE, lambda e: e.tensor_tensor(
                        out=aqn[:].rearrange("p (h x) -> p h x", h=8), in0=aq.rearrange("p (h x) -> p h x", h=8),
                        in1=sk[:, 16:24].unsqueeze(2).to_broadcast([128, 8, 128]), op=ALU.mult), [p_, sk], [aqn])
                    K.op(POOL, lambda e: e.tensor_tensor(
                        out=aqn[:].rearrange("p (h x) -> p h x", h=8), in0=aqn[:].rearrange("p (h x) -> p h x", h=8),
                        in1=C("qg").unsqueeze(1).to_broadcast([128, 8, 128]), op=ALU.mult), [aqn, cst], [aqn])
                    rope(DVE, POOL, aqn, aqn[:], 8, csa[0], n, aq_r, aq_r[:], tm3)
                    for h in range(4):
                        K.op(PE, lambda e: e.transpose(out=ps_t1[:, 128 * h:128 * h + 128], in_=rq_r[:, 128 * h:128 * h + 128],
                                                       identity=ident[:]), [rq_r, ident], [ps_t1], selfsync=False)
                    for h in range(4):
                        K.op(PE, lambda e: e.transpose(out=ps_t1[:, 512 + 128 * h:640 + 128 * h],
                                                       in_=rk_r[:, 128 * h:128 * h + 128], identity=ident[:]),
                             [rk_r, ident], [ps_t1], selfsync=False)
                    for h in range(8):
                        K.op(PE, lambda e: e.transpose(out=ps_t2[:, 128 * h:128 * h + 128], in_=aq_r[:, 128 * h:128 * h + 128],
                                                       identity=ident[:]), [aq_r, ident], [ps_t2], selfsync=False)
                    rb = rqbT[n % 2]
                    K.op(ACT, lambda e: e.activation(out=rqT[:], in_=ps_t1[:, 0:512], func=AF.Copy), [ps_t1], [rqT])
                    K.op(DVE, lambda e: e.tensor_tensor(out=rqfT[:], in0=ps_t1[:, 0:512],
                                                        in1=qdf[:].rearrange("p h c -> p (h c)"), op=ALU.mult),
                         [ps_t1, qdf], [rqfT])
                    K.op(DVE, lambda e: e.tensor_tensor(out=rb[:], in0=ps_t1[:, 0:512],
                                                        in1=qdb[:].rearrange("p h c -> p (h c)"), op=ALU.mult),
                         [ps_t1, qdb], [rb])
                    K.op(ACT, lambda e: e.activation(out=rkT[:], in_=ps_t1[:, 512:1024], func=AF.Copy), [ps_t1], [rkT])
                    at = aqT[n % 2]
                    K.op(ACT, lambda e: e.activation(out=at[:], in_=ps_t2[:], func=AF.Copy), [ps_t2], [at])
                    K.dma(SP, QT, QT[b], at, at[:].rearrange("p (h t) -> p h t", h=8), chk_dst=False)
                    K.dma(SP, RQB, RQB[b], rb, rb[:], chk_dst=False)
                    for h in range(4):
                        K.op(PE, lambda e: e.matmul(ps_p[:, 128 * h:128 * h + 128], lhsT=rkT[:, 128 * h:128 * h + 128],
                                                    rhs=rqT[:, 128 * h:128 * h + 128], start=True, stop=True),
                             [rkT, rqT], [ps_p], selfsync=False)
                    K.op(DVE, lambda e: e.tensor_tensor(out=PT[:], in0=ps_p[:], in1=MT[:].rearrange("p h c -> p (h c)"),
                                                        op=ALU.mult), [ps_p, MT], [PT])
                    for h in range(4):
                        po = ps_o[h // 2]
                        osl = po[:, 256 * (h % 2):256 * (h % 2) + 256]
                        K.op(PE, lambda e: e.matmul(osl, lhsT=PT[:, 128 * h:128 * h + 128], rhs=rv_bf[:, 256 * h:256 * h + 256],
                                                    start=True, stop=False), [PT, rv_bf], [po], selfsync=False)
                        K.op(PE, lambda e: e.matmul(osl, lhsT=rqfT[:, 128 * h:128 * h + 128], rhs=Sf_bf[:, 256 * h:256 * h + 256],
                                                    start=False, stop=True), [rqfT, Sf_bf], [po], selfsync=False)
                    ros = ro_s[n % 2]
                    K.op(ACT, lambda e: e.activation(out=ros[:, 0:512], in_=ps_o[0][:], func=AF.Copy), [ps_o[0]], [ros])
                    K.op(ACT, lambda e: e.activation(out=ros[:, 512:1024], in_=ps_o[1][:], func=AF.Copy), [ps_o[1]], [ros])
                    K.dma(SP, RO, RO[b], ros, ros[:], chk_dst=False)
                    for d, kd in enumerate((kdf, kdb)):
                        K.op(POOL, lambda e: e.tensor_tensor(
                            out=kw[d][:].rearrange("p (h x) -> p h x", h=4), in0=rk_r[:].rearrange("p (h x) -> p h x", h=4),
                            in1=kd[:].unsqueeze(2).to_broadcast([128, 4, 128]), op=ALU.mult), [rk_r, kd], [kw[d]])
                    for hp in range(2):
                        pk = ps_kv[hp]
                        for hh in range(2):
                            h = 2 * hp + hh
                            K.op(PE, lambda e: e.matmul(pk[:, 256 * hh:256 * hh + 256], lhsT=kw[0][:, 128 * h:128 * h + 128],
                                                        rhs=rv_bf[:, 256 * h:256 * h + 256], start=True, stop=True),
                                 [kw[0], rv_bf], [pk], selfsync=False)
                        for hh in range(2):
                            h = 2 * hp + hh
                            K.op(DVE, lambda e: e.scalar_tensor_tensor(out=Sf[:, h, :], in0=Sf[:, h, :], scalar=cdf[:, h:h + 1],
                                                                       in1=pk[:, 256 * hh:256 * hh + 256], op0=ALU.mult,
                                                                       op1=ALU.add), [Sf, cdf, pk], [Sf])
                    K.op(POOL, lambda e: e.tensor_copy(out=Sf_bf[:], in_=Sf[:].rearrange("p h v -> p (h v)")), [Sf], [Sf_bf])
                    kvs = kvb_s[n % 2]
                    for hp in range(2):
                        pk = ps_kv[hp]
                        for hh in range(2):
                            h = 2 * hp + hh
                            K.op(PE, lambda e: e.matmul(pk[:, 256 * hh:256 * hh + 256], lhsT=kw[1][:, 128 * h:128 * h + 128],
                                                        rhs=rv_bf[:, 256 * h:256 * h + 256], start=True, stop=True),
                                 [kw[1], rv_bf], [pk], selfsync=False)
                        K.op(ACT, lambda e: e.activation(out=kvs[:, 512 * hp:512 * hp + 512], in_=pk[:], func=AF.Copy),
                             [pk], [kvs])
                    K.dma(SP, KVB, KVB[b], kvs, kvs[:], chk_dst=False)
            K.barrier()
            with contextlib.ExitStack() as st:
                ro_l = [K.sb(st, f"ro_l{i}", [128, 1024], F32, dma=True) for i in range(2)]
                rg_l = [K.sb(st, f"rg_l{i}", [128, 1024], F32, dma=True) for i in range(2)]
                rqb_l = [K.sb(st, f"rqb_l{i}", [128, 512], BF16, dma=True) for i in range(2)]
                kvb_l = [K.sb(st, f"kvb_l{i}", [128, 1024], F32, dma=True) for i in range(2)]
                Sb_bf = K.sb(st, "Sb_bf", [128, 1024], BF16)
                o32 = K.sb(st, "o32", [128, 1024], F32)
                osq = K.sb(st, "osq", [128, 1024], F32)
                sg = K.sb(st, "sg", [128, 1024], F32)
                mixr = K.sb(st, "mixr", [128, 1024], BF16)
                mT = [K.sb(st, f"mT{i}", [128, 1024], BF16) for i in range(2)]
                sk = K.sb(st, "sk", [128, 12], F32)
                ps_o = [K.ps(st, f"po{j}", [128, 512]) for j in range(2)]
                ps_t = K.ps(st, "pt", [128, 1024], BF16)

                def loadB(n):
                    b = own[n]
                    K.dma(SP, ro_l[n % 2], ro_l[n % 2][:], RO, RO[b])
                    K.dma(SP, rg_l[n % 2], rg_l[n % 2][:], PROJ, PROJ[b, :, 2048:3072])
                    K.dma(SP, rqb_l[n % 2], rqb_l[n % 2][:], RQB, RQB[b])
                    if n + 1 < no:
                        K.dma(SP, kvb_l[n % 2], kvb_l[n % 2][:], KVB, KVB[own[n + 1]])

                loadB(no - 1)
                for n in range(no - 1, -1, -1):
                    b = own[n]
                    if n - 1 >= 0:
                        loadB(n - 1)
                    if n + 1 < no:
                        kv = kvb_l[n % 2]
                        for h in range(4):
                            K.op(DVE, lambda e: e.scalar_tensor_tensor(out=Sb[:, h, :], in0=Sb[:, h, :], scalar=cdb[:, h:h + 1],
                                                                       in1=kv[:, 256 * h:256 * h + 256], op0=ALU.mult,
                                                                       op1=ALU.add), [Sb, cdb, kv], [Sb])
                    K.op(POOL, lambda e: e.tensor_copy(out=Sb_bf[:], in_=Sb[:].rearrange("p h v -> p (h v)")), [Sb], [Sb_bf])
                    rb, ro, rg = rqb_l[n % 2], ro_l[n % 2], rg_l[n % 2]
                    for h in range(4):
                        po = ps_o[h // 2]
                        K.op(PE, lambda e: e.matmul(po[:, 256 * (h % 2):256 * (h % 2) + 256], lhsT=rb[:, 128 * h:128 * h + 128],
                                                    rhs=Sb_bf[:, 256 * h:256 * h + 256], start=True, stop=True),
                             [rb, Sb_bf], [po], selfsync=False)
                    for hp in range(2):
                        K.op(DVE, lambda e: e.tensor_tensor(out=o32[:, 512 * hp:512 * hp + 512], in0=ps_o[hp][:],
                                                            in1=ro[:, 512 * hp:512 * hp + 512], op=ALU.add),
                             [ps_o[hp], ro], [o32])
                    K.op(POOL, lambda e: e.tensor_tensor(out=osq[:], in0=o32[:], in1=o32[:], op=ALU.mult), [o32], [osq])
                    K.op(DVE, lambda e: e.tensor_reduce(out=sk[:, 0:4], in_=osq[:].rearrange("p (h x) -> p h x", h=4),
                                                        axis=AX.X, op=ALU.add), [osq], [sk])
                    rstd_from(None, sk, sk[:, 0:4], 1.0 / 256, sk, sk[:, 4:8], sk, sk[:, 8:12])
                    K.op(ACT, lambda e: e.activation(out=sg[:], in_=rg[:], func=AF.Silu), [rg], [sg])
                    for h in range(4):
                        K.op(DVE, lambda e: e.scalar_tensor_tensor(
                            out=mixr[:, 256 * h:256 * h + 256], in0=o32[:, 256 * h:256 * h + 256], scalar=sk[:, 8 + h:9 + h],
                            in1=sg[:, 256 * h:256 * h + 256], op0=ALU.mult, op1=ALU.mult), [o32, sk, sg], [mixr])
                    for c in range(8):
                        K.op(PE, lambda e: e.transpose(out=ps_t[:, 128 * c:128 * c + 128], in_=mixr[:, 128 * c:128 * c + 128],
                                                       identity=ident[:]), [mixr, ident], [ps_t], selfsync=False)
                    mt = mT[n % 2]
                    K.op(ACT, lambda e: e.activation(out=mt[:], in_=ps_t[:], func=AF.Copy), [ps_t], [mt])
                    K.dma(SP, MIXT, MIXT[:, 0:8, b * 128:(b + 1) * 128], mt, mt[:].rearrange("p (c t) -> p c t", c=8),
                          chk_dst=False)
            K.barrier()

        for si, s in enumerate(cfg.seqs):
            own = list(range(s["own0"], s["own0"] + s["nown"]))
            nb = s["nb"]
            with contextlib.ExitStack() as st:
                KT = K.sb(st, "KT", [128, 2, nb * 128], BF16, dma=True)
                V = K.sb(st, "V", [128, nb, 256], BF16, dma=True)
                npc = 4 if nb >= 8 else 1
                bounds = [nb * q // npc for q in range(npc + 1)]
                for q in range(npc):
                    a, b_ = bounds[q], bounds[q + 1]
                    K.dma(SP, KT, KT[:, :, a * 128:b_ * 128], KTs[si], KTs[si][:, :, a * 128:b_ * 128], chk_dst=(q == 0))
                    K.dma(SP, V, V[:, a:b_, :], Vs[si], Vs[si][:, a:b_, :], chk_dst=(q == 0))
                qT = [K.sb(st, f"qT{i}", [128, 8, 128], BF16, dma=True) for i in range(2)]
                late2 = None
                if si == 1:
                    stg2 = [K.sb(st, f"stgm{i}", [128, 2048], F32, dma=True) for i in range(2)]
                    wbf2 = [K.sb(st, f"wbfm{i}", [128, 2048], BF16) for i in range(2)]
                    cast_engs[0] = [0, 2]
                    late2 = run_units(late_units[KC:], stg2, wbf2, SP)
                NPS, NPT, LOOK = 3, 4, 2
                ps_s = [K.ps(st, f"pss{j}", [128, 512]) for j in range(NPS)]
                ps_ot = [K.ps(st, f"pso{j}", [128, 512]) for j in range(2)]
                ps_dn = [K.ps(st, f"psd{j}", [128, 512]) for j in range(2)]
                PTs = [K.sb(st, f"PTa{i}", [128, 512], BF16) for i in range(NPT)]
                rden = K.sb(st, "rden", [128, 512], F32)
                ma = [K.sb(st, f"ma{i}", [128, 512], BF16) for i in range(2)]
                K.dma(SP, qT[0], qT[0][:], QT, QT[own[0]])
                pairs = [(n, b, kvh, kb) for n, b in enumerate(own) for kvh in range(2) for kb in range(nb)]

                def emit_S(j):
                    n, b, kvh, kb = pairs[j]
                    if kvh == 0 and kb == 0 and n + 1 < len(own):
                        K.dma(SP, qT[(n + 1) % 2], qT[(n + 1) % 2][:], QT, QT[own[n + 1]])
                    q_ = qT[n % 2]
                    qr = q_[:, 4 * kvh:4 * kvh + 4, :].rearrange("p h t -> p (h t)")
                    nk = 16 if kb == 0 else 128
                    pss, pt = ps_s[j % NPS], PTs[j % NPT]
                    K.op(PE, lambda e: e.matmul(pss[0:nk, :], lhsT=KT[:, kvh, kb * 128:kb * 128 + nk], rhs=qr,
                                                start=True, stop=True), [KT, q_], [pss], selfsync=False)
                    K.op(ACT, lambda e: e.activation(out=pt[0:nk, :], in_=pss[0:nk, :], func=AF.Exp,
                                                     bias=negB[0:nk, 0:1]), [pss, negB], [pt])

                def emit_PV(j):
                    n, b, kvh, kb = pairs[j]
                    nk = 16 if kb == 0 else 128
                    pt = PTs[j % NPT]
                    pot, pdn = ps_ot[kvh], ps_dn[kvh]
                    K.op(PE, lambda e: e.matmul(pot[:], lhsT=V[0:nk, kb, 128 * kvh:128 * kvh + 128], rhs=pt[0:nk, :],
                                                start=(kb == 0), stop=(kb == nb - 1)), [V, pt], [pot], selfsync=False)
                    K.op(PE, lambda e: e.matmul(pdn[:], lhsT=ones[0:nk, :], rhs=pt[0:nk, :],
                                                start=(kb == 0), stop=(kb == nb - 1)), [ones, pt], [pdn], selfsync=False)
                    if kb == nb - 1:
                        K.op(DVE, lambda e: e.reciprocal(out=rden[:], in_=pdn[:]), [pdn], [rden])
                        m_ = ma[kvh]
                        K.op(DVE, lambda e: e.tensor_tensor(out=m_[:], in0=pot[:], in1=rden[:], op=ALU.mult), [pot, rden], [m_])
                        K.dma(SP, MIXT, MIXT[:, 8 + 4 * kvh:12 + 4 * kvh, b * 128:(b + 1) * 128], m_,
                              m_[:].rearrange("p (h t) -> p h t", h=4), chk_dst=False)

                for j in range(min(LOOK, len(pairs))):
                    emit_S(j)
                every = max(1, len(pairs) // (len(late_units) - KC + 1))
                for j in range(len(pairs)):
                    if j + LOOK < len(pairs):
                        emit_S(j + LOOK)
                    emit_PV(j)
                    if late2 is not None and j % every == every - 1:
                        next(late2, None)
                if late2 is not None:
                    for _ in late2:
                        pass
                    cast_engs[0] = [0, 1, 2]
            K.barrier()

        NT = NOWN // 4
        with contextlib.ExitStack() as st:
            hT = K.sb(st, "hT", [128, KC, 512], F32, dma=True)
            actT = K.sb(st, "actT", [128, KC, 512], BF16, dma=True)
            aT = K.sb(st, "aT", [128, 64, 512], BF16)
            NWB = 3
            wb = [K.sb(st, f"wb{i}", [128, 8192], BF16, dma=True) for i in range(NWB)]
            rt = K.sb(st, "rt", [128, 512], F32)
            rstd = K.sb(st, "rstd", [128, 512], F32)
            rl = [K.sb(st, f"rl{i}", [128, 512], F32) for i in range(2)]
            pm = [K.ps(st, f"pm{j}", [128, 512]) for j in range(4)]
            ps_stat = K.ps(st, "pstat", [128, 512])
            slabs = []
            for t in range(NT):
                for g in range(4):
                    slabs.append((Wouts, Wouts[g].rearrange("p k j -> p (k j)")))
                for g in range(16):
                    slabs.append((Wups, Wups[g].rearrange("p k j -> p (k j)")))
                for g in range(16):
                    slabs.append((Wdowns, Wdowns[g].rearrange("p f j -> p (f j)")))
            wi = [0]

            def issue_slab():
                i = wi[0]
                if i < len(slabs):
                    K.dma(SP, wb[i % NWB], wb[i % NWB][:], slabs[i][0], slabs[i][1])
                wi[0] += 1

            used = [0]

            def next_slab():
                i = used[0]
                used[0] += 1
                return wb[i % NWB]

            for _ in range(NWB - 1):
                issue_slab()
            cnt = 0
            for t in range(NT):
                tok = slice(t * 512, (t + 1) * 512)
                K.dma(SP, hT, hT[:].rearrange("p k (b t) -> p k b t", b=4), xown,
                      xown[4 * t:4 * t + 4].rearrange("b p k t -> p k b t"))
                K.dma(SP, actT, actT[:], MIXT, MIXT[:, :, tok])
                for g in range(4):
                    issue_slab()
                    w = next_slab()
                    wv = w[:].rearrange("p (k j) -> p k j", k=KC)
                    for dd in range(4):
                        dc = 4 * g + dd
                        p_ = pm[cnt % 4]
                        cnt += 1
                        for kc in range(KC):
                            K.op(PE, lambda e: e.matmul(p_[:], lhsT=wv[:, kc, 128 * dd:128 * dd + 128], rhs=actT[:, kc, :],
                                                        start=(kc == 0), stop=(kc == KC - 1)), [w, actT], [p_], selfsync=False)
                        K.op(DVE, lambda e: e.tensor_tensor(out=hT[:, dc, :], in0=p_[:], in1=hT[:, dc, :], op=ALU.add),
                             [p_, hT], [hT])
                K.op(ACT, lambda e: e.activation(out=actT[:], in_=hT[:], func=AF.Square), [hT], [actT])
                for kc in range(KC):
                    K.op(PE, lambda e: e.matmul(ps_stat[:], lhsT=ones[:], rhs=actT[:, kc, :], start=(kc == 0),
                                                stop=(kc == KC - 1)), [ones, actT], [ps_stat], selfsync=False)
                rstd_from(None, ps_stat, ps_stat[:], 1.0 / D, rt, rt[:], rstd, rstd[:])
                for (E, a, b_) in ((DVE, 0, 8), (POOL, 8, 16)):
                    K.op(E, lambda e: e.tensor_tensor(out=actT[:, a:b_, :], in0=hT[:, a:b_, :],
                                                      in1=rstd[:].unsqueeze(1).to_broadcast([128, b_ - a, 512]),
                                                      op=ALU.mult), [hT, rstd], [actT])
                for g in range(16):
                    issue_slab()
                    w = next_slab()
                    wv = w[:].rearrange("p (k j) -> p k j", k=KC)
                    for ff in range(4):
                        fc = 4 * g + ff
                        p_ = pm[cnt % 4]
                        r_ = rl[cnt % 2]
                        cnt += 1
                        for kc in range(KC):
                            K.op(PE, lambda e: e.matmul(p_[:], lhsT=wv[:, kc, 128 * ff:128 * ff + 128], rhs=actT[:, kc, :],
                                                        start=(kc == 0), stop=(kc == KC - 1)), [w, actT], [p_], selfsync=False)
                        K.op(ACT, lambda e: e.activation(out=r_[:], in_=p_[:], func=AF.Relu), [p_], [r_])
                        K.op(POOL if fc % 2 == 0 else DVE, lambda e: e.tensor_tensor(out=aT[:, fc, :], in0=r_[:], in1=r_[:],
                                                                                      op=ALU.mult), [r_], [aT])
                for dc in range(16):
                    issue_slab()
                    w = next_slab()
                    wv = w[:].rearrange("p (f j) -> p f j", f=64)
                    p_ = pm[cnt % 4]
                    cnt += 1
                    for fc in range(64):
                        K.op(PE, lambda e: e.matmul(p_[:], lhsT=wv[:, fc, :], rhs=aT[:, fc, :], start=(fc == 0),
                                                    stop=(fc == 63)), [w, aT], [p_], selfsync=False)
                    K.op(DVE, lambda e: e.tensor_tensor(out=hT[:, dc, :], in0=p_[:], in1=hT[:, dc, :], op=ALU.add),
                         [p_, hT], [hT])
                K.op(ACT, lambda e: e.activation(out=actT[:], in_=hT[:], func=AF.Square), [hT], [actT])
                for kc in range(KC):
                    K.op(PE, lambda e: e.matmul(ps_stat[:], lhsT=ones[:], rhs=actT[:, kc, :], start=(kc == 0),
                                                stop=(kc == KC - 1)), [ones, actT], [ps_stat], selfsync=False)
                rstd_from(None, ps_stat, ps_stat[:], 1.0 / D, rt, rt[:], rstd, rstd[:])
                gfc = C("gfc")
                for dc in range(16):
                    K.op(DVE, lambda e: e.scalar_tensor_tensor(
                        out=hT[:, dc, :], in0=hT[:, dc, :], scalar=gfc[:, dc:dc + 1], in1=rstd[:], op0=ALU.mult,
                        op1=ALU.mult), [hT, cst, rstd], [hT])
                K.dma(SP, yT, yT[:, :, tok], hT, hT[:], chk_dst=False)
        SP.waited.pop(id(yT.dsem), None)
        SP.wait((yT.dsem, yT.dval))
    K.es.close()
    return nc


N_META = 16
GRID_W = 64
ROPE_THETA = 10000.0


def _blocks_T(h):
    nb = h.shape[0] // 128
    return np.ascontiguousarray(h.reshape(nb, 128, KC, 128).transpose(0, 3, 2, 1))


def _consts(q_g, k_g, lgf, lgb, g1, g2, gf):
    c = np.zeros((128, NCST), np.float32)

    def put(name, arr):
        o, w = C_OFF[name]
        c[:, o:o + w] = arr

    idx = np.arange(128, dtype=np.float32)
    dmat = idx[None, :] - idx[:, None]
    put("ident", np.eye(128, dtype=np.float32))
    put("dp", np.maximum(dmat, 0.0))
    put("dn", np.maximum(-dmat, 0.0))
    put("cp1", np.broadcast_to(idx[None, :] + 1.0, (128, 128)))
    put("c128m", np.broadcast_to(128.0 - idx[None, :], (128, 128)))
    rfreq = (np.float32(ROPE_THETA) ** (-np.linspace(0.0, 1.0, 64, dtype=np.float32))).astype(np.float32)
    afreq = (np.float32(ROPE_THETA) ** (-np.arange(32, dtype=np.float32) / np.float32(32))).astype(np.float32)
    put("rfreq", np.broadcast_to(rfreq[None], (128, 64)))
    put("afreq", np.broadcast_to(afreq[None], (128, 32)))
    put("g1c", g1.reshape(KC, 128).T)
    put("g2c", g2.reshape(KC, 128).T)
    put("gfc", gf.reshape(KC, 128).T)
    put("qg", np.broadcast_to(q_g[None], (128, 128)))
    put("kg", np.broadcast_to(k_g[None], (128, 128)))
    put("lgf", np.broadcast_to(lgf[None], (128, 4)))
    put("lgb", np.broadcast_to(lgb[None], (128, 4)))
    put("colc", idx[:, None])
    put("col127", 127.0 - idx[:, None])
    return c


def _seq_meta(nb_real):
    L = (nb_real + 1) * 128
    pos = np.zeros(L, np.float32)
    row = np.zeros(L, np.float32)
    col = np.zeros(L, np.float32)
    valid = np.zeros(L, np.float32)
    pos[0:16] = 112 + np.arange(16)
    valid[0:16] = 1
    j = np.arange(nb_real * 128)
    pos[128:] = 128 + j
    row[128:] = j // GRID_W
    col[128:] = j % GRID_W
    valid[128:] = 1
    return pos, row, col, valid


def prepare(cfg, x_prompt, x_sample, meta_tokens, ln1_g, w_in, q_norm_g, k_norm_g, ret_log_decay_fwd,
            ret_log_decay_bwd, w_out, ln2_g, w_up, w_down, final_norm_g):
    f32 = np.float32
    cst = _consts(np.asarray(q_norm_g[0], f32), np.asarray(k_norm_g[0], f32), np.asarray(ret_log_decay_fwd[0], f32),
                  np.asarray(ret_log_decay_bwd[0], f32), np.asarray(ln1_g[0], f32), np.asarray(ln2_g[0], f32),
                  np.asarray(final_norm_g, f32))
    mblk = np.zeros((128, D), f32)
    mblk[0:16] = meta_tokens
    mblkT = _blocks_T(mblk)
    xs_T = _blocks_T(np.asarray(x_sample[0], f32))
    xp_T = [_blocks_T(np.asarray(x_prompt[b], f32)) for b in range(x_prompt.shape[0])]
    shared = dict(cst=cst, w_in=np.ascontiguousarray(w_in[0], f32), w_out=np.ascontiguousarray(w_out[0], f32),
                  w_up=np.ascontiguousarray(w_up[0], f32), w_down=np.ascontiguousarray(w_down[0], f32))
    in_maps = []
    for c in range(8):
        pb, ph = c // 2, c % 2
        xb1 = np.concatenate([mblkT, xp_T[pb], mblkT, xs_T], axis=0)
        own_p = list(range(ph * cfg.OWN_P, (ph + 1) * cfg.OWN_P))
        own_s = list(range(c * cfg.OWN_S, (c + 1) * cfg.OWN_S))
        xown = np.concatenate([xp_T[pb][own_p], xs_T[own_s]], axis=0)
        metas, metao = [], []
        for (nbr, ownl) in ((cfg.NBP, own_p), (cfg.NBS, own_s)):
            pos, row, col, valid = _seq_meta(nbr)
            start = 128.0 + ownl[0] * 128
            end = 128.0 + (ownl[-1] + 1) * 128
            mf = ((pos < start) & (valid > 0)).astype(f32)
            mb = ((pos >= end) & (valid > 0)).astype(f32)
            df = np.where(mf > 0, start - 1.0 - pos, 0.0).astype(f32)
            db = np.where(mb > 0, pos - end, 0.0).astype(f32)
            m = np.stack([pos, row, col, df, mf, db, mb], axis=-1).reshape(nbr + 1, 128, 7)
            metas.append(m)
            osl = [1 + o for o in ownl]
            metao.append(m[osl][:, :, 0:3])
        meta1 = np.ascontiguousarray(np.concatenate(metas, axis=0).transpose(1, 0, 2))
        metao = np.ascontiguousarray(np.concatenate(metao, axis=0).transpose(1, 0, 2))
        d = dict(xb1=np.ascontiguousarray(xb1), xown=np.ascontiguousarray(xown), meta1=meta1, metao=metao)
        d.update(shared)
        in_maps.append(d)
    return in_maps


def assemble(cfg, results, nbatch):
    yp = np.zeros((nbatch, cfg.NBP * 128, D), np.float32)
    ys = np.zeros((1, cfg.NBS * 128, D), np.float32)
    for c in range(8):
        y = results[c]["yT"]
        y = y.transpose(2, 1, 0).reshape(cfg.NOWN * 128, D)
        pb, ph = c // 2, c % 2
        np_ = cfg.OWN_P * 128
        yp[pb, ph * np_:(ph + 1) * np_] = y[0:np_]
        ns_ = cfg.OWN_S * 128
        ys[0, c * ns_:(c + 1) * ns_] = y[np_:np_ + ns_]
    return yp, ys


_NC_CACHE = {}


def kernel(x_prompt, x_sample, meta_tokens, ln1_g, w_in, q_norm_g, k_norm_g, ret_log_decay_fwd,
           ret_log_decay_bwd, w_out, ln2_g, w_up, w_down, final_norm_g):
    cfg = Cfg(16, 128)
    args = [np.asarray(a) for a in (x_prompt, x_sample, meta_tokens, ln1_g, w_in, q_norm_g, k_norm_g,
                                    ret_log_decay_fwd, ret_log_decay_bwd, w_out, ln2_g, w_up, w_down, final_norm_g)]
    in_maps = prepare(cfg, *args)
    nc = build(cfg)
    res = run_bass_kernel_spmd(nc, in_maps, core_ids=list(range(8)))
    yp, ys = assemble(cfg, res.results, args[0].shape[0])
    return (yp, ys)
```

```python
import contextlib
import math
import numpy as np
import ml_dtypes
import concourse.bass as bass
import concourse.mybir as mybir
from concourse.bass_utils import run_bass_kernel_spmd

F32 = mybir.dt.float32
BF16 = mybir.dt.bfloat16
I32 = mybir.dt.int32
AF = mybir.ActivationFunctionType
ALU = mybir.AluOpType
AX = mybir.AxisListType

D = 2048
KC = 16
DFF = 8192
EPS = 1e-6
TWO_PI = 2.0 * math.pi


class Buf:
    def __init__(self, t, name):
        self.t = t
        self.name = name
        self.w = None
        self.r = {}
        self.dsem = None
        self.dval = 0

    def __getitem__(self, key):
        return self.t[key]


class Eng:
    def __init__(self, name, eng, sem):
        self.name = name
        self.e = eng
        self.sem = sem
        self.count = 0
        self.waited = {}

    def wait(self, dep):
        sem, val = dep
        if self.waited.get(id(sem), 0) >= val:
            return
        self.waited[id(sem)] = val
        self.e.wait_ge(sem, val)


class Kern:
    def __init__(self, nc, n_dma_sems=96):
        self.nc = nc
        self.es = contextlib.ExitStack()
        self.uid = 0
        self.PE = self._mk("pe", nc.tensor)
        self.DVE = self._mk("dve", nc.vector)
        self.ACT = self._mk("act", nc.scalar)
        self.POOL = self._mk("pool", nc.gpsimd)
        self.SP = self._mk("sp", nc.sync)
        self.engs = [self.PE, self.DVE, self.ACT, self.POOL, self.SP]
        self.pool = [[self.es.enter_context(nc.semaphore(f"dq{i}")), 0] for i in range(n_dma_sems)]
        self.live = []

    def _mk(self, name, eng):
        return Eng(name, eng, self.es.enter_context(self.nc.semaphore("s_" + name)))

    def _give_sem(self, b):
        ent = self.pool.pop()
        b.dsem, b.dval = ent[0], ent[1]
        b._ent = ent
        self.live.append(b)

    def release(self, bufs):
        for b in bufs:
            if b.dsem is not None and b in self.live:
                self.live.remove(b)
                b._ent[1] = b.dval
                self.pool.append(b._ent)

    def sb(self, stack, name, shape, dt, dma=False):
        self.uid += 1
        t = stack.enter_context(self.nc.sbuf_tensor(f"{name}_{self.uid}", list(shape), dt))
        b = Buf(t, name)
        if dma:
            self._give_sem(b)
        stack.callback(self.release, [b])
        return b

    def ps(self, stack, name, shape, dt=F32):
        self.uid += 1
        t = stack.enter_context(self.nc.psum_tensor(f"{name}_{self.uid}", list(shape), dt))
        return Buf(t, name)

    def dram(self, name, shape, dt, kind="Internal"):
        t = self.nc.dram_tensor(name, list(shape), dt, kind=kind)
        b = Buf(t.ap(), name)
        self._give_sem(b)
        return b

    def op(self, E, fn, reads=(), writes=(), selfsync=True):
        deps = []
        for b in reads:
            if b.w is not None:
                deps.append(b.w)
        for b in writes:
            if b.w is not None:
                deps.append(b.w)
            deps.extend(b.r.values())
        for d in deps:
            if d[0] is E.sem and not selfsync:
                continue
            E.wait(d)
        ins = fn(E.e)
        E.count += 1
        ins.then_inc(E.sem, 1)
        me = (E.sem, E.count)
        for b in reads:
            b.r[id(E.sem)] = me
        for b in writes:
            b.w = me
            b.r = {}
        return ins

    def dma(self, Q, dst, dst_ap, src, src_ap, chk_dst=True, **kw):
        deps = []
        if src.w is not None:
            deps.append(src.w)
        if chk_dst:
            if dst.w is not None:
                deps.append(dst.w)
            deps.extend(dst.r.values())
        for d in deps:
            Q.wait(d)
        ins = Q.e.dma_start(out=dst_ap, in_=src_ap, **kw)
        dst.dval += 16
        ins.then_inc(dst.dsem, 16)
        me = (dst.dsem, dst.dval)
        src.r[id(dst.dsem)] = me
        dst.w = me
        if chk_dst:
            dst.r = {}
        return ins

    def barrier(self):
        marks = [(E.sem, E.count) for E in self.engs if E.count > 0]
        marks += [(b.dsem, b.dval) for b in self.live if b.dval > 0]
        marks += [(ent[0], ent[1]) for ent in self.pool if ent[1] > 0]
        for E in self.engs:
            for m in marks:
                if m[0] is E.sem:
                    continue
                E.wait(m)


C_OFF = {}
_o = 0
for _n, _w in [("ident", 128), ("dp", 128), ("dn", 128), ("cp1", 128), ("c128m", 128), ("rfreq", 64),
               ("afreq", 32), ("g1c", 16), ("g2c", 16), ("gfc", 16), ("qg", 128), ("kg", 128), ("lgf", 4),
               ("lgb", 4), ("colc", 1), ("col127", 1)]:
    C_OFF[_n] = (_o, _w)
    _o += _w
NCST = _o


class Cfg:
    def __init__(self, nbp=16, nbs=128):
        self.NBP = nbp
        self.NBS = nbs
        self.OWN_P = nbp // 2
        self.OWN_S = nbs // 8
        self.NOWN = self.OWN_P + self.OWN_S
        self.NB1 = nbp + 1 + nbs + 1
        assert self.NOWN % 4 == 0
        self.seqs = [dict(name="p", nb=nbp + 1, b0=0, own0=0, nown=self.OWN_P),
                     dict(name="s", nb=nbs + 1, b0=nbp + 1, own0=self.OWN_P, nown=self.OWN_S)]


def build(cfg, debug=False):
    nc = bass.Bass("TRN2", target_bir_lowering=False)
    K = Kern(nc)
    NB1, NOWN = cfg.NB1, cfg.NOWN
    PE, DVE, ACT, POOL, SP = K.PE, K.DVE, K.ACT, K.POOL, K.SP
    skind = "ExternalOutput" if debug else "Internal"

    xb1 = K.dram("xb1", [NB1, 128, KC, 128], F32, "ExternalInput")
    xown = K.dram("xown", [NOWN, 128, KC, 128], F32, "ExternalInput")
    meta1 = K.dram("meta1", [128, NB1, 7], F32, "ExternalInput")
    metao = K.dram("metao", [128, NOWN, 3], F32, "ExternalInput")
    cst_d = K.dram("cst", [128, NCST], F32, "ExternalInput")
    w_in = K.dram("w_in", [D, 4608], F32, "ExternalInput")
    w_out = K.dram("w_out", [D, D], F32, "ExternalInput")
    w_up = K.dram("w_up", [D, DFF], F32, "ExternalInput")
    w_down = K.dram("w_down", [DFF, D], F32, "ExternalInput")
    yT = K.dram("yT", [128, KC, NOWN * 128], F32, "ExternalOutput")

    W1s = K.dram("W1s", [128, KC, 2048], BF16)
    Wown = K.dram("Wown", [8, 128, KC, 512], BF16)
    Wouts = K.dram("Wouts", [4, 128, KC, 512], BF16)
    Wups = K.dram("Wups", [16, 128, KC, 512], BF16)
    Wdowns = K.dram("Wdowns", [16, 128, 64, 128], BF16)
    KTs = [K.dram(f"KT_{s['name']}", [128, 2, s["nb"] * 128], BF16, skind) for s in cfg.seqs]
    Vs = [K.dram(f"V_{s['name']}", [128, s["nb"], 256], BF16, skind) for s in cfg.seqs]
    PROJ = K.dram("PROJ", [NOWN, 128, 4096], F32, skind)
    QT = K.dram("QT", [NOWN, 128, 8, 128], BF16, skind)
    RO = K.dram("RO", [NOWN, 128, 1024], F32, skind)
    RQB = K.dram("RQB", [NOWN, 128, 512], BF16, skind)
    KVB = K.dram("KVB", [NOWN, 128, 1024], F32, skind)
    MIXT = K.dram("MIXT", [128, KC, NOWN * 128], BF16, skind)
    SDBG = K.dram("SDBG", [4, 128, 1024], F32, skind) if debug else None

    with contextlib.ExitStack() as G:
        cst = K.sb(G, "cst", [128, NCST], F32, dma=True)
        K.dma(SP, cst, cst[:], cst_d, cst_d[:])

        def C(name, rows=128):
            o, w = C_OFF[name]
            return cst[0:rows, o:o + w]

        ident = K.sb(G, "ident", [128, 128], BF16)
        ones = K.sb(G, "ones", [128, 128], BF16)
        epsb = K.sb(G, "epsb", [128, 1], F32)
        negB = K.sb(G, "negB", [128, 1], F32)
        kdf = K.sb(G, "kdf", [128, 4], F32)
        kdb = K.sb(G, "kdb", [128, 4], F32)
        cdf = K.sb(G, "cdf", [128, 4], F32)
        cdb = K.sb(G, "cdb", [128, 4], F32)
        wfb = K.sb(G, "wfb", [128, NB1, 8], F32)
        m1 = K.sb(G, "m1", [128, NB1, 7], F32, dma=True)
        mo = K.sb(G, "mo", [128, NOWN, 3], F32, dma=True)
        Sst = [[K.sb(G, f"S{d}{s['name']}", [128, 4, 256], F32) for d in "fb"] for s in cfg.seqs]
        tmpc = K.sb(G, "tmpc", [128, max(NB1, 128)], F32)
        tmpd = K.sb(G, "tmpd", [128, max(NB1, 128)], F32)

        K.dma(SP, m1, m1[:], meta1, meta1[:])
        K.dma(SP, mo, mo[:], metao, metao[:])

        K.op(DVE, lambda e: e.tensor_copy(out=ident[:], in_=C("ident")), [cst], [ident])
        K.op(DVE, lambda e: e.memset(ones[:], 1.0), [], [ones])
        K.op(DVE, lambda e: e.memset(epsb[:], EPS), [], [epsb])
        for s in range(2):
            for d in range(2):
                K.op(POOL, lambda e: e.memset(Sst[s][d][:], 0.0), [], [Sst[s][d]])
        K.op(DVE, lambda e: e.tensor_reduce(out=tmpc[:, 0:1], in_=C("qg"), axis=AX.X, op=ALU.max,
                                            apply_absolute_value=True), [cst], [tmpc])
        K.op(DVE, lambda e: e.tensor_reduce(out=tmpc[:, 1:2], in_=C("kg"), axis=AX.X, op=ALU.max,
                                            apply_absolute_value=True), [cst], [tmpc])
        K.op(DVE, lambda e: e.scalar_tensor_tensor(out=negB[:], in0=tmpc[:, 0:1], scalar=-math.sqrt(128.0),
                                                   in1=tmpc[:, 1:2], op0=ALU.mult, op1=ALU.mult), [tmpc], [negB])
        lgf, lgb = C("lgf"), C("lgb")
        def make_decay_tables(st):
            MT = K.sb(st, "MT", [128, 4, 128], F32)
            qdf = K.sb(st, "qdf", [128, 4, 128], F32)
            qdb = K.sb(st, "qdb", [128, 4, 128], F32)
            for h in range(4):
                K.op(DVE, lambda e: e.tensor_scalar(out=tmpc[:, 0:128], in0=C("dp"), scalar1=lgf[:, h:h + 1],
                                                    scalar2=None, op0=ALU.mult), [cst], [tmpc])
                K.op(DVE, lambda e: e.scalar_tensor_tensor(out=tmpd[:, 0:128], in0=C("dn"), scalar=lgb[:, h:h + 1],
                                                           in1=tmpc[:, 0:128], op0=ALU.mult, op1=ALU.add),
                     [cst, tmpc], [tmpd])
                K.op(ACT, lambda e: e.activation(out=MT[:, h, :], in_=tmpd[:, 0:128], func=AF.Exp), [tmpd], [MT])
                K.op(ACT, lambda e: e.activation(out=qdf[:, h, :], in_=C("cp1"), func=AF.Exp, scale=lgf[:, h:h + 1]),
                     [cst], [qdf])
                K.op(ACT, lambda e: e.activation(out=qdb[:, h, :], in_=C("c128m"), func=AF.Exp, scale=lgb[:, h:h + 1]),
                     [cst], [qdb])
            return MT, qdf, qdb

        for h in range(4):
            K.op(ACT, lambda e: e.activation(out=kdf[:, h:h + 1], in_=C("col127"), func=AF.Exp,
                                             scale=lgf[:, h:h + 1]), [cst], [kdf])
            K.op(ACT, lambda e: e.activation(out=kdb[:, h:h + 1], in_=C("colc"), func=AF.Exp,
                                             scale=lgb[:, h:h + 1]), [cst], [kdb])
            for d, (lg, dcol, mcol) in enumerate([(lgf, 3, 4), (lgb, 5, 6)]):
                K.op(ACT, lambda e: e.activation(out=tmpc[:, 0:NB1], in_=m1[:, :, dcol], func=AF.Exp,
                                                 scale=lg[:, h:h + 1]), [m1, cst], [tmpc])
                K.op(DVE, lambda e: e.tensor_tensor(out=wfb[:, :, 4 * d + h], in0=tmpc[:, 0:NB1],
                                                    in1=m1[:, :, mcol], op=ALU.mult), [tmpc, m1], [wfb])
        K.op(ACT, lambda e: e.activation(out=cdf[:], in_=lgf, func=AF.Exp, scale=128.0), [cst], [cdf])
        K.op(ACT, lambda e: e.activation(out=cdb[:], in_=lgb, func=AF.Exp, scale=128.0), [cst], [cdb])

        def cs_tmps(st, Gn):
            return (K.sb(st, "cs_a", [128, Gn, 64], F32), K.sb(st, "cs_k", [128, Gn, 64], I32),
                    K.sb(st, "cs_f", [128, Gn, 64], F32))

        def build_cs(st, name, tmps, Gn, scale):
            ang, ki, kf = tmps
            cs = K.sb(st, name + "_cs", [128, Gn, 128], F32)

            def fill(specs_now, Gc):
                for (fr, pos, off) in specs_now:
                    nf = fr.shape[-1]
                    K.op(DVE, lambda e: e.tensor_tensor(
                        out=ang[:, 0:Gc, off:off + nf], in0=fr.unsqueeze(1).to_broadcast([128, Gc, nf]),
                        in1=pos.unsqueeze(2).to_broadcast([128, Gc, nf]), op=ALU.mult), [cst, m1, mo], [ang])
                K.op(DVE, lambda e: e.tensor_scalar(out=ki[:, 0:Gc, :], in0=ang[:, 0:Gc, :], scalar1=1.0 / TWO_PI,
                                                    scalar2=None, op0=ALU.mult), [ang], [ki])
                K.op(DVE, lambda e: e.tensor_copy(out=kf[:, 0:Gc, :], in_=ki[:, 0:Gc, :]), [ki], [kf])
                K.op(DVE, lambda e: e.scalar_tensor_tensor(out=ang[:, 0:Gc, :], in0=kf[:, 0:Gc, :], scalar=-TWO_PI,
                                                           in1=ang[:, 0:Gc, :], op0=ALU.mult, op1=ALU.add),
                     [kf, ang], [ang])
                K.op(ACT, lambda e: e.activation(out=cs[:, 0:Gc, 0:64], in_=ang[:, 0:Gc, :], func=AF.Sin,
                                                 scale=1.0 - 1e-6), [ang], [cs])
                K.op(ACT, lambda e: e.activation(out=kf[:, 0:Gc, :], in_=ang[:, 0:Gc, :], func=AF.Sin,
                                                 scale=0.5), [ang], [kf])
                K.op(DVE, lambda e: e.scalar_tensor_tensor(out=kf[:, 0:Gc, :], in0=kf[:, 0:Gc, :], scalar=-2.0,
                                                           in1=kf[:, 0:Gc, :], op0=ALU.mult, op1=ALU.mult),
                     [kf], [kf])
                if scale == 1.0:
                    K.op(DVE, lambda e: e.tensor_scalar(out=cs[:, 0:Gc, 64:128], in0=kf[:, 0:Gc, :], scalar1=1.0,
                                                        scalar2=None, op0=ALU.add), [kf], [cs])
                else:
                    K.op(DVE, lambda e: e.tensor_scalar(out=cs[:, 0:Gc, 64:128], in0=kf[:, 0:Gc, :], scalar1=1.0,
                                                        scalar2=scale, op0=ALU.add, op1=ALU.mult), [kf], [cs])
                    K.op(DVE, lambda e: e.tensor_scalar(out=cs[:, 0:Gc, 0:64], in0=cs[:, 0:Gc, 0:64],
                                                        scalar1=scale, scalar2=None, op0=ALU.mult), [cs], [cs])
            return cs, fill

        def rope(E1, E2, xb, x_ap, nh, cs, g, ob, o_ap, tm):
            xv = x_ap.rearrange("p (h i t) -> p h i t", h=nh, t=2)
            ov = o_ap.rearrange("p (h i t) -> p h i t", h=nh, t=2)
            x1, x2 = xv[:, :, :, 0], xv[:, :, :, 1]
            sb_ = cs[:, g, 0:64].unsqueeze(1).to_broadcast([128, nh, 64])
            cb_ = cs[:, g, 64:128].unsqueeze(1).to_broadcast([128, nh, 64])
            t = [tm[i][:, 0:nh * 64].rearrange("p (h i) -> p h i", h=nh) for i in range(4)]
            K.op(E1, lambda e: e.tensor_tensor(out=t[0], in0=x1, in1=cb_, op=ALU.mult), [xb, cs], [tm[0]])
            K.op(E1, lambda e: e.tensor_tensor(out=t[1], in0=x2, in1=sb_, op=ALU.mult), [xb, cs], [tm[1]])
            K.op(E1, lambda e: e.tensor_tensor(out=ov[:, :, :, 0], in0=t[0], in1=t[1], op=ALU.subtract),
                 [tm[0], tm[1]], [ob])
            K.op(E2, lambda e: e.tensor_tensor(out=t[2], in0=x1, in1=sb_, op=ALU.mult), [xb, cs], [tm[2]])
            K.op(E2, lambda e: e.tensor_tensor(out=t[3], in0=x2, in1=cb_, op=ALU.mult), [xb, cs], [tm[3]])
            K.op(E2, lambda e: e.tensor_tensor(out=ov[:, :, :, 1], in0=t[2], in1=t[3], op=ALU.add),
                 [tm[2], tm[3]], [ob])

        def rstd_from(E_sq, ssq_b, ssq_ap, inv_n, tmp_b, tmp_ap, out_b, out_ap):
            rows = ssq_ap.shape[0]
            K.op(ACT, lambda e: e.activation(out=tmp_ap, in_=ssq_ap, func=AF.Sqrt, scale=inv_n,
                                             bias=epsb[0:rows, 0:1]), [ssq_b, epsb], [tmp_b])
            K.op(DVE, lambda e: e.reciprocal(out=out_ap, in_=tmp_ap), [tmp_b], [out_b])

        def norm_block(xTb, sqb, uTb, ps_stat, rt, rstd, ntok):
            K.op(ACT, lambda e: e.activation(out=sqb[:, :, 0:ntok], in_=xTb[:, :, 0:ntok], func=AF.Square),
                 [xTb], [sqb])
            for kc in range(KC):
                K.op(PE, lambda e: e.matmul(ps_stat[:, 0:ntok], lhsT=ones[:], rhs=sqb[:, kc, 0:ntok],
                                            start=(kc == 0), stop=(kc == KC - 1)), [ones, sqb], [ps_stat],
                     selfsync=False)
            rstd_from(None, ps_stat, ps_stat[:, 0:ntok], 1.0 / D, rt, rt[:, 0:ntok], rstd, rstd[:, 0:ntok])
            h = KC // 2
            for (E, a, b) in ((DVE, 0, h), (POOL, h, KC)):
                K.op(E, lambda e: e.tensor_tensor(
                    out=uTb[:, a:b, 0:ntok], in0=xTb[:, a:b, 0:ntok],
                    in1=rstd[:, 0:ntok].unsqueeze(1).to_broadcast([128, b - a, ntok]), op=ALU.mult),
                    [xTb, rstd], [uTb])

        cast_rr = [0]

        cast_engs = [[0, 1, 2]]

        def cast(srcb, src_ap, dstb, dst_ap, gcol=None):
            i = cast_engs[0][cast_rr[0] % len(cast_engs[0])]
            cast_rr[0] += 1
            if i == 0:
                if gcol is None:
                    K.op(DVE, lambda e: e.tensor_copy(out=dst_ap, in_=src_ap), [srcb], [dstb])
                else:
                    K.op(DVE, lambda e: e.tensor_scalar(out=dst_ap, in0=src_ap, scalar1=gcol, scalar2=None,
                                                        op0=ALU.mult), [srcb, cst], [dstb])
            elif i == 1:
                K.op(ACT, lambda e: e.activation(out=dst_ap, in_=src_ap, func=AF.Copy,
                                                 scale=(1.0 if gcol is None else gcol)), [srcb, cst], [dstb])
            else:
                if gcol is None:
                    K.op(POOL, lambda e: e.tensor_copy(out=dst_ap, in_=src_ap), [srcb], [dstb])
                else:
                    K.op(POOL, lambda e: e.tensor_scalar(out=dst_ap, in0=src_ap, scalar1=gcol, scalar2=None,
                                                         op0=ALU.mult), [srcb, cst], [dstb])

        with contextlib.ExitStack() as st:
            NBUF = 3
            stg = [K.sb(st, f"stg{i}", [128, 2048], F32, dma=True) for i in range(NBUF)]
            wbf = [K.sb(st, f"wbf{i}", [128, 2048], BF16) for i in range(NBUF)]
            g1c, g2c = C("g1c"), C("g2c")
            units = []
            for kc in range(KC):
                rows = slice(kc * 128, (kc + 1) * 128)
                units.append((w_in, w_in[rows, 0:2048], 2048, g1c[:, kc:kc + 1],
                              [(Wown, Wown[0:4, :, kc, :].rearrange("g p j -> p g j"), 0, 2048, 4),
                               (W1s, W1s[:, kc, 512:2048], 512, 1536, 0)]))
                units.append((w_in, w_in[rows, 2048:4096], 2048, g1c[:, kc:kc + 1],
                              [(Wown, Wown[4:8, :, kc, :].rearrange("g p j -> p g j"), 0, 2048, 4)]))
                units.append((w_in, w_in[rows, 4096:4608], 512, g1c[:, kc:kc + 1],
                              [(W1s, W1s[:, kc, 0:512], 0, 512, 0)]))
            for kc in range(KC):
                rows = slice(kc * 128, (kc + 1) * 128)
                units.append((w_out, w_out[rows, :], 2048, None,
                              [(Wouts, Wouts[:, :, kc, :].rearrange("g p j -> p g j"), 0, 2048, 4)]))
            for kc in range(KC):
                rows = slice(kc * 128, (kc + 1) * 128)
                for q in range(4):
                    units.append((w_up, w_up[rows, q * 2048:(q + 1) * 2048], 2048, g2c[:, kc:kc + 1],
                                  [(Wups, Wups[4 * q:4 * q + 4, :, kc, :].rearrange("g p j -> p g j"), 0, 2048, 4)]))
            for fc in range(64):
                rows = slice(fc * 128, (fc + 1) * 128)
                units.append((w_down, w_down[rows, :], 2048, None,
                              [(Wdowns, Wdowns[:, :, fc, :].rearrange("g p j -> p g j"), 0, 2048, 16)]))

            def run_units(ulist, stg, wbf, Q):
                nb_ = len(stg)

                def ld(i):
                    src, sap, n, _, _ = ulist[i]
                    K.dma(SP, stg[i % nb_], stg[i % nb_][:, 0:n], src, sap)

                for i in range(min(nb_ - 1, len(ulist))):
                    ld(i)
                for i, (src, sap, n, gcol, stores) in enumerate(ulist):
                    if i + nb_ - 1 < len(ulist):
                        ld(i + nb_ - 1)
                    s_, w_ = stg[i % nb_], wbf[i % nb_]
                    cast(s_, s_[:, 0:n], w_, w_[:, 0:n], gcol)
                    for (dstb, dap, off, nn, ng) in stores:
                        sap2 = w_[:, off:off + nn]
                        if ng:
                            sap2 = sap2.rearrange("p (g j) -> p g j", g=ng)
                        K.dma(Q, dstb, dap, w_, sap2, chk_dst=False)
                    yield i

            n_in = 3 * KC
            for _ in run_units(units[:n_in], stg, wbf, ACT):
                pass
            late_units = units[n_in:]
        K.barrier()

        GT = 4
        with contextlib.ExitStack() as st:
            W1 = K.sb(st, "W1", [128, KC, 2048], BF16, dma=True)
            stg1 = [K.sb(st, f"stgl{i}", [128, 2048], F32, dma=True) for i in range(2)]
            wbf1 = [K.sb(st, f"wbfl{i}", [128, 2048], BF16) for i in range(2)]
            late = run_units(late_units[:KC], stg1, wbf1, SP)
            for q in range(4):
                K.dma(SP, W1, W1[:, 4 * q:4 * q + 4, :], W1s, W1s[:, 4 * q:4 * q + 4, :], chk_dst=(q == 0))
            NX = 3
            xTs = [K.sb(st, f"xT{i}", [128, KC, 128], F32, dma=True) for i in range(NX)]
            sqbs = [K.sb(st, f"sq{i}", [128, KC, 128], BF16) for i in range(2)]
            uTs = [K.sb(st, f"uT{i}", [128, KC, 128], BF16) for i in range(2)]
            rt = K.sb(st, "rt", [128, 128], F32)
            rstd = K.sb(st, "rstd", [128, 128], F32)
            pp = [K.ps(st, f"pp{j}", [128, 512]) for j in range(4)]
            ps_stat = K.ps(st, "pstat", [128, 512])
            ps_tr = K.ps(st, "ptr", [128, 1024], BF16)
            ps_kv = [K.ps(st, f"pkv{j}", [128, 512]) for j in range(2)]
            cst_ = cs_tmps(st, GT)
            cs_r = [build_cs(st, f"csr{i}", cst_, GT, 128.0 ** -0.5) for i in range(2)]
            cs_a = [build_cs(st, f"csa{i}", cst_, GT, 1.0) for i in range(2)]
            rk32 = K.sb(st, "rk32", [128, 512], F32)
            ak32 = K.sb(st, "ak32", [128, 256], F32)
            akn = K.sb(st, "akn", [128, 256], F32)
            tm = [K.sb(st, f"tm{i}", [128, 256], F32) for i in range(4)]
            tm2 = [K.sb(st, f"tn{i}", [128, 256], F32) for i in range(4)]
            rk_r = K.sb(st, "rk_r", [128, 512], BF16)
            ND = 3
            kws = [[K.sb(st, f"kw{j}{i}", [128, 512], BF16) for i in range(2)] for j in range(ND)]
            rv_bfs = [K.sb(st, f"rv_bf{j}", [128, 1024], BF16) for j in range(ND)]
            ak_rs = [K.sb(st, f"ak_r{j}", [128, 256], BF16) for j in range(ND)]
            av_bf = [K.sb(st, f"av_bf{i}", [128, 256], BF16) for i in range(2)]
            ktb = [K.sb(st, f"ktb{i}", [128, 2, 128], BF16) for i in range(2)]
            sk = K.sb(st, "sk", [128, 8], F32)
            junk = K.sb(st, "junk", [128, 128], F32)

            def load_x(i):
                K.dma(SP, xTs[i % NX], xTs[i % NX][:], xb1, xb1[i])

            blocks = []
            for si, s in enumerate(cfg.seqs):
                for bl in range(s["nb"]):
                    blocks.append((s["b0"] + bl, si, bl))

            def stage_sq(i):
                xTb, sqb = xTs[i % NX], sqbs[i % 2]
                K.op(ACT, lambda e: e.activation(out=sqb[:], in_=xTb[:], func=AF.Square), [xTb], [sqb])

            def stage_stat(i):
                sqb = sqbs[i % 2]
                for kc in range(KC):
                    K.op(PE, lambda e: e.matmul(ps_stat[:, 0:128], lhsT=ones[:], rhs=sqb[:, kc, :],
                                                start=(kc == 0), stop=(kc == KC - 1)), [ones, sqb], [ps_stat],
                         selfsync=False)
                K.op(ACT, lambda e: e.activation(out=rt[:], in_=ps_stat[:, 0:128], func=AF.Sqrt, scale=1.0 / D,
                                                 bias=epsb[:, 0:1]), [ps_stat, epsb], [rt])

            def stage_uT(i):
                xTb, uTb = xTs[i % NX], uTs[i % 2]
                K.op(DVE, lambda e: e.reciprocal(out=rstd[:], in_=rt[:]), [rt], [rstd])
                h = KC // 2
                for (E, a_, b_) in ((DVE, 0, h), (POOL, h, KC)):
                    K.op(E, lambda e: e.tensor_tensor(
                        out=uTb[:, a_:b_, :], in0=xTb[:, a_:b_, :],
                        in1=rstd[:].unsqueeze(1).to_broadcast([128, b_ - a_, 128]), op=ALU.mult),
                        [xTb, rstd], [uTb])

            def stage_tables(i):
                g, gi = i % GT, (i // GT) % 2
                if g == 0:
                    Gc = min(GT, NB1 - i)
                    cs_r[gi][1]([(C("rfreq"), m1[:, i:i + Gc, 0], 0)], Gc)
                    cs_a[gi][1]([(C("afreq"), m1[:, i:i + Gc, 1], 0), (C("afreq"), m1[:, i:i + Gc, 2], 32)], Gc)

            def stage_proj(i, banks):
                uTb = uTs[i % 2]
                for j in banks:
                    for kc in range(KC):
                        K.op(PE, lambda e: e.matmul(pp[j][:], lhsT=uTb[:, kc, :], rhs=W1[:, kc, 512 * j:512 * j + 512],
                                                    start=(kc == 0), stop=(kc == KC - 1)), [uTb, W1], [pp[j]],
                             selfsync=False)

            def stage_post(i, si, bl):
                g, gi = i % GT, (i // GT) % 2
                csr, csa = cs_r[gi][0], cs_a[gi][0]
                kw, rv_bf, ak_r = kws[i % ND], rv_bfs[i % ND], ak_rs[i % ND]
                K.op(ACT, lambda e: e.activation(out=ak32[:], in_=pp[0][:, 0:256], func=AF.Copy), [pp[0]], [ak32])
                avb = av_bf[i % 2]
                K.op(ACT, lambda e: e.activation(out=avb[:], in_=pp[0][:, 256:512], func=AF.Copy), [pp[0]], [avb])
                K.dma(SP, Vs[si], Vs[si][:, bl, :], avb, avb[:], chk_dst=False)
                for h in range(2):
                    K.op(ACT, lambda e: e.activation(out=junk[:], in_=ak32[:, 128 * h:128 * h + 128], func=AF.Square,
                                                     accum_out=sk[:, h:h + 1]), [ak32], [junk, sk])
                rstd_from(None, sk, sk[:, 0:2], 1.0 / 128, sk, sk[:, 2:4], sk, sk[:, 4:6])
                for h in range(2):
                    K.op(DVE, lambda e: e.scalar_tensor_tensor(out=akn[:, 128 * h:128 * h + 128],
                                                               in0=ak32[:, 128 * h:128 * h + 128],
                                                               scalar=sk[:, 4 + h:5 + h], in1=C("kg"),
                                                               op0=ALU.mult, op1=ALU.mult), [ak32, sk, cst], [akn])
                rope(POOL, DVE, akn, akn[:], 2, csa, g, ak_r, ak_r[:], tm2)
                K.op(ACT, lambda e: e.activation(out=rk32[:], in_=pp[1][:], func=AF.Copy), [pp[1]], [rk32])
                K.op(ACT, lambda e: e.activation(out=rv_bf[:, 0:512], in_=pp[2][:], func=AF.Copy), [pp[2]], [rv_bf])
                K.op(ACT, lambda e: e.activation(out=rv_bf[:, 512:1024], in_=pp[3][:], func=AF.Copy), [pp[3]], [rv_bf])
                rope(DVE, POOL, rk32, rk32[:], 4, csr, g, rk_r, rk_r[:], tm)
                for d in range(2):
                    K.op(DVE if d == 0 else POOL, lambda e: e.tensor_tensor(
                        out=kw[d][:].rearrange("p (h x) -> p h x", h=4),
                        in0=rk_r[:].rearrange("p (h x) -> p h x", h=4),
                        in1=wfb[:, i, 4 * d:4 * d + 4].unsqueeze(2).to_broadcast([128, 4, 128]), op=ALU.mult),
                        [rk_r, wfb], [kw[d]])

            def stage_def_tr(i, si, bl):
                ak_r = ak_rs[i % ND]
                for h in range(2):
                    K.op(PE, lambda e: e.transpose(out=ps_tr[:, 128 * h:128 * h + 128], in_=ak_r[:, 128 * h:128 * h + 128],
                                                   identity=ident[:]), [ak_r, ident], [ps_tr], selfsync=False)
                kt = ktb[i % 2]
                K.op(DVE, lambda e: e.tensor_copy(out=kt[:].rearrange("p h t -> p (h t)"), in_=ps_tr[:, 0:256]),
                     [ps_tr], [kt])
                K.dma(SP, KTs[si], KTs[si][:, :, bl * 128:(bl + 1) * 128], kt, kt[:], chk_dst=False)

            def stage_def_state(i, si, bl, d):
                Sd = Sst[si][d]
                kw, rv_bf = kws[i % ND], rv_bfs[i % ND]
                for hp in range(2):
                    pk = ps_kv[hp]
                    for hh in range(2):
                        h = 2 * hp + hh
                        K.op(PE, lambda e: e.matmul(pk[:, 256 * hh:256 * hh + 256],
                                                    lhsT=kw[d][:, 128 * h:128 * h + 128],
                                                    rhs=rv_bf[:, 256 * h:256 * h + 256], start=True, stop=True),
                             [kw[d], rv_bf], [pk], selfsync=False)
                for hp in range(2):
                    pk = ps_kv[hp]
                    sv = Sd[:, 2 * hp:2 * hp + 2, :].rearrange("p h v -> p (h v)")
                    K.op(DVE, lambda e: e.tensor_tensor(out=sv, in0=sv, in1=pk[:], op=ALU.add), [Sd, pk], [Sd])

            for i0 in range(min(NX, NB1)):
                load_x(i0)
            stage_sq(0)
            if NB1 > 1:
                stage_sq(1)
            stage_stat(0)
            stage_uT(0)
            for idx, (i, si, bl) in enumerate(blocks):
                prev = blocks[idx - 2] if idx > 1 else None
                if i + 1 < NB1:
                    stage_stat(i + 1)
                if prev:
                    stage_def_tr(*prev)
                    stage_def_state(*prev, 0)
                if i + 2 < NB1:
                    stage_sq(i + 2)
                stage_tables(i)
                stage_proj(i, [0])
                if prev:
                    stage_def_state(*prev, 1)
                if i + 1 < NB1:
                    stage_uT(i + 1)
                if i + 3 < NB1:
                    load_x(i + 3)
                stage_proj(i, [1, 2, 3])
                stage_post(i, si, bl)
                next(late, None)
            for bk in blocks[-2:]:
                stage_def_tr(*bk)
                stage_def_state(*bk, 0)
                stage_def_state(*bk, 1)
            for _ in late:
                pass
            if debug:
                for si in range(2):
                    for d in range(2):
                        K.dma(SP, SDBG, SDBG[2 * si + d], Sst[si][d], Sst[si][d][:].rearrange("p h v -> p (h v)"),
                              chk_dst=False)
        K.barrier()

        with contextlib.ExitStack() as st:
            uTo = [K.sb(st, f"uTo{b}", [128, KC, 128], BF16) for b in range(NOWN)]
            xTs = [K.sb(st, f"xT{i}", [128, KC, 128], F32, dma=True) for i in range(2)]
            sqb = K.sb(st, "sq", [128, KC, 128], BF16)
            rt = K.sb(st, "rt", [128, 128], F32)
            rstd = K.sb(st, "rstd", [128, 128], F32)
            ps_stat = K.ps(st, "pstat", [128, 512])
            pp = [K.ps(st, f"pp{j}", [128, 512]) for j in range(4)]
            wg = [K.sb(st, f"wg{i}", [128, KC, 512], BF16, dma=True) for i in range(2)]
            stg = [K.sb(st, f"pstg{i}", [128, 512], F32) for i in range(4)]
            K.dma(SP, wg[0], wg[0][:], Wown, Wown[0])
            K.dma(SP, xTs[0], xTs[0][:], xown, xown[0])
            for b in range(NOWN):
                if b + 1 < NOWN:
                    K.dma(SP, xTs[(b + 1) % 2], xTs[(b + 1) % 2][:], xown, xown[b + 1])
                norm_block(xTs[b % 2], sqb, uTo[b], ps_stat, rt, rstd, 128)
            n = 0
            for cg in range(8):
                if cg + 1 < 8:
                    K.dma(SP, wg[(cg + 1) % 2], wg[(cg + 1) % 2][:], Wown, Wown[cg + 1])
                w = wg[cg % 2]
                for b in range(NOWN):
                    p_, s_ = pp[n % 4], stg[n % 4]
                    for kc in range(KC):
                        K.op(PE, lambda e: e.matmul(p_[:], lhsT=uTo[b][:, kc, :], rhs=w[:, kc, :], start=(kc == 0),
                                                    stop=(kc == KC - 1)), [uTo[b], w], [p_], selfsync=False)
                    if n % 2 == 0:
                        K.op(ACT, lambda e: e.activation(out=s_[:], in_=p_[:], func=AF.Copy), [p_], [s_])
                    else:
                        K.op(DVE, lambda e: e.tensor_copy(out=s_[:], in_=p_[:]), [p_], [s_])
                    K.dma(SP, PROJ, PROJ[b, :, 512 * cg:512 * cg + 512], s_, s_[:], chk_dst=False)
                    n += 1
        K.barrier()

        for si, s in enumerate(cfg.seqs):
            Sf, Sb = Sst[si]
            own = list(range(s["own0"], s["own0"] + s["nown"]))
            no = len(own)
            with contextlib.ExitStack() as st:
                pr = [K.sb(st, f"pr{i}", [128, 4096], F32, dma=True) for i in range(2)]
                MT, qdf, qdb = make_decay_tables(st)
                cst_ = cs_tmps(st, no)
                csr1 = build_cs(st, "csr1", cst_, no, 1.0)
                csrs = build_cs(st, "csrs", cst_, no, 128.0 ** -0.5)
                csa = build_cs(st, "csa", cst_, no, 128.0 ** -0.5)
                o0 = own[0]
                csr1[1]([(C("rfreq"), mo[:, o0:o0 + no, 0], 0)], no)
                csrs[1]([(C("rfreq"), mo[:, o0:o0 + no, 0], 0)], no)
                csa[1]([(C("afreq"), mo[:, o0:o0 + no, 1], 0), (C("afreq"), mo[:, o0:o0 + no, 2], 32)], no)
                tm = [K.sb(st, f"tm{i}", [128, 256], F32) for i in range(4)]
                tm2 = [K.sb(st, f"tn{i}", [128, 256], F32) for i in range(4)]
                tm3 = [K.sb(st, f"to{i}", [128, 512], F32) for i in range(4)]
                rq_r = K.sb(st, "rq_r", [128, 512], BF16)
                rk_r = K.sb(st, "rk_r", [128, 512], BF16)
                rv_bf = K.sb(st, "rv_bf", [128, 1024], BF16)
                aqsq = K.sb(st, "aqsq", [128, 1024], F32)
                aqn = K.sb(st, "aqn", [128, 1024], F32)
                aq_r = K.sb(st, "aq_r", [128, 1024], BF16)
                sk = K.sb(st, "sk", [128, 24], F32)
                ps_t1 = K.ps(st, "pt1", [128, 1024], BF16)
                ps_t2 = K.ps(st, "pt2", [128, 1024], BF16)
                ps_p = K.ps(st, "ppt", [128, 512])
                ps_o = [K.ps(st, f"po{j}", [128, 512]) for j in range(2)]
                ps_kv = [K.ps(st, f"pkv{j}", [128, 512]) for j in range(2)]
                rqT = K.sb(st, "rqT", [128, 512], BF16)
                rqfT = K.sb(st, "rqfT", [128, 512], BF16)
                rqbT = [K.sb(st, f"rqbT{i}", [128, 512], BF16) for i in range(2)]
                rkT = K.sb(st, "rkT", [128, 512], BF16)
                aqT = [K.sb(st, f"aqT{i}", [128, 1024], BF16) for i in range(2)]
                PT = K.sb(st, "PT", [128, 512], BF16)
                kw = [K.sb(st, f"kw{i}", [128, 512], BF16) for i in range(2)]
                Sf_bf = K.sb(st, "Sf_bf", [128, 1024], BF16)
                ro_s = [K.sb(st, f"ro_s{i}", [128, 1024], F32) for i in range(2)]
                kvb_s = [K.sb(st, f"kvb_s{i}", [128, 1024], F32) for i in range(2)]
                K.op(DVE, lambda e: e.tensor_copy(out=Sf_bf[:], in_=Sf[:].rearrange("p h v -> p (h v)")), [Sf], [Sf_bf])
                K.dma(SP, pr[0], pr[0][:], PROJ, PROJ[own[0]])
                for n, b in enumerate(own):
                    if n + 1 < no:
                        K.dma(SP, pr[(n + 1) % 2], pr[(n + 1) % 2][:], PROJ, PROJ[own[n + 1]])
                    p_ = pr[n % 2]
                    rope(DVE, POOL, p_, p_[:, 0:512], 4, csr1[0], n, rq_r, rq_r[:], tm)
                    rope(DVE, POOL, p_, p_[:, 512:1024], 4, csrs[0], n, rk_r, rk_r[:], tm2)
                    K.op(ACT, lambda e: e.activation(out=rv_bf[:], in_=p_[:, 1024:2048], func=AF.Copy), [p_], [rv_bf])
                    aq = p_[:, 3072:4096]
                    for h in range(8):
                        K.op(ACT, lambda e: e.activation(out=aqsq[:, 128 * h:128 * h + 128], in_=aq[:, 128 * h:128 * h + 128],
                                                         func=AF.Square, accum_out=sk[:, h:h + 1]), [p_], [aqsq, sk])
                    rstd_from(None, sk, sk[:, 0:8], 1.0 / 128, sk, sk[:, 8:16], sk, sk[:, 16:24])
                    for h in range(8):
                        K.op(DVE, lambda e: e.scalar_tensor_tensor(
                            out=aqn[:, 128 * h:128 * h + 128], in0=aq[:, 128 * h:128 * h + 128], scalar=sk[:, 16 + h:17 + h],
                            in1=C("qg"), op0=ALU.mult, op1=ALU.mult), [p_, sk, cst], [aqn])
                    rope(DVE, POOL, aqn, aqn[:], 8, csa[0], n, aq_r, aq_r[:], tm3)
                    for h in range(4):
                        K.op(PE, lambda e: e.transpose(out=ps_t1[:, 128 * h:128 * h + 128], in_=rq_r[:, 128 * h:128 * h + 128],
                                                       identity=ident[:]), [rq_r, ident], [ps_t1], selfsync=False)
                    for h in range(4):
                        K.op(PE, lambda e: e.transpose(out=ps_t1[:, 512 + 128 * h:640 + 128 * h],
                                                       in_=rk_r[:, 128 * h:128 * h + 128], identity=ident[:]),
                             [rk_r, ident], [ps_t1], selfsync=False)
                    for h in range(8):
                        K.op(PE, lambda e: e.transpose(out=ps_t2[:, 128 * h:128 * h + 128], in_=aq_r[:, 128 * h:128 * h + 128],
                                                       identity=ident[:]), [aq_r, ident], [ps_t2], selfsync=False)
                    rb = rqbT[n % 2]
                    K.op(ACT, lambda e: e.activation(out=rqT[:], in_=ps_t1[:, 0:512], func=AF.Copy), [ps_t1], [rqT])
                    K.op(DVE, lambda e: e.tensor_tensor(out=rqfT[:], in0=ps_t1[:, 0:512],
                                                        in1=qdf[:].rearrange("p h c -> p (h c)"), op=ALU.mult),
                         [ps_t1, qdf], [rqfT])
                    K.op(DVE, lambda e: e.tensor_tensor(out=rb[:], in0=ps_t1[:, 0:512],
                                                        in1=qdb[:].rearrange("p h c -> p (h c)"), op=ALU.mult),
                         [ps_t1, qdb], [rb])
                    K.op(ACT, lambda e: e.activation(out=rkT[:], in_=ps_t1[:, 512:1024], func=AF.Copy), [ps_t1], [rkT])
                    at = aqT[n % 2]
                    K.op(ACT, lambda e: e.activation(out=at[:], in_=ps_t2[:], func=AF.Copy), [ps_t2], [at])
                    K.dma(SP, QT, QT[b], at, at[:].rearrange("p (h t) -> p h t", h=8), chk_dst=False)
                    K.dma(SP, RQB, RQB[b], rb, rb[:], chk_dst=False)
                    for h in range(4):
                        K.op(PE, lambda e: e.matmul(ps_p[:, 128 * h:128 * h + 128], lhsT=rkT[:, 128 * h:128 * h + 128],
                                                    rhs=rqT[:, 128 * h:128 * h + 128], start=True, stop=True),
                             [rkT, rqT], [ps_p], selfsync=False)
                    K.op(DVE, lambda e: e.tensor_tensor(out=PT[:], in0=ps_p[:], in1=MT[:].rearrange("p h c -> p (h c)"),
                                                        op=ALU.mult), [ps_p, MT], [PT])
                    for h in range(4):
                        po = ps_o[h // 2]
                        osl = po[:, 256 * (h % 2):256 * (h % 2) + 256]
                        K.op(PE, lambda e: e.matmul(osl, lhsT=PT[:, 128 * h:128 * h + 128], rhs=rv_bf[:, 256 * h:256 * h + 256],
                                                    start=True, stop=False), [PT, rv_bf], [po], selfsync=False)
                        K.op(PE, lambda e: e.matmul(osl, lhsT=rqfT[:, 128 * h:128 * h + 128], rhs=Sf_bf[:, 256 * h:256 * h + 256],
                                                    start=False, stop=True), [rqfT, Sf_bf], [po], selfsync=False)
                    ros = ro_s[n % 2]
                    K.op(ACT, lambda e: e.activation(out=ros[:, 0:512], in_=ps_o[0][:], func=AF.Copy), [ps_o[0]], [ros])
                    K.op(ACT, lambda e: e.activation(out=ros[:, 512:1024], in_=ps_o[1][:], func=AF.Copy), [ps_o[1]], [ros])
                    K.dma(SP, RO, RO[b], ros, ros[:], chk_dst=False)
                    for d, kd in enumerate((kdf, kdb)):
                        K.op(POOL, lambda e: e.tensor_tensor(
                            out=kw[d][:].rearrange("p (h x) -> p h x", h=4), in0=rk_r[:].rearrange("p (h x) -> p h x", h=4),
                            in1=kd[:].unsqueeze(2).to_broadcast([128, 4, 128]), op=ALU.mult), [rk_r, kd], [kw[d]])
                    for hp in range(2):
                        pk = ps_kv[hp]
                        for hh in range(2):
                            h = 2 * hp + hh
                            K.op(PE, lambda e: e.matmul(pk[:, 256 * hh:256 * hh + 256], lhsT=kw[0][:, 128 * h:128 * h + 128],
                                                        rhs=rv_bf[:, 256 * h:256 * h + 256], start=True, stop=True),
                                 [kw[0], rv_bf], [pk], selfsync=False)
                        for hh in range(2):
                            h = 2 * hp + hh
                            K.op(DVE, lambda e: e.scalar_tensor_tensor(out=Sf[:, h, :], in0=Sf[:, h, :], scalar=cdf[:, h:h + 1],
                                                                       in1=pk[:, 256 * hh:256 * hh + 256], op0=ALU.mult,
                                                                       op1=ALU.add), [Sf, cdf, pk], [Sf])
                    K.op(ACT, lambda e: e.activation(out=Sf_bf[:], in_=Sf[:].rearrange("p h v -> p (h v)"), func=AF.Copy),
                         [Sf], [Sf_bf])
                    kvs = kvb_s[n % 2]
                    for hp in range(2):
                        pk = ps_kv[hp]
                        for hh in range(2):
                            h = 2 * hp + hh
                            K.op(PE, lambda e: e.matmul(pk[:, 256 * hh:256 * hh + 256], lhsT=kw[1][:, 128 * h:128 * h + 128],
                                                        rhs=rv_bf[:, 256 * h:256 * h + 256], start=True, stop=True),
                                 [kw[1], rv_bf], [pk], selfsync=False)
                        K.op(ACT, lambda e: e.activation(out=kvs[:, 512 * hp:512 * hp + 512], in_=pk[:], func=AF.Copy),
                             [pk], [kvs])
                    K.dma(SP, KVB, KVB[b], kvs, kvs[:], chk_dst=False)
            K.barrier()
            with contextlib.ExitStack() as st:
                ro_l = [K.sb(st, f"ro_l{i}", [128, 1024], F32, dma=True) for i in range(2)]
                rg_l = [K.sb(st, f"rg_l{i}", [128, 1024], F32, dma=True) for i in range(2)]
                rqb_l = [K.sb(st, f"rqb_l{i}", [128, 512], BF16, dma=True) for i in range(2)]
                kvb_l = [K.sb(st, f"kvb_l{i}", [128, 1024], F32, dma=True) for i in range(2)]
                Sb_bf = K.sb(st, "Sb_bf", [128, 1024], BF16)
                o32 = K.sb(st, "o32", [128, 1024], F32)
                osq = K.sb(st, "osq", [128, 1024], F32)
                sg = K.sb(st, "sg", [128, 1024], F32)
                mixr = K.sb(st, "mixr", [128, 1024], BF16)
                mT = [K.sb(st, f"mT{i}", [128, 1024], BF16) for i in range(2)]
                sk = K.sb(st, "sk", [128, 12], F32)
                ps_o = [K.ps(st, f"po{j}", [128, 512]) for j in range(2)]
                ps_t = K.ps(st, "pt", [128, 1024], BF16)

                def loadB(n):
                    b = own[n]
                    K.dma(SP, ro_l[n % 2], ro_l[n % 2][:], RO, RO[b])
                    K.dma(SP, rg_l[n % 2], rg_l[n % 2][:], PROJ, PROJ[b, :, 2048:3072])
                    K.dma(SP, rqb_l[n % 2], rqb_l[n % 2][:], RQB, RQB[b])
                    if n + 1 < no:
                        K.dma(SP, kvb_l[n % 2], kvb_l[n % 2][:], KVB, KVB[own[n + 1]])

                loadB(no - 1)
                for n in range(no - 1, -1, -1):
                    b = own[n]
                    if n - 1 >= 0:
                        loadB(n - 1)
                    if n + 1 < no:
                        kv = kvb_l[n % 2]
                        for h in range(4):
                            K.op(DVE, lambda e: e.scalar_tensor_tensor(out=Sb[:, h, :], in0=Sb[:, h, :], scalar=cdb[:, h:h + 1],
                                                                       in1=kv[:, 256 * h:256 * h + 256], op0=ALU.mult,
                                                                       op1=ALU.add), [Sb, cdb, kv], [Sb])
                    K.op(ACT, lambda e: e.activation(out=Sb_bf[:], in_=Sb[:].rearrange("p h v -> p (h v)"), func=AF.Copy),
                         [Sb], [Sb_bf])
                    rb, ro, rg = rqb_l[n % 2], ro_l[n % 2], rg_l[n % 2]
                    for h in range(4):
                        po = ps_o[h // 2]
                        K.op(PE, lambda e: e.matmul(po[:, 256 * (h % 2):256 * (h % 2) + 256], lhsT=rb[:, 128 * h:128 * h + 128],
                                                    rhs=Sb_bf[:, 256 * h:256 * h + 256], start=True, stop=True),
                             [rb, Sb_bf], [po], selfsync=False)
                    for hp in range(2):
                        K.op(DVE, lambda e: e.tensor_tensor(out=o32[:, 512 * hp:512 * hp + 512], in0=ps_o[hp][:],
                                                            in1=ro[:, 512 * hp:512 * hp + 512], op=ALU.add),
                             [ps_o[hp], ro], [o32])
                    for h in range(4):
                        K.op(ACT, lambda e: e.activation(out=osq[:, 256 * h:256 * h + 256], in_=o32[:, 256 * h:256 * h + 256],
                                                         func=AF.Square, accum_out=sk[:, h:h + 1]), [o32], [osq, sk])
                    rstd_from(None, sk, sk[:, 0:4], 1.0 / 256, sk, sk[:, 4:8], sk, sk[:, 8:12])
                    K.op(ACT, lambda e: e.activation(out=sg[:], in_=rg[:], func=AF.Silu), [rg], [sg])
                    for h in range(4):
                        K.op(DVE, lambda e: e.scalar_tensor_tensor(
                            out=mixr[:, 256 * h:256 * h + 256], in0=o32[:, 256 * h:256 * h + 256], scalar=sk[:, 8 + h:9 + h],
                            in1=sg[:, 256 * h:256 * h + 256], op0=ALU.mult, op1=ALU.mult), [o32, sk, sg], [mixr])
                    for c in range(8):
                        K.op(PE, lambda e: e.transpose(out=ps_t[:, 128 * c:128 * c + 128], in_=mixr[:, 128 * c:128 * c + 128],
                                                       identity=ident[:]), [mixr, ident], [ps_t], selfsync=False)
                    mt = mT[n % 2]
                    K.op(ACT, lambda e: e.activation(out=mt[:], in_=ps_t[:], func=AF.Copy), [ps_t], [mt])
                    K.dma(SP, MIXT, MIXT[:, 0:8, b * 128:(b + 1) * 128], mt, mt[:].rearrange("p (c t) -> p c t", c=8),
                          chk_dst=False)
            K.barrier()

        for si, s in enumerate(cfg.seqs):
            own = list(range(s["own0"], s["own0"] + s["nown"]))
            nb = s["nb"]
            with contextlib.ExitStack() as st:
                KT = K.sb(st, "KT", [128, 2, nb * 128], BF16, dma=True)
                V = K.sb(st, "V", [128, nb, 256], BF16, dma=True)
                npc = 4 if nb >= 8 else 1
                bounds = [nb * q // npc for q in range(npc + 1)]
                for q in range(npc):
                    a, b_ = bounds[q], bounds[q + 1]
                    K.dma(SP, KT, KT[:, :, a * 128:b_ * 128], KTs[si], KTs[si][:, :, a * 128:b_ * 128], chk_dst=(q == 0))
                    K.dma(SP, V, V[:, a:b_, :], Vs[si], Vs[si][:, a:b_, :], chk_dst=(q == 0))
                qT = [K.sb(st, f"qT{i}", [128, 8, 128], BF16, dma=True) for i in range(2)]
                late2 = None
                if si == 1:
                    stg2 = [K.sb(st, f"stgm{i}", [128, 2048], F32, dma=True) for i in range(2)]
                    wbf2 = [K.sb(st, f"wbfm{i}", [128, 2048], BF16) for i in range(2)]
                    cast_engs[0] = [0, 2]
                    late2 = run_units(late_units[KC:], stg2, wbf2, SP)
                NPS, NPT, LOOK = 3, 4, 2
                ps_s = [K.ps(st, f"pss{j}", [128, 512]) for j in range(NPS)]
                ps_ot = [K.ps(st, f"pso{j}", [128, 512]) for j in range(2)]
                ps_dn = [K.ps(st, f"psd{j}", [128, 512]) for j in range(2)]
                PTs = [K.sb(st, f"PTa{i}", [128, 512], BF16) for i in range(NPT)]
                rden = K.sb(st, "rden", [128, 512], F32)
                ma = [K.sb(st, f"ma{i}", [128, 512], BF16) for i in range(2)]
                K.dma(SP, qT[0], qT[0][:], QT, QT[own[0]])
                pairs = [(n, b, kvh, kb) for n, b in enumerate(own) for kvh in range(2) for kb in range(nb)]

                def emit_S(j):
                    n, b, kvh, kb = pairs[j]
                    if kvh == 0 and kb == 0 and n + 1 < len(own):
                        K.dma(SP, qT[(n + 1) % 2], qT[(n + 1) % 2][:], QT, QT[own[n + 1]])
                    q_ = qT[n % 2]
                    qr = q_[:, 4 * kvh:4 * kvh + 4, :].rearrange("p h t -> p (h t)")
                    nk = 16 if kb == 0 else 128
                    pss, pt = ps_s[j % NPS], PTs[j % NPT]
                    K.op(PE, lambda e: e.matmul(pss[0:nk, :], lhsT=KT[:, kvh, kb * 128:kb * 128 + nk], rhs=qr,
                                                start=True, stop=True), [KT, q_], [pss], selfsync=False)
                    K.op(ACT, lambda e: e.activation(out=pt[0:nk, :], in_=pss[0:nk, :], func=AF.Exp,
                                                     bias=negB[0:nk, 0:1]), [pss, negB], [pt])

                def emit_PV(j):
                    n, b, kvh, kb = pairs[j]
                    nk = 16 if kb == 0 else 128
                    pt = PTs[j % NPT]
                    pot, pdn = ps_ot[kvh], ps_dn[kvh]
                    K.op(PE, lambda e: e.matmul(pot[:], lhsT=V[0:nk, kb, 128 * kvh:128 * kvh + 128], rhs=pt[0:nk, :],
                                                start=(kb == 0), stop=(kb == nb - 1)), [V, pt], [pot], selfsync=False)
                    K.op(PE, lambda e: e.matmul(pdn[:], lhsT=ones[0:nk, :], rhs=pt[0:nk, :],
                                                start=(kb == 0), stop=(kb == nb - 1)), [ones, pt], [pdn], selfsync=False)
                    if kb == nb - 1:
                        K.op(DVE, lambda e: e.reciprocal(out=rden[:], in_=pdn[:]), [pdn], [rden])
                        m_ = ma[kvh]
                        K.op(DVE, lambda e: e.tensor_tensor(out=m_[:], in0=pot[:], in1=rden[:], op=ALU.mult), [pot, rden], [m_])
                        K.dma(SP, MIXT, MIXT[:, 8 + 4 * kvh:12 + 4 * kvh, b * 128:(b + 1) * 128], m_,
                              m_[:].rearrange("p (h t) -> p h t", h=4), chk_dst=False)

                for j in range(min(LOOK, len(pairs))):
                    emit_S(j)
                every = max(1, len(pairs) // (len(late_units) - KC + 1))
                for j in range(len(pairs)):
                    if j + LOOK < len(pairs):
                        emit_S(j + LOOK)
                    emit_PV(j)
                    if late2 is not None and j % every == every - 1:
                        next(late2, None)
                if late2 is not None:
                    for _ in late2:
                        pass
                    cast_engs[0] = [0, 1, 2]
            K.barrier()

        NT = NOWN // 4
        with contextlib.ExitStack() as st:
            hT = K.sb(st, "hT", [128, KC, 512], F32, dma=True)
            actT = K.sb(st, "actT", [128, KC, 512], BF16, dma=True)
            aT = K.sb(st, "aT", [128, 64, 512], BF16)
            NWB = 3
            wb = [K.sb(st, f"wb{i}", [128, 8192], BF16, dma=True) for i in range(NWB)]
            rt = K.sb(st, "rt", [128, 512], F32)
            rstd = K.sb(st, "rstd", [128, 512], F32)
            rl = [K.sb(st, f"rl{i}", [128, 512], F32) for i in range(2)]
            pm = [K.ps(st, f"pm{j}", [128, 512]) for j in range(4)]
            ps_stat = K.ps(st, "pstat", [128, 512])
            slabs = []
            for t in range(NT):
                for g in range(4):
                    slabs.append((Wouts, Wouts[g].rearrange("p k j -> p (k j)")))
                for g in range(16):
                    slabs.append((Wups, Wups[g].rearrange("p k j -> p (k j)")))
                for g in range(16):
                    slabs.append((Wdowns, Wdowns[g].rearrange("p f j -> p (f j)")))
            wi = [0]

            def issue_slab():
                i = wi[0]
                if i < len(slabs):
                    K.dma(SP, wb[i % NWB], wb[i % NWB][:], slabs[i][0], slabs[i][1])
                wi[0] += 1

            used = [0]

            def next_slab():
                i = used[0]
                used[0] += 1
                return wb[i % NWB]

            for _ in range(NWB - 1):
                issue_slab()
            cnt = 0
            for t in range(NT):
                tok = slice(t * 512, (t + 1) * 512)
                K.dma(SP, hT, hT[:].rearrange("p k (b t) -> p k b t", b=4), xown,
                      xown[4 * t:4 * t + 4].rearrange("b p k t -> p k b t"))
                K.dma(SP, actT, actT[:], MIXT, MIXT[:, :, tok])
                for g in range(4):
                    issue_slab()
                    w = next_slab()
                    wv = w[:].rearrange("p (k j) -> p k j", k=KC)
                    for dd in range(4):
                        dc = 4 * g + dd
                        p_ = pm[cnt % 4]
                        cnt += 1
                        for kc in range(KC):
                            K.op(PE, lambda e: e.matmul(p_[:], lhsT=wv[:, kc, 128 * dd:128 * dd + 128], rhs=actT[:, kc, :],
                                                        start=(kc == 0), stop=(kc == KC - 1)), [w, actT], [p_], selfsync=False)
                        K.op(DVE, lambda e: e.tensor_tensor(out=hT[:, dc, :], in0=p_[:], in1=hT[:, dc, :], op=ALU.add),
                             [p_, hT], [hT])
                K.op(ACT, lambda e: e.activation(out=actT[:], in_=hT[:], func=AF.Square), [hT], [actT])
                for kc in range(KC):
                    K.op(PE, lambda e: e.matmul(ps_stat[:], lhsT=ones[:], rhs=actT[:, kc, :], start=(kc == 0),
                                                stop=(kc == KC - 1)), [ones, actT], [ps_stat], selfsync=False)
                rstd_from(None, ps_stat, ps_stat[:], 1.0 / D, rt, rt[:], rstd, rstd[:])
                for (E, a, b_) in ((DVE, 0, 8), (POOL, 8, 16)):
                    K.op(E, lambda e: e.tensor_tensor(out=actT[:, a:b_, :], in0=hT[:, a:b_, :],
                                                      in1=rstd[:].unsqueeze(1).to_broadcast([128, b_ - a, 512]),
                                                      op=ALU.mult), [hT, rstd], [actT])
                for g in range(16):
                    issue_slab()
                    w = next_slab()
                    wv = w[:].rearrange("p (k j) -> p k j", k=KC)
                    for ff in range(4):
                        fc = 4 * g + ff
                        p_ = pm[cnt % 4]
                        r_ = rl[cnt % 2]
                        cnt += 1
                        for kc in range(KC):
                            K.op(PE, lambda e: e.matmul(p_[:], lhsT=wv[:, kc, 128 * ff:128 * ff + 128], rhs=actT[:, kc, :],
                                                        start=(kc == 0), stop=(kc == KC - 1)), [w, actT], [p_], selfsync=False)
                        K.op(ACT, lambda e: e.activation(out=r_[:], in_=p_[:], func=AF.Relu), [p_], [r_])
                        K.op(POOL if fc % 2 == 0 else DVE, lambda e: e.tensor_tensor(out=aT[:, fc, :], in0=r_[:], in1=r_[:],
                                                                                      op=ALU.mult), [r_], [aT])
                for dc in range(16):
                    issue_slab()
                    w = next_slab()
                    wv = w[:].rearrange("p (f j) -> p f j", f=64)
                    p_ = pm[cnt % 4]
                    cnt += 1
                    for fc in range(64):
                        K.op(PE, lambda e: e.matmul(p_[:], lhsT=wv[:, fc, :], rhs=aT[:, fc, :], start=(fc == 0),
                                                    stop=(fc == 63)), [w, aT], [p_], selfsync=False)
                    K.op(DVE, lambda e: e.tensor_tensor(out=hT[:, dc, :], in0=p_[:], in1=hT[:, dc, :], op=ALU.add),
                         [p_, hT], [hT])
                K.op(ACT, lambda e: e.activation(out=actT[:], in_=hT[:], func=AF.Square), [hT], [actT])
                for kc in range(KC):
                    K.op(PE, lambda e: e.matmul(ps_stat[:], lhsT=ones[:], rhs=actT[:, kc, :], start=(kc == 0),
                                                stop=(kc == KC - 1)), [ones, actT], [ps_stat], selfsync=False)
                rstd_from(None, ps_stat, ps_stat[:], 1.0 / D, rt, rt[:], rstd, rstd[:])
                gfc = C("gfc")
                for dc in range(16):
                    K.op(DVE, lambda e: e.scalar_tensor_tensor(
                        out=hT[:, dc, :], in0=hT[:, dc, :], scalar=gfc[:, dc:dc + 1], in1=rstd[:], op0=ALU.mult,
                        op1=ALU.mult), [hT, cst, rstd], [hT])
                K.dma(SP, yT, yT[:, :, tok], hT, hT[:], chk_dst=False)
        SP.waited.pop(id(yT.dsem), None)
        SP.wait((yT.dsem, yT.dval))
    K.es.close()
    return nc


N_META = 16
GRID_W = 64
ROPE_THETA = 10000.0


def _blocks_T(h):
    nb = h.shape[0] // 128
    return np.ascontiguousarray(h.reshape(nb, 128, KC, 128).transpose(0, 3, 2, 1))


def _consts(q_g, k_g, lgf, lgb, g1, g2, gf):
    c = np.zeros((128, NCST), np.float32)

    def put(name, arr):
        o, w = C_OFF[name]
        c[:, o:o + w] = arr

    idx = np.arange(128, dtype=np.float32)
    dmat = idx[None, :] - idx[:, None]
    put("ident", np.eye(128, dtype=np.float32))
    put("dp", np.maximum(dmat, 0.0))
    put("dn", np.maximum(-dmat, 0.0))
    put("cp1", np.broadcast_to(idx[None, :] + 1.0, (128, 128)))
    put("c128m", np.broadcast_to(128.0 - idx[None, :], (128, 128)))
    rfreq = (np.float32(ROPE_THETA) ** (-np.linspace(0.0, 1.0, 64, dtype=np.float32))).astype(np.float32)
    afreq = (np.float32(ROPE_THETA) ** (-np.arange(32, dtype=np.float32) / np.float32(32))).astype(np.float32)
    put("rfreq", np.broadcast_to(rfreq[None], (128, 64)))
    put("afreq", np.broadcast_to(afreq[None], (128, 32)))
    put("g1c", g1.reshape(KC, 128).T)
    put("g2c", g2.reshape(KC, 128).T)
    put("gfc", gf.reshape(KC, 128).T)
    put("qg", np.broadcast_to(q_g[None], (128, 128)))
    put("kg", np.broadcast_to(k_g[None], (128, 128)))
    put("lgf", np.broadcast_to(lgf[None], (128, 4)))
    put("lgb", np.broadcast_to(lgb[None], (128, 4)))
    put("colc", idx[:, None])
    put("col127", 127.0 - idx[:, None])
    return c


def _seq_meta(nb_real):
    L = (nb_real + 1) * 128
    pos = np.zeros(L, np.float32)
    row = np.zeros(L, np.float32)
    col = np.zeros(L, np.float32)
    valid = np.zeros(L, np.float32)
    pos[0:16] = 112 + np.arange(16)
    valid[0:16] = 1
    j = np.arange(nb_real * 128)
    pos[128:] = 128 + j
    row[128:] = j // GRID_W
    col[128:] = j % GRID_W
    valid[128:] = 1
    return pos, row, col, valid


def prepare(cfg, x_prompt, x_sample, meta_tokens, ln1_g, w_in, q_norm_g, k_norm_g, ret_log_decay_fwd,
            ret_log_decay_bwd, w_out, ln2_g, w_up, w_down, final_norm_g):
    f32 = np.float32
    cst = _consts(np.asarray(q_norm_g[0], f32), np.asarray(k_norm_g[0], f32), np.asarray(ret_log_decay_fwd[0], f32),
                  np.asarray(ret_log_decay_bwd[0], f32), np.asarray(ln1_g[0], f32), np.asarray(ln2_g[0], f32),
                  np.asarray(final_norm_g, f32))
    mblk = np.zeros((128, D), f32)
    mblk[0:16] = meta_tokens
    mblkT = _blocks_T(mblk)
    xs_T = _blocks_T(np.asarray(x_sample[0], f32))
    xp_T = [_blocks_T(np.asarray(x_prompt[b], f32)) for b in range(x_prompt.shape[0])]
    shared = dict(cst=cst, w_in=np.ascontiguousarray(w_in[0], f32), w_out=np.ascontiguousarray(w_out[0], f32),
                  w_up=np.ascontiguousarray(w_up[0], f32), w_down=np.ascontiguousarray(w_down[0], f32))
    in_maps = []
    for c in range(8):
        pb, ph = c // 2, c % 2
        xb1 = np.concatenate([mblkT, xp_T[pb], mblkT, xs_T], axis=0)
        own_p = list(range(ph * cfg.OWN_P, (ph + 1) * cfg.OWN_P))
        own_s = list(range(c * cfg.OWN_S, (c + 1) * cfg.OWN_S))
        xown = np.concatenate([xp_T[pb][own_p], xs_T[own_s]], axis=0)
        metas, metao = [], []
        for (nbr, ownl) in ((cfg.NBP, own_p), (cfg.NBS, own_s)):
            pos, row, col, valid = _seq_meta(nbr)
            start = 128.0 + ownl[0] * 128
            end = 128.0 + (ownl[-1] + 1) * 128
            mf = ((pos < start) & (valid > 0)).astype(f32)
            mb = ((pos >= end) & (valid > 0)).astype(f32)
            df = np.where(mf > 0, start - 1.0 - pos, 0.0).astype(f32)
            db = np.where(mb > 0, pos - end, 0.0).astype(f32)
            m = np.stack([pos, row, col, df, mf, db, mb], axis=-1).reshape(nbr + 1, 128, 7)
            metas.append(m)
            osl = [1 + o for o in ownl]
            metao.append(m[osl][:, :, 0:3])
        meta1 = np.ascontiguousarray(np.concatenate(metas, axis=0).transpose(1, 0, 2))
        metao = np.ascontiguousarray(np.concatenate(metao, axis=0).transpose(1, 0, 2))
        d = dict(xb1=np.ascontiguousarray(xb1), xown=np.ascontiguousarray(xown), meta1=meta1, metao=metao)
        d.update(shared)
        in_maps.append(d)
    return in_maps


def assemble(cfg, results, nbatch):
    yp = np.zeros((nbatch, cfg.NBP * 128, D), np.float32)
    ys = np.zeros((1, cfg.NBS * 128, D), np.float32)
    for c in range(8):
        y = results[c]["yT"]
        y = y.transpose(2, 1, 0).reshape(cfg.NOWN * 128, D)
        pb, ph = c // 2, c % 2
        np_ = cfg.OWN_P * 128
        yp[pb, ph * np_:(ph + 1) * np_] = y[0:np_]
        ns_ = cfg.OWN_S * 128
        ys[0, c * ns_:(c + 1) * ns_] = y[np_:np_ + ns_]
    return yp, ys


_NC_CACHE = {}


def kernel(x_prompt, x_sample, meta_tokens, ln1_g, w_in, q_norm_g, k_norm_g, ret_log_decay_fwd,
           ret_log_decay_bwd, w_out, ln2_g, w_up, w_down, final_norm_g):
    cfg = Cfg(16, 128)
    args = [np.asarray(a) for a in (x_prompt, x_sample, meta_tokens, ln1_g, w_in, q_norm_g, k_norm_g,
                                    ret_log_decay_fwd, ret_log_decay_bwd, w_out, ln2_g, w_up, w_down, final_norm_g)]
    in_maps = prepare(cfg, *args)
    nc = build(cfg)
    res = run_bass_kernel_spmd(nc, in_maps, core_ids=list(range(8)))
    yp, ys = assemble(cfg, res.results, args[0].shape[0])
    return (yp, ys)
```

```python
import contextlib
import math
import numpy as np
import ml_dtypes
import concourse.bass as bass
import concourse.mybir as mybir
from concourse.bass_utils import run_bass_kernel_spmd

F32 = mybir.dt.float32
BF16 = mybir.dt.bfloat16
I32 = mybir.dt.int32
AF = mybir.ActivationFunctionType
ALU = mybir.AluOpType
AX = mybir.AxisListType

D = 2048
KC = 16
DFF = 8192
EPS = 1e-6
TWO_PI = 2.0 * math.pi


class Buf:
    def __init__(self, t, name):
        self.t = t
        self.name = name
        self.w = None
        self.r = {}
        self.dsem = None
        self.dval = 0
        self.is_dram = False

    def __getitem__(self, key):
        return self.t[key]


class Eng:
    def __init__(self, name, eng, sem):
        self.name = name
        self.e = eng
        self.sem = sem
        self.count = 0
        self.waited = {}

    def wait(self, dep):
        sem, val = dep
        if self.waited.get(id(sem), 0) >= val:
            return
        self.waited[id(sem)] = val
        self.e.wait_ge(sem, val)


class Kern:
    def __init__(self, nc, n_dma_sems=96):
        self.nc = nc
        self.es = contextlib.ExitStack()
        self.uid = 0
        self.PE = self._mk("pe", nc.tensor)
        self.DVE = self._mk("dve", nc.vector)
        self.ACT = self._mk("act", nc.scalar)
        self.POOL = self._mk("pool", nc.gpsimd)
        self.SP = self._mk("sp", nc.sync)
        self.engs = [self.PE, self.DVE, self.ACT, self.POOL, self.SP]
        self.pool = [[self.es.enter_context(nc.semaphore(f"dq{i}")), 0] for i in range(n_dma_sems)]
        self.live = []

    def _mk(self, name, eng):
        return Eng(name, eng, self.es.enter_context(self.nc.semaphore("s_" + name)))

    def _give_sem(self, b):
        ent = self.pool.pop()
        b.dsem, b.dval = ent[0], ent[1]
        b._ent = ent
        self.live.append(b)

    def release(self, bufs):
        for b in bufs:
            if b.dsem is not None and b in self.live:
                self.live.remove(b)
                b._ent[1] = b.dval
                self.pool.append(b._ent)

    def sb(self, stack, name, shape, dt, dma=False):
        self.uid += 1
        t = stack.enter_context(self.nc.sbuf_tensor(f"{name}_{self.uid}", list(shape), dt))
        b = Buf(t, name)
        if dma:
            self._give_sem(b)
        stack.callback(self.release, [b])
        return b

    def ps(self, stack, name, shape, dt=F32):
        self.uid += 1
        t = stack.enter_context(self.nc.psum_tensor(f"{name}_{self.uid}", list(shape), dt))
        return Buf(t, name)

    def dram(self, name, shape, dt, kind="Internal"):
        t = self.nc.dram_tensor(name, list(shape), dt, kind=kind)
        b = Buf(t.ap(), name)
        b.is_dram = True
        return b

    def op(self, E, fn, reads=(), writes=(), selfsync=True):
        deps = []
        for b in reads:
            if b.w is not None:
                deps.append(b.w)
        for b in writes:
            if b.w is not None:
                deps.append(b.w)
            deps.extend(b.r.values())
        for d in deps:
            if d[0] is E.sem and not selfsync:
                continue
            E.wait(d)
        ins = fn(E.e)
        E.count += 1
        ins.then_inc(E.sem, 1)
        me = (E.sem, E.count)
        for b in reads:
            b.r[id(E.sem)] = me
        for b in writes:
            b.w = me
            b.r = {}
        return ins

    def dma(self, Q, dst, dst_ap, src, src_ap, chk_dst=True, **kw):
        deps = []
        if src.w is not None:
            deps.append(src.w)
        if chk_dst:
            if dst.w is not None:
                deps.append(dst.w)
            deps.extend(dst.r.values())
        sb_ = src if dst.is_dram else dst
        if sb_.dsem is None:
            self._give_sem(sb_)
        if sb_.dval > 0:
            deps.append((sb_.dsem, sb_.dval))
        for d in deps:
            Q.wait(d)
        ins = Q.e.dma_start(out=dst_ap, in_=src_ap, **kw)
        sb_.dval += 16
        ins.then_inc(sb_.dsem, 16)
        me = (sb_.dsem, sb_.dval)
        src.r[id(sb_.dsem)] = me
        dst.w = me
        if chk_dst:
            dst.r = {}
        return ins

    def barrier(self, engs=None):
        marks = [(E.sem, E.count) for E in self.engs if E.count > 0]
        marks += [(b.dsem, b.dval) for b in self.live if b.dval > 0]
        marks += [(ent[0], ent[1]) for ent in self.pool if ent[1] > 0]
        for E in (engs or self.engs):
            for m in marks:
                if m[0] is E.sem:
                    continue
                E.wait(m)


C_OFF = {}
_o = 0
for _n, _w in [("ident", 128), ("dp", 128), ("dn", 128), ("cp1", 128), ("c128m", 128), ("rfreq", 64),
               ("afreq", 32), ("g1c", 16), ("g2c", 16), ("gfc", 16), ("qg", 128), ("kg", 128), ("lgf", 4),
               ("lgb", 4), ("colc", 1), ("col127", 1)]:
    C_OFF[_n] = (_o, _w)
    _o += _w
NCST = _o


class Cfg:
    def __init__(self, nbp=16, nbs=128):
        self.NBP = nbp
        self.NBS = nbs
        self.OWN_P = nbp // 2
        self.OWN_S = nbs // 8
        self.NOWN = self.OWN_P + self.OWN_S
        self.NB1 = nbp + 1 + nbs + 1
        assert self.NOWN % 4 == 0
        self.seqs = [dict(name="p", nb=nbp + 1, b0=0, own0=0, nown=self.OWN_P),
                     dict(name="s", nb=nbs + 1, b0=nbp + 1, own0=self.OWN_P, nown=self.OWN_S)]


def build(cfg, debug=False):
    nc = bass.Bass("TRN2", target_bir_lowering=False)
    K = Kern(nc)
    NB1, NOWN = cfg.NB1, cfg.NOWN
    PE, DVE, ACT, POOL, SP = K.PE, K.DVE, K.ACT, K.POOL, K.SP
    skind = "ExternalOutput" if debug else "Internal"

    xb1 = K.dram("xb1", [NB1, 128, KC, 128], F32, "ExternalInput")
    xown = K.dram("xown", [NOWN, 128, KC, 128], F32, "ExternalInput")
    meta1 = K.dram("meta1", [128, NB1, 7], F32, "ExternalInput")
    metao = K.dram("metao", [128, NOWN, 3], F32, "ExternalInput")
    cst_d = K.dram("cst", [128, NCST], F32, "ExternalInput")
    w_in = K.dram("w_in", [D, 4608], F32, "ExternalInput")
    w_out = K.dram("w_out", [D, D], F32, "ExternalInput")
    w_up = K.dram("w_up", [D, DFF], F32, "ExternalInput")
    w_down = K.dram("w_down", [DFF, D], F32, "ExternalInput")
    yT = K.dram("yT", [128, KC, NOWN * 128], F32, "ExternalOutput")

    W1s = K.dram("W1s", [128, KC, 2048], BF16)
    Wown = K.dram("Wown", [8, 128, KC, 512], BF16)
    Wouts = K.dram("Wouts", [4, 128, KC, 512], BF16)
    Wups = K.dram("Wups", [16, 128, KC, 512], BF16)
    Wdowns = K.dram("Wdowns", [16, 128, 64, 128], BF16)
    KTs = [K.dram(f"KT_{s['name']}", [128, 2, s["nb"] * 128], BF16, skind) for s in cfg.seqs]
    Vs = [K.dram(f"V_{s['name']}", [128, s["nb"], 256], BF16, skind) for s in cfg.seqs]
    PROJ = K.dram("PROJ", [NOWN, 128, 4096], F32, skind)
    QT = K.dram("QT", [NOWN, 128, 8, 128], BF16, skind)
    RO = K.dram("RO", [NOWN, 128, 1024], F32, skind)
    RQB = K.dram("RQB", [NOWN, 128, 512], BF16, skind)
    KVB = K.dram("KVB", [NOWN, 128, 1024], F32, skind)
    MIXT = K.dram("MIXT", [128, KC, NOWN * 128], BF16, skind)
    SDBG = K.dram("SDBG", [4, 128, 1024], F32, skind) if debug else None

    with contextlib.ExitStack() as G:
        cst = K.sb(G, "cst", [128, NCST], F32, dma=True)
        K.dma(SP, cst, cst[:], cst_d, cst_d[:])

        def C(name, rows=128):
            o, w = C_OFF[name]
            return cst[0:rows, o:o + w]

        ident = K.sb(G, "ident", [128, 128], BF16)
        ones = K.sb(G, "ones", [128, 128], BF16)
        epsb = K.sb(G, "epsb", [128, 1], F32)
        negB = K.sb(G, "negB", [128, 1], F32)
        kdf = K.sb(G, "kdf", [128, 4], F32)
        kdb = K.sb(G, "kdb", [128, 4], F32)
        cdf = K.sb(G, "cdf", [128, 4], F32)
        cdb = K.sb(G, "cdb", [128, 4], F32)
        wfb = K.sb(G, "wfb", [128, NB1, 8], F32)
        m1 = K.sb(G, "m1", [128, NB1, 7], F32, dma=True)
        mo = K.sb(G, "mo", [128, NOWN, 3], F32, dma=True)
        Sst = [[K.sb(G, f"S{d}{s['name']}", [128, 4, 256], F32) for d in "fb"] for s in cfg.seqs]
        tmpc = K.sb(G, "tmpc", [128, max(NB1, 128)], F32)
        tmpd = K.sb(G, "tmpd", [128, max(NB1, 128)], F32)

        K.dma(SP, m1, m1[:], meta1, meta1[:])
        K.dma(SP, mo, mo[:], metao, metao[:])

        K.op(DVE, lambda e: e.tensor_copy(out=ident[:], in_=C("ident")), [cst], [ident])
        K.op(DVE, lambda e: e.memset(ones[:], 1.0), [], [ones])
        K.op(DVE, lambda e: e.memset(epsb[:], EPS), [], [epsb])
        for s in range(2):
            for d in range(2):
                K.op(POOL, lambda e: e.memset(Sst[s][d][:], 0.0), [], [Sst[s][d]])
        K.op(DVE, lambda e: e.tensor_reduce(out=tmpc[:, 0:1], in_=C("qg"), axis=AX.X, op=ALU.max,
                                            apply_absolute_value=True), [cst], [tmpc])
        K.op(DVE, lambda e: e.tensor_reduce(out=tmpc[:, 1:2], in_=C("kg"), axis=AX.X, op=ALU.max,
                                            apply_absolute_value=True), [cst], [tmpc])
        K.op(DVE, lambda e: e.scalar_tensor_tensor(out=negB[:], in0=tmpc[:, 0:1], scalar=-math.sqrt(128.0),
                                                   in1=tmpc[:, 1:2], op0=ALU.mult, op1=ALU.mult), [tmpc], [negB])
        lgf, lgb = C("lgf"), C("lgb")
        def make_decay_tables(st):
            MT = K.sb(st, "MT", [128, 4, 128], F32)
            qdf = K.sb(st, "qdf", [128, 4, 128], F32)
            qdb = K.sb(st, "qdb", [128, 4, 128], F32)
            for h in range(4):
                K.op(DVE, lambda e: e.tensor_scalar(out=tmpc[:, 0:128], in0=C("dp"), scalar1=lgf[:, h:h + 1],
                                                    scalar2=None, op0=ALU.mult), [cst], [tmpc])
                K.op(DVE, lambda e: e.scalar_tensor_tensor(out=tmpd[:, 0:128], in0=C("dn"), scalar=lgb[:, h:h + 1],
                                                           in1=tmpc[:, 0:128], op0=ALU.mult, op1=ALU.add),
                     [cst, tmpc], [tmpd])
                K.op(ACT, lambda e: e.activation(out=MT[:, h, :], in_=tmpd[:, 0:128], func=AF.Exp), [tmpd], [MT])
                K.op(ACT, lambda e: e.activation(out=qdf[:, h, :], in_=C("cp1"), func=AF.Exp, scale=lgf[:, h:h + 1]),
                     [cst], [qdf])
                K.op(ACT, lambda e: e.activation(out=qdb[:, h, :], in_=C("c128m"), func=AF.Exp, scale=lgb[:, h:h + 1]),
                     [cst], [qdb])
            return MT, qdf, qdb

        for h in range(4):
            K.op(ACT, lambda e: e.activation(out=kdf[:, h:h + 1], in_=C("col127"), func=AF.Exp,
                                             scale=lgf[:, h:h + 1]), [cst], [kdf])
            K.op(ACT, lambda e: e.activation(out=kdb[:, h:h + 1], in_=C("colc"), func=AF.Exp,
                                             scale=lgb[:, h:h + 1]), [cst], [kdb])
            for d, (lg, dcol, mcol) in enumerate([(lgf, 3, 4), (lgb, 5, 6)]):
                K.op(ACT, lambda e: e.activation(out=tmpc[:, 0:NB1], in_=m1[:, :, dcol], func=AF.Exp,
                                                 scale=lg[:, h:h + 1]), [m1, cst], [tmpc])
                K.op(DVE, lambda e: e.tensor_tensor(out=wfb[:, :, 4 * d + h], in0=tmpc[:, 0:NB1],
                                                    in1=m1[:, :, mcol], op=ALU.mult), [tmpc, m1], [wfb])
        K.op(ACT, lambda e: e.activation(out=cdf[:], in_=lgf, func=AF.Exp, scale=128.0), [cst], [cdf])
        K.op(ACT, lambda e: e.activation(out=cdb[:], in_=lgb, func=AF.Exp, scale=128.0), [cst], [cdb])

        def cs_tmps(st, Gn):
            return (K.sb(st, "cs_a", [128, Gn, 64], F32), K.sb(st, "cs_k", [128, Gn, 64], I32),
                    K.sb(st, "cs_f", [128, Gn, 64], F32))

        def build_cs(st, name, tmps, Gn, scale):
            ang, ki, kf = tmps
            cs = K.sb(st, name + "_cs", [128, Gn, 128], F32)

            def fill(specs_now, Gc):
                for (fr, pos, off) in specs_now:
                    nf = fr.shape[-1]
                    K.op(DVE, lambda e: e.tensor_tensor(
                        out=ang[:, 0:Gc, off:off + nf], in0=fr.unsqueeze(1).to_broadcast([128, Gc, nf]),
                        in1=pos.unsqueeze(2).to_broadcast([128, Gc, nf]), op=ALU.mult), [cst, m1, mo], [ang])
                K.op(DVE, lambda e: e.tensor_scalar(out=ki[:, 0:Gc, :], in0=ang[:, 0:Gc, :], scalar1=1.0 / TWO_PI,
                                                    scalar2=None, op0=ALU.mult), [ang], [ki])
                K.op(DVE, lambda e: e.tensor_copy(out=kf[:, 0:Gc, :], in_=ki[:, 0:Gc, :]), [ki], [kf])
                K.op(DVE, lambda e: e.scalar_tensor_tensor(out=ang[:, 0:Gc, :], in0=kf[:, 0:Gc, :], scalar=-TWO_PI,
                                                           in1=ang[:, 0:Gc, :], op0=ALU.mult, op1=ALU.add),
                     [kf, ang], [ang])
                K.op(ACT, lambda e: e.activation(out=cs[:, 0:Gc, 0:64], in_=ang[:, 0:Gc, :], func=AF.Sin,
                                                 scale=1.0 - 1e-6), [ang], [cs])
                K.op(ACT, lambda e: e.activation(out=kf[:, 0:Gc, :], in_=ang[:, 0:Gc, :], func=AF.Sin,
                                                 scale=0.5), [ang], [kf])
                K.op(DVE, lambda e: e.scalar_tensor_tensor(out=kf[:, 0:Gc, :], in0=kf[:, 0:Gc, :], scalar=-2.0,
                                                           in1=kf[:, 0:Gc, :], op0=ALU.mult, op1=ALU.mult),
                     [kf], [kf])
                if scale == 1.0:
                    K.op(DVE, lambda e: e.tensor_scalar(out=cs[:, 0:Gc, 64:128], in0=kf[:, 0:Gc, :], scalar1=1.0,
                                                        scalar2=None, op0=ALU.add), [kf], [cs])
                else:
                    K.op(DVE, lambda e: e.tensor_scalar(out=cs[:, 0:Gc, 64:128], in0=kf[:, 0:Gc, :], scalar1=1.0,
                                                        scalar2=scale, op0=ALU.add, op1=ALU.mult), [kf], [cs])
                    K.op(DVE, lambda e: e.tensor_scalar(out=cs[:, 0:Gc, 0:64], in0=cs[:, 0:Gc, 0:64],
                                                        scalar1=scale, scalar2=None, op0=ALU.mult), [cs], [cs])
            return cs, fill

        def rope(E1, E2, xb, x_ap, nh, cs, g, ob, o_ap, tm):
            xv = x_ap.rearrange("p (h i t) -> p h i t", h=nh, t=2)
            ov = o_ap.rearrange("p (h i t) -> p h i t", h=nh, t=2)
            x1, x2 = xv[:, :, :, 0], xv[:, :, :, 1]
            sb_ = cs[:, g, 0:64].unsqueeze(1).to_broadcast([128, nh, 64])
            cb_ = cs[:, g, 64:128].unsqueeze(1).to_broadcast([128, nh, 64])
            t = [tm[i][:, 0:nh * 64].rearrange("p (h i) -> p h i", h=nh) for i in range(4)]
            K.op(E1, lambda e: e.tensor_tensor(out=t[0], in0=x1, in1=cb_, op=ALU.mult), [xb, cs], [tm[0]])
            K.op(E1, lambda e: e.tensor_tensor(out=t[1], in0=x2, in1=sb_, op=ALU.mult), [xb, cs], [tm[1]])
            K.op(E1, lambda e: e.tensor_tensor(out=ov[:, :, :, 0], in0=t[0], in1=t[1], op=ALU.subtract),
                 [tm[0], tm[1]], [ob])
            K.op(E2, lambda e: e.tensor_tensor(out=t[2], in0=x1, in1=sb_, op=ALU.mult), [xb, cs], [tm[2]])
            K.op(E2, lambda e: e.tensor_tensor(out=t[3], in0=x2, in1=cb_, op=ALU.mult), [xb, cs], [tm[3]])
            K.op(E2, lambda e: e.tensor_tensor(out=ov[:, :, :, 1], in0=t[2], in1=t[3], op=ALU.add),
                 [tm[2], tm[3]], [ob])

        def rstd_from(E_sq, ssq_b, ssq_ap, inv_n, tmp_b, tmp_ap, out_b, out_ap):
            rows = ssq_ap.shape[0]
            K.op(ACT, lambda e: e.activation(out=tmp_ap, in_=ssq_ap, func=AF.Sqrt, scale=inv_n,
                                             bias=epsb[0:rows, 0:1]), [ssq_b, epsb], [tmp_b])
            K.op(DVE, lambda e: e.reciprocal(out=out_ap, in_=tmp_ap), [tmp_b], [out_b])

        def norm_block(xTb, sqb, uTb, ps_stat, rt, rstd, ntok):
            K.op(ACT, lambda e: e.activation(out=sqb[:, :, 0:ntok], in_=xTb[:, :, 0:ntok], func=AF.Square),
                 [xTb], [sqb])
            for kc in range(KC):
                K.op(PE, lambda e: e.matmul(ps_stat[:, 0:ntok], lhsT=ones[:], rhs=sqb[:, kc, 0:ntok],
                                            start=(kc == 0), stop=(kc == KC - 1)), [ones, sqb], [ps_stat],
                     selfsync=False)
            rstd_from(None, ps_stat, ps_stat[:, 0:ntok], 1.0 / D, rt, rt[:, 0:ntok], rstd, rstd[:, 0:ntok])
            h = KC // 2
            for (E, a, b) in ((DVE, 0, h), (POOL, h, KC)):
                K.op(E, lambda e: e.tensor_tensor(
                    out=uTb[:, a:b, 0:ntok], in0=xTb[:, a:b, 0:ntok],
                    in1=rstd[:, 0:ntok].unsqueeze(1).to_broadcast([128, b - a, ntok]), op=ALU.mult),
                    [xTb, rstd], [uTb])

        cast_rr = [0]

        cast_engs = [[0, 1, 2]]

        def cast(srcb, src_ap, dstb, dst_ap, gcol=None):
            i = cast_engs[0][cast_rr[0] % len(cast_engs[0])]
            cast_rr[0] += 1
            if i == 0:
                if gcol is None:
                    K.op(DVE, lambda e: e.tensor_copy(out=dst_ap, in_=src_ap), [srcb], [dstb])
                else:
                    K.op(DVE, lambda e: e.tensor_scalar(out=dst_ap, in0=src_ap, scalar1=gcol, scalar2=None,
                                                        op0=ALU.mult), [srcb, cst], [dstb])
            elif i == 1:
                K.op(ACT, lambda e: e.activation(out=dst_ap, in_=src_ap, func=AF.Copy,
                                                 scale=(1.0 if gcol is None else gcol)), [srcb, cst], [dstb])
            else:
                if gcol is None:
                    K.op(POOL, lambda e: e.tensor_copy(out=dst_ap, in_=src_ap), [srcb], [dstb])
                else:
                    K.op(POOL, lambda e: e.tensor_scalar(out=dst_ap, in0=src_ap, scalar1=gcol, scalar2=None,
                                                         op0=ALU.mult), [srcb, cst], [dstb])

        with contextlib.ExitStack() as st:
            NBUF = 3
            stg = [K.sb(st, f"stg{i}", [128, 2048], F32, dma=True) for i in range(NBUF)]
            wbf = [K.sb(st, f"wbf{i}", [128, 2048], BF16) for i in range(NBUF)]
            g1c, g2c = C("g1c"), C("g2c")
            units = []
            for kc in range(KC):
                rows = slice(kc * 128, (kc + 1) * 128)
                units.append((w_in, w_in[rows, 0:2048], 2048, g1c[:, kc:kc + 1],
                              [(Wown, Wown[0:4, :, kc, :].rearrange("g p j -> p g j"), 0, 2048, 4),
                               (W1s, W1s[:, kc, 512:2048], 512, 1536, 0)]))
                units.append((w_in, w_in[rows, 2048:4096], 2048, g1c[:, kc:kc + 1],
                              [(Wown, Wown[4:8, :, kc, :].rearrange("g p j -> p g j"), 0, 2048, 4)]))
                units.append((w_in, w_in[rows, 4096:4608], 512, g1c[:, kc:kc + 1],
                              [(W1s, W1s[:, kc, 0:512], 0, 512, 0)]))
            for kc in range(KC):
                rows = slice(kc * 128, (kc + 1) * 128)
                units.append((w_out, w_out[rows, :], 2048, None,
                              [(Wouts, Wouts[:, :, kc, :].rearrange("g p j -> p g j"), 0, 2048, 4)]))
            for kc in range(KC):
                rows = slice(kc * 128, (kc + 1) * 128)
                for q in range(4):
                    units.append((w_up, w_up[rows, q * 2048:(q + 1) * 2048], 2048, g2c[:, kc:kc + 1],
                                  [(Wups, Wups[4 * q:4 * q + 4, :, kc, :].rearrange("g p j -> p g j"), 0, 2048, 4)]))
            for fc in range(64):
                rows = slice(fc * 128, (fc + 1) * 128)
                units.append((w_down, w_down[rows, :], 2048, None,
                              [(Wdowns, Wdowns[:, :, fc, :].rearrange("g p j -> p g j"), 0, 2048, 16)]))

            def run_units(ulist, stg, wbf, Q):
                nb_ = len(stg)

                def ld(i):
                    src, sap, n, _, _ = ulist[i]
                    K.dma(SP, stg[i % nb_], stg[i % nb_][:, 0:n], src, sap)

                for i in range(min(nb_ - 1, len(ulist))):
                    ld(i)
                for i, (src, sap, n, gcol, stores) in enumerate(ulist):
                    if i + nb_ - 1 < len(ulist):
                        ld(i + nb_ - 1)
                    s_, w_ = stg[i % nb_], wbf[i % nb_]
                    cast(s_, s_[:, 0:n], w_, w_[:, 0:n], gcol)
                    for (dstb, dap, off, nn, ng) in stores:
                        sap2 = w_[:, off:off + nn]
                        if ng:
                            sap2 = sap2.rearrange("p (g j) -> p g j", g=ng)
                        K.dma(Q, dstb, dap, w_, sap2, chk_dst=False)
                    yield i

            n_in = 3 * KC
            for _ in run_units(units[:n_in], stg, wbf, ACT):
                pass
            late_units = units[n_in:]
        K.barrier()

        GT = 4
        with contextlib.ExitStack() as st:
            W1 = K.sb(st, "W1", [128, KC, 2048], BF16, dma=True)
            stg1 = [K.sb(st, f"stgl{i}", [128, 2048], F32, dma=True) for i in range(2)]
            wbf1 = [K.sb(st, f"wbfl{i}", [128, 2048], BF16) for i in range(2)]
            late = run_units(late_units[:KC], stg1, wbf1, SP)
            for q in range(4):
                K.dma(SP, W1, W1[:, 4 * q:4 * q + 4, :], W1s, W1s[:, 4 * q:4 * q + 4, :], chk_dst=(q == 0))
            NX = 3
            xTs = [K.sb(st, f"xT{i}", [128, KC, 128], F32, dma=True) for i in range(NX)]
            sqbs = [K.sb(st, f"sq{i}", [128, KC, 128], BF16) for i in range(2)]
            uTs = [K.sb(st, f"uT{i}", [128, KC, 128], BF16) for i in range(2)]
            rt = K.sb(st, "rt", [128, 128], F32)
            rstd = K.sb(st, "rstd", [128, 128], F32)
            pp = [K.ps(st, f"pp{j}", [128, 512]) for j in range(4)]
            ps_stat = K.ps(st, "pstat", [128, 512])
            ps_tr = K.ps(st, "ptr", [128, 1024], BF16)
            ps_kv = [K.ps(st, f"pkv{j}", [128, 512]) for j in range(2)]
            cst_ = cs_tmps(st, GT)
            cs_r = [build_cs(st, f"csr{i}", cst_, GT, 128.0 ** -0.5) for i in range(2)]
            cs_a = [build_cs(st, f"csa{i}", cst_, GT, 1.0) for i in range(2)]
            rk32 = K.sb(st, "rk32", [128, 512], F32)
            ak32 = K.sb(st, "ak32", [128, 256], F32)
            akn = K.sb(st, "akn", [128, 256], F32)
            tm = [K.sb(st, f"tm{i}", [128, 256], F32) for i in range(4)]
            tm2 = [K.sb(st, f"tn{i}", [128, 256], F32) for i in range(4)]
            rk_r = K.sb(st, "rk_r", [128, 512], BF16)
            ND = 3
            kws = [[K.sb(st, f"kw{j}{i}", [128, 512], BF16) for i in range(2)] for j in range(ND)]
            rv_bfs = [K.sb(st, f"rv_bf{j}", [128, 1024], BF16) for j in range(ND)]
            ak_rs = [K.sb(st, f"ak_r{j}", [128, 256], BF16) for j in range(ND)]
            av_bf = [K.sb(st, f"av_bf{i}", [128, 256], BF16) for i in range(2)]
            ktb = [K.sb(st, f"ktb{i}", [128, 2, 128], BF16) for i in range(2)]
            sk = K.sb(st, "sk", [128, 8], F32)
            junk = K.sb(st, "junk", [128, 128], F32)

            def load_x(i):
                K.dma(SP, xTs[i % NX], xTs[i % NX][:], xb1, xb1[i])

            blocks = []
            for si, s in enumerate(cfg.seqs):
                for bl in range(s["nb"]):
                    blocks.append((s["b0"] + bl, si, bl))

            def stage_sq(i):
                xTb, sqb = xTs[i % NX], sqbs[i % 2]
                K.op(ACT, lambda e: e.activation(out=sqb[:], in_=xTb[:], func=AF.Square), [xTb], [sqb])

            def stage_stat(i):
                sqb = sqbs[i % 2]
                for kc in range(KC):
                    K.op(PE, lambda e: e.matmul(ps_stat[:, 0:128], lhsT=ones[:], rhs=sqb[:, kc, :],
                                                start=(kc == 0), stop=(kc == KC - 1)), [ones, sqb], [ps_stat],
                         selfsync=False)
                K.op(ACT, lambda e: e.activation(out=rt[:], in_=ps_stat[:, 0:128], func=AF.Sqrt, scale=1.0 / D,
                                                 bias=epsb[:, 0:1]), [ps_stat, epsb], [rt])

            def stage_uT(i):
                xTb, uTb = xTs[i % NX], uTs[i % 2]
                K.op(DVE, lambda e: e.reciprocal(out=rstd[:], in_=rt[:]), [rt], [rstd])
                h = KC // 2
                for (E, a_, b_) in ((DVE, 0, h), (POOL, h, KC)):
                    K.op(E, lambda e: e.tensor_tensor(
                        out=uTb[:, a_:b_, :], in0=xTb[:, a_:b_, :],
                        in1=rstd[:].unsqueeze(1).to_broadcast([128, b_ - a_, 128]), op=ALU.mult),
                        [xTb, rstd], [uTb])

            def stage_tables(i):
                g, gi = i % GT, (i // GT) % 2
                if g == 0:
                    Gc = min(GT, NB1 - i)
                    cs_r[gi][1]([(C("rfreq"), m1[:, i:i + Gc, 0], 0)], Gc)
                    cs_a[gi][1]([(C("afreq"), m1[:, i:i + Gc, 1], 0), (C("afreq"), m1[:, i:i + Gc, 2], 32)], Gc)

            def stage_proj(i, banks):
                uTb = uTs[i % 2]
                for j in banks:
                    for kc in range(KC):
                        K.op(PE, lambda e: e.matmul(pp[j][:], lhsT=uTb[:, kc, :], rhs=W1[:, kc, 512 * j:512 * j + 512],
                                                    start=(kc == 0), stop=(kc == KC - 1)), [uTb, W1], [pp[j]],
                             selfsync=False)

            def stage_post(i, si, bl):
                g, gi = i % GT, (i // GT) % 2
                csr, csa = cs_r[gi][0], cs_a[gi][0]
                kw, rv_bf, ak_r = kws[i % ND], rv_bfs[i % ND], ak_rs[i % ND]
                K.op(ACT, lambda e: e.activation(out=ak32[:], in_=pp[0][:, 0:256], func=AF.Copy), [pp[0]], [ak32])
                avb = av_bf[i % 2]
                K.op(ACT, lambda e: e.activation(out=avb[:], in_=pp[0][:, 256:512], func=AF.Copy), [pp[0]], [avb])
                K.dma(SP, Vs[si], Vs[si][:, bl, :], avb, avb[:], chk_dst=False)
                for h in range(2):
                    K.op(ACT, lambda e: e.activation(out=junk[:], in_=ak32[:, 128 * h:128 * h + 128], func=AF.Square,
                                                     accum_out=sk[:, h:h + 1]), [ak32], [junk, sk])
                rstd_from(None, sk, sk[:, 0:2], 1.0 / 128, sk, sk[:, 2:4], sk, sk[:, 4:6])
                for h in range(2):
                    K.op(DVE, lambda e: e.scalar_tensor_tensor(out=akn[:, 128 * h:128 * h + 128],
                                                               in0=ak32[:, 128 * h:128 * h + 128],
                                                               scalar=sk[:, 4 + h:5 + h], in1=C("kg"),
                                                               op0=ALU.mult, op1=ALU.mult), [ak32, sk, cst], [akn])
                rope(POOL, DVE, akn, akn[:], 2, csa, g, ak_r, ak_r[:], tm2)
                K.op(ACT, lambda e: e.activation(out=rk32[:], in_=pp[1][:], func=AF.Copy), [pp[1]], [rk32])
                K.op(ACT, lambda e: e.activation(out=rv_bf[:, 0:512], in_=pp[2][:], func=AF.Copy), [pp[2]], [rv_bf])
                K.op(ACT, lambda e: e.activation(out=rv_bf[:, 512:1024], in_=pp[3][:], func=AF.Copy), [pp[3]], [rv_bf])
                rope(DVE, POOL, rk32, rk32[:], 4, csr, g, rk_r, rk_r[:], tm)
                for d in range(2):
                    K.op(DVE if d == 0 else POOL, lambda e: e.tensor_tensor(
                        out=kw[d][:].rearrange("p (h x) -> p h x", h=4),
                        in0=rk_r[:].rearrange("p (h x) -> p h x", h=4),
                        in1=wfb[:, i, 4 * d:4 * d + 4].unsqueeze(2).to_broadcast([128, 4, 128]), op=ALU.mult),
                        [rk_r, wfb], [kw[d]])

            def stage_def_tr(i, si, bl):
                ak_r = ak_rs[i % ND]
                for h in range(2):
                    K.op(PE, lambda e: e.transpose(out=ps_tr[:, 128 * h:128 * h + 128], in_=ak_r[:, 128 * h:128 * h + 128],
                                                   identity=ident[:]), [ak_r, ident], [ps_tr], selfsync=False)
                kt = ktb[i % 2]
                K.op(DVE, lambda e: e.tensor_copy(out=kt[:].rearrange("p h t -> p (h t)"), in_=ps_tr[:, 0:256]),
                     [ps_tr], [kt])
                K.dma(SP, KTs[si], KTs[si][:, :, bl * 128:(bl + 1) * 128], kt, kt[:], chk_dst=False)

            def stage_def_state(i, si, bl, d):
                Sd = Sst[si][d]
                kw, rv_bf = kws[i % ND], rv_bfs[i % ND]
                for hp in range(2):
                    pk = ps_kv[hp]
                    for hh in range(2):
                        h = 2 * hp + hh
                        K.op(PE, lambda e: e.matmul(pk[:, 256 * hh:256 * hh + 256],
                                                    lhsT=kw[d][:, 128 * h:128 * h + 128],
                                                    rhs=rv_bf[:, 256 * h:256 * h + 256], start=True, stop=True),
                             [kw[d], rv_bf], [pk], selfsync=False)
                for hp in range(2):
                    pk = ps_kv[hp]
                    sv = Sd[:, 2 * hp:2 * hp + 2, :].rearrange("p h v -> p (h v)")
                    K.op(DVE, lambda e: e.tensor_tensor(out=sv, in0=sv, in1=pk[:], op=ALU.add), [Sd, pk], [Sd])

            for i0 in range(min(NX, NB1)):
                load_x(i0)
            stage_sq(0)
            if NB1 > 1:
                stage_sq(1)
            stage_stat(0)
            stage_uT(0)
            for idx, (i, si, bl) in enumerate(blocks):
                prev = blocks[idx - 2] if idx > 1 else None
                if i + 1 < NB1:
                    stage_stat(i + 1)
                if prev:
                    stage_def_tr(*prev)
                    stage_def_state(*prev, 0)
                if i + 2 < NB1:
                    stage_sq(i + 2)
                stage_tables(i)
                stage_proj(i, [0])
                if prev:
                    stage_def_state(*prev, 1)
                if i + 1 < NB1:
                    stage_uT(i + 1)
                if i + 3 < NB1:
                    load_x(i + 3)
                stage_proj(i, [1, 2, 3])
                stage_post(i, si, bl)
                next(late, None)
            for bk in blocks[-2:]:
                stage_def_tr(*bk)
                stage_def_state(*bk, 0)
                stage_def_state(*bk, 1)
            for _ in late:
                pass
            if debug:
                for si in range(2):
                    for d in range(2):
                        K.dma(SP, SDBG, SDBG[2 * si + d], Sst[si][d], Sst[si][d][:].rearrange("p h v -> p (h v)"),
                              chk_dst=False)
        K.barrier()

        with contextlib.ExitStack() as st:
            uTo = [K.sb(st, f"uTo{b}", [128, KC, 128], BF16) for b in range(NOWN)]
            xTs = [K.sb(st, f"xT{i}", [128, KC, 128], F32, dma=True) for i in range(2)]
            sqb = K.sb(st, "sq", [128, KC, 128], BF16)
            rt = K.sb(st, "rt", [128, 128], F32)
            rstd = K.sb(st, "rstd", [128, 128], F32)
            ps_stat = K.ps(st, "pstat", [128, 512])
            pp = [K.ps(st, f"pp{j}", [128, 512]) for j in range(4)]
            wg = [K.sb(st, f"wg{i}", [128, KC, 512], BF16, dma=True) for i in range(2)]
            stg = [K.sb(st, f"pstg{i}", [128, 512], F32) for i in range(4)]
            K.dma(SP, wg[0], wg[0][:], Wown, Wown[0])
            K.dma(SP, xTs[0], xTs[0][:], xown, xown[0])
            def own_norm(b):
                if b + 1 < NOWN:
                    K.dma(SP, xTs[(b + 1) % 2], xTs[(b + 1) % 2][:], xown, xown[b + 1])
                norm_block(xTs[b % 2], sqb, uTo[b], ps_stat, rt, rstd, 128)

            own_norm(0)
            n = 0
            for cg in range(8):
                if cg + 1 < 8:
                    K.dma(SP, wg[(cg + 1) % 2], wg[(cg + 1) % 2][:], Wown, Wown[cg + 1])
                w = wg[cg % 2]
                for b in range(NOWN):
                    if cg == 0 and b + 1 < NOWN:
                        own_norm(b + 1)
                    p_, s_ = pp[n % 4], stg[n % 4]
                    for kc in range(KC):
                        K.op(PE, lambda e: e.matmul(p_[:], lhsT=uTo[b][:, kc, :], rhs=w[:, kc, :], start=(kc == 0),
                                                    stop=(kc == KC - 1)), [uTo[b], w], [p_], selfsync=False)
                    if n % 2 == 0:
                        K.op(ACT, lambda e: e.activation(out=s_[:], in_=p_[:], func=AF.Copy), [p_], [s_])
                    else:
                        K.op(DVE, lambda e: e.tensor_copy(out=s_[:], in_=p_[:]), [p_], [s_])
                    K.dma(SP, PROJ, PROJ[b, :, 512 * cg:512 * cg + 512], s_, s_[:], chk_dst=False)
                    n += 1
        K.barrier()

        for si, s in enumerate(cfg.seqs):
            Sf, Sb = Sst[si]
            own = list(range(s["own0"], s["own0"] + s["nown"]))
            no = len(own)
            with contextlib.ExitStack() as st:
                pr = [K.sb(st, f"pr{i}", [128, 4096], F32, dma=True) for i in range(2)]
                MT, qdf, qdb = make_decay_tables(st)
                cst_ = cs_tmps(st, no)
                csr1 = build_cs(st, "csr1", cst_, no, 1.0)
                csrs = build_cs(st, "csrs", cst_, no, 128.0 ** -0.5)
                csa = build_cs(st, "csa", cst_, no, 128.0 ** -0.5)
                o0 = own[0]
                csr1[1]([(C("rfreq"), mo[:, o0:o0 + no, 0], 0)], no)
                csrs[1]([(C("rfreq"), mo[:, o0:o0 + no, 0], 0)], no)
                csa[1]([(C("afreq"), mo[:, o0:o0 + no, 1], 0), (C("afreq"), mo[:, o0:o0 + no, 2], 32)], no)
                tm = [K.sb(st, f"tm{i}", [128, 256], F32) for i in range(4)]
                tm2 = [K.sb(st, f"tn{i}", [128, 256], F32) for i in range(4)]
                tm3 = [K.sb(st, f"to{i}", [128, 512], F32) for i in range(4)]
                rq_r = K.sb(st, "rq_r", [128, 512], BF16)
                rk_r = K.sb(st, "rk_r", [128, 512], BF16)
                rv_bf = K.sb(st, "rv_bf", [128, 1024], BF16)
                aqsq = K.sb(st, "aqsq", [128, 1024], F32)
                aqn = K.sb(st, "aqn", [128, 1024], F32)
                aq_r = K.sb(st, "aq_r", [128, 1024], BF16)
                sk = K.sb(st, "sk", [128, 24], F32)
                ps_t1 = K.ps(st, "pt1", [128, 1024], BF16)
                ps_t2 = K.ps(st, "pt2", [128, 1024], BF16)
                ps_p = K.ps(st, "ppt", [128, 512])
                ps_o = [K.ps(st, f"po{j}", [128, 512]) for j in range(2)]
                ps_kv = [K.ps(st, f"pkv{j}", [128, 512]) for j in range(2)]
                rqT = K.sb(st, "rqT", [128, 512], BF16)
                rqfT = K.sb(st, "rqfT", [128, 512], BF16)
                rqbT = [K.sb(st, f"rqbT{i}", [128, 512], BF16) for i in range(2)]
                rkT = K.sb(st, "rkT", [128, 512], BF16)
                aqT = [K.sb(st, f"aqT{i}", [128, 1024], BF16) for i in range(2)]
                PT = K.sb(st, "PT", [128, 512], BF16)
                kw = [K.sb(st, f"kw{i}", [128, 512], BF16) for i in range(2)]
                Sf_bf = K.sb(st, "Sf_bf", [128, 1024], BF16)
                ro_s = [K.sb(st, f"ro_s{i}", [128, 1024], F32) for i in range(2)]
                kvb_s = [K.sb(st, f"kvb_s{i}", [128, 1024], F32) for i in range(2)]
                K.op(DVE, lambda e: e.tensor_copy(out=Sf_bf[:], in_=Sf[:].rearrange("p h v -> p (h v)")), [Sf], [Sf_bf])
                K.dma(SP, pr[0], pr[0][:], PROJ, PROJ[own[0]])
                for n, b in enumerate(own):
                    if n + 1 < no:
                        K.dma(SP, pr[(n + 1) % 2], pr[(n + 1) % 2][:], PROJ, PROJ[own[n + 1]])
                    p_ = pr[n % 2]
                    rope(DVE, POOL, p_, p_[:, 0:512], 4, csr1[0], n, rq_r, rq_r[:], tm)
                    rope(DVE, POOL, p_, p_[:, 512:1024], 4, csrs[0], n, rk_r, rk_r[:], tm2)
                    K.op(ACT, lambda e: e.activation(out=rv_bf[:], in_=p_[:, 1024:2048], func=AF.Copy), [p_], [rv_bf])
                    aq = p_[:, 3072:4096]
                    for h in range(8):
                        K.op(ACT, lambda e: e.activation(out=aqsq[:, 128 * h:128 * h + 128], in_=aq[:, 128 * h:128 * h + 128],
                                                         func=AF.Square, accum_out=sk[:, h:h + 1]), [p_], [aqsq, sk])
                    rstd_from(None, sk, sk[:, 0:8], 1.0 / 128, sk, sk[:, 8:16], sk, sk[:, 16:24])
                    for h in range(8):
                        K.op(DVE, lambda e: e.scalar_tensor_tensor(
                            out=aqn[:, 128 * h:128 * h + 128], in0=aq[:, 128 * h:128 * h + 128], scalar=sk[:, 16 + h:17 + h],
                            in1=C("qg"), op0=ALU.mult, op1=ALU.mult), [p_, sk, cst], [aqn])
                    rope(DVE, POOL, aqn, aqn[:], 8, csa[0], n, aq_r, aq_r[:], tm3)
                    for h in range(4):
                        K.op(PE, lambda e: e.transpose(out=ps_t1[:, 128 * h:128 * h + 128], in_=rq_r[:, 128 * h:128 * h + 128],
                                                       identity=ident[:]), [rq_r, ident], [ps_t1], selfsync=False)
                    for h in range(4):
                        K.op(PE, lambda e: e.transpose(out=ps_t1[:, 512 + 128 * h:640 + 128 * h],
                                                       in_=rk_r[:, 128 * h:128 * h + 128], identity=ident[:]),
                             [rk_r, ident], [ps_t1], selfsync=False)
                    for h in range(8):
                        K.op(PE, lambda e: e.transpose(out=ps_t2[:, 128 * h:128 * h + 128], in_=aq_r[:, 128 * h:128 * h + 128],
                                                       identity=ident[:]), [aq_r, ident], [ps_t2], selfsync=False)
                    rb = rqbT[n % 2]
                    K.op(ACT, lambda e: e.activation(out=rqT[:], in_=ps_t1[:, 0:512], func=AF.Copy), [ps_t1], [rqT])
                    K.op(DVE, lambda e: e.tensor_tensor(out=rqfT[:], in0=ps_t1[:, 0:512],
                                                        in1=qdf[:].rearrange("p h c -> p (h c)"), op=ALU.mult),
                         [ps_t1, qdf], [rqfT])
                    K.op(DVE, lambda e: e.tensor_tensor(out=rb[:], in0=ps_t1[:, 0:512],
                                                        in1=qdb[:].rearrange("p h c -> p (h c)"), op=ALU.mult),
                         [ps_t1, qdb], [rb])
                    K.op(ACT, lambda e: e.activation(out=rkT[:], in_=ps_t1[:, 512:1024], func=AF.Copy), [ps_t1], [rkT])
                    at = aqT[n % 2]
                    K.op(ACT, lambda e: e.activation(out=at[:], in_=ps_t2[:], func=AF.Copy), [ps_t2], [at])
                    K.dma(SP, QT, QT[b], at, at[:].rearrange("p (h t) -> p h t", h=8), chk_dst=False)
                    K.dma(SP, RQB, RQB[b], rb, rb[:], chk_dst=False)
                    for h in range(4):
                        K.op(PE, lambda e: e.matmul(ps_p[:, 128 * h:128 * h + 128], lhsT=rkT[:, 128 * h:128 * h + 128],
                                                    rhs=rqT[:, 128 * h:128 * h + 128], start=True, stop=True),
                             [rkT, rqT], [ps_p], selfsync=False)
                    K.op(DVE, lambda e: e.tensor_tensor(out=PT[:], in0=ps_p[:], in1=MT[:].rearrange("p h c -> p (h c)"),
                                                        op=ALU.mult), [ps_p, MT], [PT])
                    for h in range(4):
                        po = ps_o[h // 2]
                        osl = po[:, 256 * (h % 2):256 * (h % 2) + 256]
                        K.op(PE, lambda e: e.matmul(osl, lhsT=PT[:, 128 * h:128 * h + 128], rhs=rv_bf[:, 256 * h:256 * h + 256],
                                                    start=True, stop=False), [PT, rv_bf], [po], selfsync=False)
                        K.op(PE, lambda e: e.matmul(osl, lhsT=rqfT[:, 128 * h:128 * h + 128], rhs=Sf_bf[:, 256 * h:256 * h + 256],
                                                    start=False, stop=True), [rqfT, Sf_bf], [po], selfsync=False)
                    ros = ro_s[n % 2]
                    K.op(ACT, lambda e: e.activation(out=ros[:, 0:512], in_=ps_o[0][:], func=AF.Copy), [ps_o[0]], [ros])
                    K.op(ACT, lambda e: e.activation(out=ros[:, 512:1024], in_=ps_o[1][:], func=AF.Copy), [ps_o[1]], [ros])
                    K.dma(SP, RO, RO[b], ros, ros[:], chk_dst=False)
                    for d, kd in enumerate((kdf, kdb)):
                        K.op(POOL, lambda e: e.tensor_tensor(
                            out=kw[d][:].rearrange("p (h x) -> p h x", h=4), in0=rk_r[:].rearrange("p (h x) -> p h x", h=4),
                            in1=kd[:].unsqueeze(2).to_broadcast([128, 4, 128]), op=ALU.mult), [rk_r, kd], [kw[d]])
                    for hp in range(2):
                        pk = ps_kv[hp]
                        for hh in range(2):
                            h = 2 * hp + hh
                            K.op(PE, lambda e: e.matmul(pk[:, 256 * hh:256 * hh + 256], lhsT=kw[0][:, 128 * h:128 * h + 128],
                                                        rhs=rv_bf[:, 256 * h:256 * h + 256], start=True, stop=True),
                                 [kw[0], rv_bf], [pk], selfsync=False)
                        for hh in range(2):
                            h = 2 * hp + hh
                            K.op(DVE, lambda e: e.scalar_tensor_tensor(out=Sf[:, h, :], in0=Sf[:, h, :], scalar=cdf[:, h:h + 1],
                                                                       in1=pk[:, 256 * hh:256 * hh + 256], op0=ALU.mult,
                                                                       op1=ALU.add), [Sf, cdf, pk], [Sf])
                    K.op(ACT, lambda e: e.activation(out=Sf_bf[:], in_=Sf[:].rearrange("p h v -> p (h v)"), func=AF.Copy),
                         [Sf], [Sf_bf])
                    kvs = kvb_s[n % 2]
                    for hp in range(2):
                        pk = ps_kv[hp]
                        for hh in range(2):
                            h = 2 * hp + hh
                            K.op(PE, lambda e: e.matmul(pk[:, 256 * hh:256 * hh + 256], lhsT=kw[1][:, 128 * h:128 * h + 128],
                                                        rhs=rv_bf[:, 256 * h:256 * h + 256], start=True, stop=True),
                                 [kw[1], rv_bf], [pk], selfsync=False)
                        K.op(ACT, lambda e: e.activation(out=kvs[:, 512 * hp:512 * hp + 512], in_=pk[:], func=AF.Copy),
                             [pk], [kvs])
                    K.dma(SP, KVB, KVB[b], kvs, kvs[:], chk_dst=False)
            K.barrier()
            with contextlib.ExitStack() as st:
                ro_l = [K.sb(st, f"ro_l{i}", [128, 1024], F32, dma=True) for i in range(2)]
                rg_l = [K.sb(st, f"rg_l{i}", [128, 1024], F32, dma=True) for i in range(2)]
                rqb_l = [K.sb(st, f"rqb_l{i}", [128, 512], BF16, dma=True) for i in range(2)]
                kvb_l = [K.sb(st, f"kvb_l{i}", [128, 1024], F32, dma=True) for i in range(2)]
                Sb_bf = K.sb(st, "Sb_bf", [128, 1024], BF16)
                o32 = K.sb(st, "o32", [128, 1024], F32)
                osq = K.sb(st, "osq", [128, 1024], F32)
                sg = K.sb(st, "sg", [128, 1024], F32)
                mixr = K.sb(st, "mixr", [128, 1024], BF16)
                mT = [K.sb(st, f"mT{i}", [128, 1024], BF16) for i in range(2)]
                sk = K.sb(st, "sk", [128, 12], F32)
                ps_o = [K.ps(st, f"po{j}", [128, 512]) for j in range(2)]
                ps_t = K.ps(st, "pt", [128, 1024], BF16)

                def loadB(n):
                    b = own[n]
                    K.dma(SP, ro_l[n % 2], ro_l[n % 2][:], RO, RO[b])
                    K.dma(SP, rg_l[n % 2], rg_l[n % 2][:], PROJ, PROJ[b, :, 2048:3072])
                    K.dma(SP, rqb_l[n % 2], rqb_l[n % 2][:], RQB, RQB[b])
                    if n + 1 < no:
                        K.dma(SP, kvb_l[n % 2], kvb_l[n % 2][:], KVB, KVB[own[n + 1]])

                loadB(no - 1)
                for n in range(no - 1, -1, -1):
                    b = own[n]
                    if n - 1 >= 0:
                        loadB(n - 1)
                    if n + 1 < no:
                        kv = kvb_l[n % 2]
                        for h in range(4):
                            K.op(DVE, lambda e: e.scalar_tensor_tensor(out=Sb[:, h, :], in0=Sb[:, h, :], scalar=cdb[:, h:h + 1],
                                                                       in1=kv[:, 256 * h:256 * h + 256], op0=ALU.mult,
                                                                       op1=ALU.add), [Sb, cdb, kv], [Sb])
                    K.op(ACT, lambda e: e.activation(out=Sb_bf[:], in_=Sb[:].rearrange("p h v -> p (h v)"), func=AF.Copy),
                         [Sb], [Sb_bf])
                    rb, ro, rg = rqb_l[n % 2], ro_l[n % 2], rg_l[n % 2]
                    for h in range(4):
                        po = ps_o[h // 2]
                        K.op(PE, lambda e: e.matmul(po[:, 256 * (h % 2):256 * (h % 2) + 256], lhsT=rb[:, 128 * h:128 * h + 128],
                                                    rhs=Sb_bf[:, 256 * h:256 * h + 256], start=True, stop=True),
                             [rb, Sb_bf], [po], selfsync=False)
                    for hp in range(2):
                        K.op(DVE, lambda e: e.tensor_tensor(out=o32[:, 512 * hp:512 * hp + 512], in0=ps_o[hp][:],
                                                            in1=ro[:, 512 * hp:512 * hp + 512], op=ALU.add),
                             [ps_o[hp], ro], [o32])
                    for h in range(4):
                        K.op(ACT, lambda e: e.activation(out=osq[:, 256 * h:256 * h + 256], in_=o32[:, 256 * h:256 * h + 256],
                                                         func=AF.Square, accum_out=sk[:, h:h + 1]), [o32], [osq, sk])
                    rstd_from(None, sk, sk[:, 0:4], 1.0 / 256, sk, sk[:, 4:8], sk, sk[:, 8:12])
                    K.op(ACT, lambda e: e.activation(out=sg[:], in_=rg[:], func=AF.Silu), [rg], [sg])
                    for h in range(4):
                        K.op(DVE, lambda e: e.scalar_tensor_tensor(
                            out=mixr[:, 256 * h:256 * h + 256], in0=o32[:, 256 * h:256 * h + 256], scalar=sk[:, 8 + h:9 + h],
                            in1=sg[:, 256 * h:256 * h + 256], op0=ALU.mult, op1=ALU.mult), [o32, sk, sg], [mixr])
                    for c in range(8):
                        K.op(PE, lambda e: e.transpose(out=ps_t[:, 128 * c:128 * c + 128], in_=mixr[:, 128 * c:128 * c + 128],
                                                       identity=ident[:]), [mixr, ident], [ps_t], selfsync=False)
                    mt = mT[n % 2]
                    K.op(ACT, lambda e: e.activation(out=mt[:], in_=ps_t[:], func=AF.Copy), [ps_t], [mt])
                    K.dma(SP, MIXT, MIXT[:, 0:8, b * 128:(b + 1) * 128], mt, mt[:].rearrange("p (c t) -> p c t", c=8),
                          chk_dst=False)
            K.barrier()

        for si, s in enumerate(cfg.seqs):
            own = list(range(s["own0"], s["own0"] + s["nown"]))
            nb = s["nb"]
            with contextlib.ExitStack() as st:
                KT = K.sb(st, "KT", [128, 2, nb * 128], BF16, dma=True)
                V = K.sb(st, "V", [128, nb, 256], BF16, dma=True)
                npc = 4 if nb >= 8 else 1
                bounds = [nb * q // npc for q in range(npc + 1)]
                for q in range(npc):
                    a, b_ = bounds[q], bounds[q + 1]
                    K.dma(SP, KT, KT[:, :, a * 128:b_ * 128], KTs[si], KTs[si][:, :, a * 128:b_ * 128], chk_dst=(q == 0))
                    K.dma(SP, V, V[:, a:b_, :], Vs[si], Vs[si][:, a:b_, :], chk_dst=(q == 0))
                qT = [K.sb(st, f"qT{i}", [128, 8, 128], BF16, dma=True) for i in range(2)]
                late2 = None
                if si == 1:
                    stg2 = [K.sb(st, f"stgm{i}", [128, 2048], F32, dma=True) for i in range(2)]
                    wbf2 = [K.sb(st, f"wbfm{i}", [128, 2048], BF16) for i in range(2)]
                    cast_engs[0] = [0, 2]
                    late2 = run_units(late_units[KC:], stg2, wbf2, SP)
                NPS, NPT, LOOK = 3, 4, 2
                ps_s = [K.ps(st, f"pss{j}", [128, 512]) for j in range(NPS)]
                ps_ot = [K.ps(st, f"pso{j}", [128, 512]) for j in range(2)]
                ps_dn = [K.ps(st, f"psd{j}", [128, 512]) for j in range(2)]
                PTs = [K.sb(st, f"PTa{i}", [128, 512], BF16) for i in range(NPT)]
                rden = K.sb(st, "rden", [128, 512], F32)
                ma = [K.sb(st, f"ma{i}", [128, 512], BF16) for i in range(2)]
                K.dma(SP, qT[0], qT[0][:], QT, QT[own[0]])
                pairs = [(n, b, kvh, kb) for n, b in enumerate(own) for kvh in range(2) for kb in range(nb)]

                def emit_S(j):
                    n, b, kvh, kb = pairs[j]
                    if kvh == 0 and kb == 0 and n + 1 < len(own):
                        K.dma(SP, qT[(n + 1) % 2], qT[(n + 1) % 2][:], QT, QT[own[n + 1]])
                    q_ = qT[n % 2]
                    qr = q_[:, 4 * kvh:4 * kvh + 4, :].rearrange("p h t -> p (h t)")
                    nk = 16 if kb == 0 else 128
                    pss, pt = ps_s[j % NPS], PTs[j % NPT]
                    K.op(PE, lambda e: e.matmul(pss[0:nk, :], lhsT=KT[:, kvh, kb * 128:kb * 128 + nk], rhs=qr,
                                                start=True, stop=True), [KT, q_], [pss], selfsync=False)
                    K.op(ACT, lambda e: e.activation(out=pt[0:nk, :], in_=pss[0:nk, :], func=AF.Exp,
                                                     bias=negB[0:nk, 0:1]), [pss, negB], [pt])

                def emit_PV(j):
                    n, b, kvh, kb = pairs[j]
                    nk = 16 if kb == 0 else 128
                    pt = PTs[j % NPT]
                    pot, pdn = ps_ot[kvh], ps_dn[kvh]
                    K.op(PE, lambda e: e.matmul(pot[:], lhsT=V[0:nk, kb, 128 * kvh:128 * kvh + 128], rhs=pt[0:nk, :],
                                                start=(kb == 0), stop=(kb == nb - 1)), [V, pt], [pot], selfsync=False)
                    K.op(PE, lambda e: e.matmul(pdn[:], lhsT=ones[0:nk, :], rhs=pt[0:nk, :],
                                                start=(kb == 0), stop=(kb == nb - 1)), [ones, pt], [pdn], selfsync=False)
                    if kb == nb - 1:
                        K.op(DVE, lambda e: e.reciprocal(out=rden[:], in_=pdn[:]), [pdn], [rden])
                        m_ = ma[kvh]
                        K.op(DVE, lambda e: e.tensor_tensor(out=m_[:], in0=pot[:], in1=rden[:], op=ALU.mult), [pot, rden], [m_])
                        K.dma(SP, MIXT, MIXT[:, 8 + 4 * kvh:12 + 4 * kvh, b * 128:(b + 1) * 128], m_,
                              m_[:].rearrange("p (h t) -> p h t", h=4), chk_dst=False)

                for j in range(min(LOOK, len(pairs))):
                    emit_S(j)
                every = max(1, len(pairs) // (len(late_units) - KC + 1))
                for j in range(len(pairs)):
                    if j + LOOK < len(pairs):
                        emit_S(j + LOOK)
                    emit_PV(j)
                    if late2 is not None and j % every == every - 1:
                        next(late2, None)
                if late2 is not None:
                    for _ in late2:
                        pass
                    cast_engs[0] = [0, 1, 2]
            K.barrier()

        NT = NOWN // 4
        with contextlib.ExitStack() as st:
            hTs = [K.sb(st, f"hT{dc}", [128, 512], F32, dma=True) for dc in range(KC)]
            actT = K.sb(st, "actT", [128, KC, 512], BF16, dma=True)
            aT = K.sb(st, "aT", [128, 64, 512], BF16)
            NWB = 3
            wb = [K.sb(st, f"wb{i}", [128, 8192], BF16, dma=True) for i in range(NWB)]
            rt = K.sb(st, "rt", [128, 512], F32)
            rstd = K.sb(st, "rstd", [128, 512], F32)
            rl = [K.sb(st, f"rl{i}", [128, 512], F32) for i in range(2)]
            pm = [K.ps(st, f"pm{j}", [128, 512]) for j in range(4)]
            ps_stat = K.ps(st, "pstat", [128, 512])
            slabs = []
            for t in range(NT):
                for g in range(4):
                    slabs.append((Wouts, Wouts[g].rearrange("p k j -> p (k j)")))
                for g in range(16):
                    slabs.append((Wups, Wups[g].rearrange("p k j -> p (k j)")))
                for g in range(16):
                    slabs.append((Wdowns, Wdowns[g].rearrange("p f j -> p (f j)")))
            wi = [0]

            def issue_slab():
                i = wi[0]
                if i < len(slabs):
                    K.dma(SP, wb[i % NWB], wb[i % NWB][:], slabs[i][0], slabs[i][1])
                wi[0] += 1

            used = [0]

            def next_slab():
                i = used[0]
                used[0] += 1
                return wb[i % NWB]

            for _ in range(NWB - 1):
                issue_slab()
            cnt = 0
            for t in range(NT):
                tok = slice(t * 512, (t + 1) * 512)
                if t == 0:
                    K.dma(SP, actT, actT[:], MIXT, MIXT[:, :, tok])
                for dc in range(KC):
                    K.dma(SP, hTs[dc], hTs[dc][:].rearrange("p (b t) -> p b t", b=4), xown,
                          xown[4 * t:4 * t + 4, :, dc, :].rearrange("b p t -> p b t"))
                for g in range(4):
                    issue_slab()
                    w = next_slab()
                    wv = w[:].rearrange("p (k j) -> p k j", k=KC)
                    for dd in range(4):
                        dc = 4 * g + dd
                        p_ = pm[cnt % 4]
                        cnt += 1
                        for kc in range(KC):
                            K.op(PE, lambda e: e.matmul(p_[:], lhsT=wv[:, kc, 128 * dd:128 * dd + 128], rhs=actT[:, kc, :],
                                                        start=(kc == 0), stop=(kc == KC - 1)), [w, actT], [p_], selfsync=False)
                        K.op(DVE, lambda e: e.tensor_tensor(out=hTs[dc][:], in0=p_[:], in1=hTs[dc][:], op=ALU.add),
                             [p_, hTs[dc]], [hTs[dc]])
                for dc in range(KC):
                    K.op(ACT, lambda e: e.activation(out=actT[:, dc, :], in_=hTs[dc][:], func=AF.Square), [hTs[dc]], [actT])
                for kc in range(KC):
                    K.op(PE, lambda e: e.matmul(ps_stat[:], lhsT=ones[:], rhs=actT[:, kc, :], start=(kc == 0),
                                                stop=(kc == KC - 1)), [ones, actT], [ps_stat], selfsync=False)
                rstd_from(None, ps_stat, ps_stat[:], 1.0 / D, rt, rt[:], rstd, rstd[:])
                for dc in range(KC):
                    K.op(POOL if dc % 3 == 2 else DVE, lambda e: e.tensor_tensor(out=actT[:, dc, :], in0=hTs[dc][:], in1=rstd[:],
                                                                                  op=ALU.mult), [hTs[dc], rstd], [actT])
                for g in range(16):
                    issue_slab()
                    w = next_slab()
                    wv = w[:].rearrange("p (k j) -> p k j", k=KC)
                    for ff in range(4):
                        fc = 4 * g + ff
                        p_ = pm[cnt % 4]
                        r_ = rl[cnt % 2]
                        cnt += 1
                        for kc in range(KC):
                            K.op(PE, lambda e: e.matmul(p_[:], lhsT=wv[:, kc, 128 * ff:128 * ff + 128], rhs=actT[:, kc, :],
                                                        start=(kc == 0), stop=(kc == KC - 1)), [w, actT], [p_], selfsync=False)
                        K.op(ACT, lambda e: e.activation(out=r_[:], in_=p_[:], func=AF.Relu), [p_], [r_])
                        K.op(POOL if fc % 2 == 0 else DVE, lambda e: e.tensor_tensor(out=aT[:, fc, :], in0=r_[:], in1=r_[:],
                                                                                      op=ALU.mult), [r_], [aT])
                for dc in range(16):
                    issue_slab()
                    w = next_slab()
                    wv = w[:].rearrange("p (f j) -> p f j", f=64)
                    p_ = pm[cnt % 4]
                    cnt += 1
                    for fc in range(64):
                        K.op(PE, lambda e: e.matmul(p_[:], lhsT=wv[:, fc, :], rhs=aT[:, fc, :], start=(fc == 0),
                                                    stop=(fc == 63)), [w, aT], [p_], selfsync=False)
                    K.op(DVE, lambda e: e.tensor_tensor(out=hTs[dc][:], in0=p_[:], in1=hTs[dc][:], op=ALU.add),
                         [p_, hTs[dc]], [hTs[dc]])
                for dc in range(KC):
                    K.op(ACT, lambda e: e.activation(out=actT[:, dc, :], in_=hTs[dc][:], func=AF.Square), [hTs[dc]], [actT])
                for kc in range(KC):
                    K.op(PE, lambda e: e.matmul(ps_stat[:], lhsT=ones[:], rhs=actT[:, kc, :], start=(kc == 0),
                                                stop=(kc == KC - 1)), [ones, actT], [ps_stat], selfsync=False)
                rstd_from(None, ps_stat, ps_stat[:], 1.0 / D, rt, rt[:], rstd, rstd[:])
                if t + 1 < NT:
                    K.dma(SP, actT, actT[:], MIXT, MIXT[:, :, (t + 1) * 512:(t + 2) * 512])
                gfc = C("gfc")
                for dc in range(16):
                    K.op(DVE, lambda e: e.scalar_tensor_tensor(
                        out=hTs[dc][:], in0=hTs[dc][:], scalar=gfc[:, dc:dc + 1], in1=rstd[:], op0=ALU.mult,
                        op1=ALU.mult), [hTs[dc], cst, rstd], [hTs[dc]])
                    K.dma(SP, yT, yT[:, dc, tok], hTs[dc], hTs[dc][:], chk_dst=False)
            K.barrier([SP])
    K.es.close()
    return nc


N_META = 16
GRID_W = 64
ROPE_THETA = 10000.0


def _blocks_T(h):
    nb = h.shape[0] // 128
    return np.ascontiguousarray(h.reshape(nb, 128, KC, 128).transpose(0, 3, 2, 1))


def _consts(q_g, k_g, lgf, lgb, g1, g2, gf):
    c = np.zeros((128, NCST), np.float32)

    def put(name, arr):
        o, w = C_OFF[name]
        c[:, o:o + w] = arr

    idx = np.arange(128, dtype=np.float32)
    dmat = idx[None, :] - idx[:, None]
    put("ident", np.eye(128, dtype=np.float32))
    put("dp", np.maximum(dmat, 0.0))
    put("dn", np.maximum(-dmat, 0.0))
    put("cp1", np.broadcast_to(idx[None, :] + 1.0, (128, 128)))
    put("c128m", np.broadcast_to(128.0 - idx[None, :], (128, 128)))
    rfreq = (np.float32(ROPE_THETA) ** (-np.linspace(0.0, 1.0, 64, dtype=np.float32))).astype(np.float32)
    afreq = (np.float32(ROPE_THETA) ** (-np.arange(32, dtype=np.float32) / np.float32(32))).astype(np.float32)
    put("rfreq", np.broadcast_to(rfreq[None], (128, 64)))
    put("afreq", np.broadcast_to(afreq[None], (128, 32)))
    put("g1c", g1.reshape(KC, 128).T)
    put("g2c", g2.reshape(KC, 128).T)
    put("gfc", gf.reshape(KC, 128).T)
    put("qg", np.broadcast_to(q_g[None], (128, 128)))
    put("kg", np.broadcast_to(k_g[None], (128, 128)))
    put("lgf", np.broadcast_to(lgf[None], (128, 4)))
    put("lgb", np.broadcast_to(lgb[None], (128, 4)))
    put("colc", idx[:, None])
    put("col127", 127.0 - idx[:, None])
    return c


def _seq_meta(nb_real):
    L = (nb_real + 1) * 128
    pos = np.zeros(L, np.float32)
    row = np.zeros(L, np.float32)
    col = np.zeros(L, np.float32)
    valid = np.zeros(L, np.float32)
    pos[0:16] = 112 + np.arange(16)
    valid[0:16] = 1
    j = np.arange(nb_real * 128)
    pos[128:] = 128 + j
    row[128:] = j // GRID_W
    col[128:] = j % GRID_W
    valid[128:] = 1
    return pos, row, col, valid


def prepare(cfg, x_prompt, x_sample, meta_tokens, ln1_g, w_in, q_norm_g, k_norm_g, ret_log_decay_fwd,
            ret_log_decay_bwd, w_out, ln2_g, w_up, w_down, final_norm_g):
    f32 = np.float32
    cst = _consts(np.asarray(q_norm_g[0], f32), np.asarray(k_norm_g[0], f32), np.asarray(ret_log_decay_fwd[0], f32),
                  np.asarray(ret_log_decay_bwd[0], f32), np.asarray(ln1_g[0], f32), np.asarray(ln2_g[0], f32),
                  np.asarray(final_norm_g, f32))
    mblk = np.zeros((128, D), f32)
    mblk[0:16] = meta_tokens
    mblkT = _blocks_T(mblk)
    xs_T = _blocks_T(np.asarray(x_sample[0], f32))
    xp_T = [_blocks_T(np.asarray(x_prompt[b], f32)) for b in range(x_prompt.shape[0])]
    shared = dict(cst=cst, w_in=np.ascontiguousarray(w_in[0], f32), w_out=np.ascontiguousarray(w_out[0], f32),
                  w_up=np.ascontiguousarray(w_up[0], f32), w_down=np.ascontiguousarray(w_down[0], f32))
    in_maps = []
    for c in range(8):
        pb, ph = c // 2, c % 2
        xb1 = np.concatenate([mblkT, xp_T[pb], mblkT, xs_T], axis=0)
        own_p = list(range(ph * cfg.OWN_P, (ph + 1) * cfg.OWN_P))
        own_s = list(range(c * cfg.OWN_S, (c + 1) * cfg.OWN_S))
        xown = np.concatenate([xp_T[pb][own_p], xs_T[own_s]], axis=0)
        metas, metao = [], []
        for (nbr, ownl) in ((cfg.NBP, own_p), (cfg.NBS, own_s)):
            pos, row, col, valid = _seq_meta(nbr)
            start = 128.0 + ownl[0] * 128
            end = 128.0 + (ownl[-1] + 1) * 128
            mf = ((pos < start) & (valid > 0)).astype(f32)
            mb = ((pos >= end) & (valid > 0)).astype(f32)
            df = np.where(mf > 0, start - 1.0 - pos, 0.0).astype(f32)
            db = np.where(mb > 0, pos - end, 0.0).astype(f32)
            m = np.stack([pos, row, col, df, mf, db, mb], axis=-1).reshape(nbr + 1, 128, 7)
            metas.append(m)
            osl = [1 + o for o in ownl]
            metao.append(m[osl][:, :, 0:3])
        meta1 = np.ascontiguousarray(np.concatenate(metas, axis=0).transpose(1, 0, 2))
        metao = np.ascontiguousarray(np.concatenate(metao, axis=0).transpose(1, 0, 2))
        d = dict(xb1=np.ascontiguousarray(xb1), xown=np.ascontiguousarray(xown), meta1=meta1, metao=metao)
        d.update(shared)
        in_maps.append(d)
    return in_maps


def assemble(cfg, results, nbatch):
    yp = np.zeros((nbatch, cfg.NBP * 128, D), np.float32)
    ys = np.zeros((1, cfg.NBS * 128, D), np.float32)
    for c in range(8):
        y = results[c]["yT"]
        y = y.transpose(2, 1, 0).reshape(cfg.NOWN * 128, D)
        pb, ph = c // 2, c % 2
        np_ = cfg.OWN_P * 128
        yp[pb, ph * np_:(ph + 1) * np_] = y[0:np_]
        ns_ = cfg.OWN_S * 128
        ys[0, c * ns_:(c + 1) * ns_] = y[np_:np_ + ns_]
    return yp, ys


_NC_CACHE = {}


def kernel(x_prompt, x_sample, meta_tokens, ln1_g, w_in, q_norm_g, k_norm_g, ret_log_decay_fwd,
           ret_log_decay_bwd, w_out, ln2_g, w_up, w_down, final_norm_g):
    cfg = Cfg(16, 128)
    args = [np.asarray(a) for a in (x_prompt, x_sample, meta_tokens, ln1_g, w_in, q_norm_g, k_norm_g,
                                    ret_log_decay_fwd, ret_log_decay_bwd, w_out, ln2_g, w_up, w_down, final_norm_g)]
    in_maps = prepare(cfg, *args)
    nc = build(cfg)
    res = run_bass_kernel_spmd(nc, in_maps, core_ids=list(range(8)))
    yp, ys = assemble(cfg, res.results, args[0].shape[0])
    return (yp, ys)
```

```python
import contextlib
import math
import numpy as np
import ml_dtypes
import concourse.bass as bass
import concourse.mybir as mybir
from concourse.bass_utils import run_bass_kernel_spmd

F32 = mybir.dt.float32
BF16 = mybir.dt.bfloat16
I32 = mybir.dt.int32
AF = mybir.ActivationFunctionType
ALU = mybir.AluOpType
AX = mybir.AxisListType

D = 2048
KC = 16
DFF = 8192
EPS = 1e-6
TWO_PI = 2.0 * math.pi
PI_LO = 3.1415925


class Buf:
    def __init__(self, t, name):
        self.t = t
        self.name = name
        self.w = None
        self.r = {}
        self.dsem = None
        self.dval = 0
        self.is_dram = False

    def __getitem__(self, key):
        return self.t[key]


class Eng:
    def __init__(self, name, eng, sem):
        self.name = name
        self.e = eng
        self.sem = sem
        self.count = 0
        self.waited = {}

    def wait(self, dep):
        sem, val = dep
        if self.waited.get(id(sem), 0) >= val:
            return
        self.waited[id(sem)] = val
        self.e.wait_ge(sem, val)


class Kern:
    def __init__(self, nc, n_dma_sems=96):
        self.nc = nc
        self.es = contextlib.ExitStack()
        self.uid = 0
        self.PE = self._mk("pe", nc.tensor)
        self.DVE = self._mk("dve", nc.vector)
        self.ACT = self._mk("act", nc.scalar)
        self.POOL = self._mk("pool", nc.gpsimd)
        self.SP = self._mk("sp", nc.sync)
        self.engs = [self.PE, self.DVE, self.ACT, self.POOL, self.SP]
        self.pool = [[self.es.enter_context(nc.semaphore(f"dq{i}")), 0] for i in range(n_dma_sems)]
        self.live = []

    def _mk(self, name, eng):
        return Eng(name, eng, self.es.enter_context(self.nc.semaphore("s_" + name)))

    def _give_sem(self, b):
        ent = self.pool.pop()
        b.dsem, b.dval = ent[0], ent[1]
        b._ent = ent
        self.live.append(b)

    def release(self, bufs):
        for b in bufs:
            if b.dsem is not None and b in self.live:
                self.live.remove(b)
                b._ent[1] = b.dval
                self.pool.append(b._ent)

    def sb(self, stack, name, shape, dt, dma=False):
        self.uid += 1
        t = stack.enter_context(self.nc.sbuf_tensor(f"{name}_{self.uid}", list(shape), dt))
        b = Buf(t, name)
        if dma:
            self._give_sem(b)
        stack.callback(self.release, [b])
        return b

    def ps(self, stack, name, shape, dt=F32):
        self.uid += 1
        t = stack.enter_context(self.nc.psum_tensor(f"{name}_{self.uid}", list(shape), dt))
        return Buf(t, name)

    def dram(self, name, shape, dt, kind="Internal"):
        t = self.nc.dram_tensor(name, list(shape), dt, kind=kind)
        b = Buf(t.ap(), name)
        b.is_dram = True
        return b

    def op(self, E, fn, reads=(), writes=(), selfsync=True):
        deps = []
        for b in reads:
            if b.w is not None:
                deps.append(b.w)
        for b in writes:
            if b.w is not None:
                deps.append(b.w)
            deps.extend(b.r.values())
        for d in deps:
            if d[0] is E.sem and not selfsync:
                continue
            E.wait(d)
        ins = fn(E.e)
        E.count += 1
        ins.then_inc(E.sem, 1)
        me = (E.sem, E.count)
        for b in reads:
            b.r[id(E.sem)] = me
        for b in writes:
            b.w = me
            b.r = {}
        return ins

    def dma(self, Q, dst, dst_ap, src, src_ap, chk_dst=True, **kw):
        deps = []
        if src.w is not None:
            deps.append(src.w)
        if chk_dst:
            if dst.w is not None:
                deps.append(dst.w)
            deps.extend(dst.r.values())
        sb_ = src if dst.is_dram else dst
        if sb_.dsem is None:
            self._give_sem(sb_)
        if sb_.dval > 0:
            deps.append((sb_.dsem, sb_.dval))
        for d in deps:
            Q.wait(d)
        ins = Q.e.dma_start(out=dst_ap, in_=src_ap, **kw)
        sb_.dval += 16
        ins.then_inc(sb_.dsem, 16)
        me = (sb_.dsem, sb_.dval)
        src.r[id(sb_.dsem)] = me
        dst.w = me
        if chk_dst:
            dst.r = {}
        return ins

    def barrier(self, engs=None):
        marks = [(E.sem, E.count) for E in self.engs if E.count > 0]
        marks += [(b.dsem, b.dval) for b in self.live if b.dval > 0]
        marks += [(ent[0], ent[1]) for ent in self.pool if ent[1] > 0]
        for E in (engs or self.engs):
            for m in marks:
                if m[0] is E.sem:
                    continue
                E.wait(m)


C_OFF = {}
_o = 0
for _n, _w in [("ident", 128), ("dp", 128), ("dn", 128), ("cp1", 128), ("c128m", 128), ("rfreq", 64),
               ("afreq", 32), ("g1c", 16), ("g2c", 16), ("gfc", 16), ("qg", 128), ("kg", 128), ("lgf", 4),
               ("lgb", 4), ("colc", 1), ("col127", 1)]:
    C_OFF[_n] = (_o, _w)
    _o += _w
NCST = _o


class Cfg:
    def __init__(self, nbp=16, nbs=128):
        self.NBP = nbp
        self.NBS = nbs
        self.OWN_P = nbp // 2
        self.OWN_S = nbs // 8
        self.NOWN = self.OWN_P + self.OWN_S
        self.NB1 = nbp + 1 + nbs + 1
        assert self.NOWN % 4 == 0
        self.seqs = [dict(name="p", nb=nbp + 1, b0=0, own0=0, nown=self.OWN_P),
                     dict(name="s", nb=nbs + 1, b0=nbp + 1, own0=self.OWN_P, nown=self.OWN_S)]


def build(cfg, debug=False):
    nc = bass.Bass("TRN2", target_bir_lowering=False)
    K = Kern(nc)
    NB1, NOWN = cfg.NB1, cfg.NOWN
    PE, DVE, ACT, POOL, SP = K.PE, K.DVE, K.ACT, K.POOL, K.SP
    skind = "ExternalOutput" if debug else "Internal"

    xb1 = K.dram("xb1", [NB1, 128, KC, 128], F32, "ExternalInput")
    xown = K.dram("xown", [NOWN, 128, KC, 128], F32, "ExternalInput")
    meta1 = K.dram("meta1", [128, NB1, 7], F32, "ExternalInput")
    metao = K.dram("metao", [128, NOWN, 3], F32, "ExternalInput")
    cst_d = K.dram("cst", [128, NCST], F32, "ExternalInput")
    w_in = K.dram("w_in", [D, 4608], F32, "ExternalInput")
    w_out = K.dram("w_out", [D, D], F32, "ExternalInput")
    w_up = K.dram("w_up", [D, DFF], F32, "ExternalInput")
    w_down = K.dram("w_down", [DFF, D], F32, "ExternalInput")
    yT = K.dram("yT", [128, KC, NOWN * 128], F32, "ExternalOutput")

    W1s = K.dram("W1s", [128, KC, 2048], BF16)
    Wown = K.dram("Wown", [8, 128, KC, 512], BF16)
    Wouts = K.dram("Wouts", [4, 128, KC, 512], BF16)
    Wups = K.dram("Wups", [16, 128, KC, 512], BF16)
    Wdowns = K.dram("Wdowns", [16, 128, 64, 128], BF16)
    KTs = [K.dram(f"KT_{s['name']}", [128, 2, s["nb"] * 128], BF16, skind) for s in cfg.seqs]
    Vs = [K.dram(f"V_{s['name']}", [128, s["nb"], 256], BF16, skind) for s in cfg.seqs]
    PROJ = K.dram("PROJ", [NOWN, 128, 4096], F32, skind)
    QT = K.dram("QT", [NOWN, 128, 8, 128], BF16, skind)
    RO = K.dram("RO", [NOWN, 128, 1024], F32, skind)
    RQB = K.dram("RQB", [NOWN, 128, 512], BF16, skind)
    KVB = K.dram("KVB", [NOWN, 128, 1024], F32, skind)
    MIXT = K.dram("MIXT", [128, KC, NOWN * 128], BF16, skind)
    SDBG = K.dram("SDBG", [4, 128, 1024], F32, skind) if debug else None

    with contextlib.ExitStack() as G:
        cst = K.sb(G, "cst", [128, NCST], F32, dma=True)
        K.dma(SP, cst, cst[:], cst_d, cst_d[:])

        def C(name, rows=128):
            o, w = C_OFF[name]
            return cst[0:rows, o:o + w]

        ident = K.sb(G, "ident", [128, 128], BF16)
        ones = K.sb(G, "ones", [128, 128], BF16)
        epsb = K.sb(G, "epsb", [128, 1], F32)
        negB = K.sb(G, "negB", [128, 1], F32)
        kdf = K.sb(G, "kdf", [128, 4], F32)
        kdb = K.sb(G, "kdb", [128, 4], F32)
        cdf = K.sb(G, "cdf", [128, 4], F32)
        cdb = K.sb(G, "cdb", [128, 4], F32)
        wfb = K.sb(G, "wfb", [128, NB1, 8], F32)
        m1 = K.sb(G, "m1", [128, NB1, 7], F32, dma=True)
        mo = K.sb(G, "mo", [128, NOWN, 3], F32, dma=True)
        Sst = [[K.sb(G, f"S{d}{s['name']}", [128, 4, 256], F32) for d in "fb"] for s in cfg.seqs]
        tmpc = K.sb(G, "tmpc", [128, max(NB1, 128)], F32)
        tmpd = K.sb(G, "tmpd", [128, max(NB1, 128)], F32)

        K.dma(SP, m1, m1[:], meta1, meta1[:])
        K.dma(SP, mo, mo[:], metao, metao[:])

        K.op(DVE, lambda e: e.tensor_copy(out=ident[:], in_=C("ident")), [cst], [ident])
        K.op(DVE, lambda e: e.memset(ones[:], 1.0), [], [ones])
        K.op(DVE, lambda e: e.memset(epsb[:], EPS), [], [epsb])
        for s in range(2):
            for d in range(2):
                K.op(POOL, lambda e: e.memset(Sst[s][d][:], 0.0), [], [Sst[s][d]])
        K.op(DVE, lambda e: e.tensor_reduce(out=tmpc[:, 0:1], in_=C("qg"), axis=AX.X, op=ALU.max,
                                            apply_absolute_value=True), [cst], [tmpc])
        K.op(DVE, lambda e: e.tensor_reduce(out=tmpc[:, 1:2], in_=C("kg"), axis=AX.X, op=ALU.max,
                                            apply_absolute_value=True), [cst], [tmpc])
        K.op(DVE, lambda e: e.scalar_tensor_tensor(out=negB[:], in0=tmpc[:, 0:1], scalar=-math.sqrt(128.0),
                                                   in1=tmpc[:, 1:2], op0=ALU.mult, op1=ALU.mult), [tmpc], [negB])
        lgf, lgb = C("lgf"), C("lgb")
        def make_decay_tables(st):
            MT = K.sb(st, "MT", [128, 4, 128], F32)
            qdf = K.sb(st, "qdf", [128, 4, 128], F32)
            qdb = K.sb(st, "qdb", [128, 4, 128], F32)
            for h in range(4):
                K.op(DVE, lambda e: e.tensor_scalar(out=tmpc[:, 0:128], in0=C("dp"), scalar1=lgf[:, h:h + 1],
                                                    scalar2=None, op0=ALU.mult), [cst], [tmpc])
                K.op(DVE, lambda e: e.scalar_tensor_tensor(out=tmpd[:, 0:128], in0=C("dn"), scalar=lgb[:, h:h + 1],
                                                           in1=tmpc[:, 0:128], op0=ALU.mult, op1=ALU.add),
                     [cst, tmpc], [tmpd])
                K.op(ACT, lambda e: e.activation(out=MT[:, h, :], in_=tmpd[:, 0:128], func=AF.Exp), [tmpd], [MT])
                K.op(ACT, lambda e: e.activation(out=qdf[:, h, :], in_=C("cp1"), func=AF.Exp, scale=lgf[:, h:h + 1]),
                     [cst], [qdf])
                K.op(ACT, lambda e: e.activation(out=qdb[:, h, :], in_=C("c128m"), func=AF.Exp, scale=lgb[:, h:h + 1]),
                     [cst], [qdb])
            return MT, qdf, qdb

        for h in range(4):
            K.op(ACT, lambda e: e.activation(out=kdf[:, h:h + 1], in_=C("col127"), func=AF.Exp,
                                             scale=lgf[:, h:h + 1]), [cst], [kdf])
            K.op(ACT, lambda e: e.activation(out=kdb[:, h:h + 1], in_=C("colc"), func=AF.Exp,
                                             scale=lgb[:, h:h + 1]), [cst], [kdb])
            for d, (lg, dcol, mcol) in enumerate([(lgf, 3, 4), (lgb, 5, 6)]):
                K.op(ACT, lambda e: e.activation(out=tmpc[:, 0:NB1], in_=m1[:, :, dcol], func=AF.Exp,
                                                 scale=lg[:, h:h + 1]), [m1, cst], [tmpc])
                K.op(DVE, lambda e: e.tensor_tensor(out=wfb[:, :, 4 * d + h], in0=tmpc[:, 0:NB1],
                                                    in1=m1[:, :, mcol], op=ALU.mult), [tmpc, m1], [wfb])
        K.op(ACT, lambda e: e.activation(out=cdf[:], in_=lgf, func=AF.Exp, scale=128.0), [cst], [cdf])
        K.op(ACT, lambda e: e.activation(out=cdb[:], in_=lgb, func=AF.Exp, scale=128.0), [cst], [cdb])

        def cs_tmps(st, Gn):
            return (K.sb(st, "cs_a", [128, Gn, 64], F32), K.sb(st, "cs_k", [128, Gn, 64], I32),
                    K.sb(st, "cs_f", [128, Gn, 64], F32))

        def build_cs(st, name, tmps, Gn, scale):
            ang, ki, kf = tmps
            cs = K.sb(st, name + "_cs", [128, Gn, 128], F32)

            def fill(specs_now, Gc):
                for (fr, pos, off) in specs_now:
                    nf = fr.shape[-1]
                    K.op(DVE, lambda e: e.tensor_tensor(
                        out=ang[:, 0:Gc, off:off + nf], in0=fr.unsqueeze(1).to_broadcast([128, Gc, nf]),
                        in1=pos.unsqueeze(2).to_broadcast([128, Gc, nf]), op=ALU.mult), [cst, m1, mo], [ang])
                K.op(DVE, lambda e: e.tensor_scalar(out=ki[:, 0:Gc, :], in0=ang[:, 0:Gc, :], scalar1=1.0 / TWO_PI,
                                                    scalar2=None, op0=ALU.mult), [ang], [ki])
                K.op(DVE, lambda e: e.tensor_copy(out=kf[:, 0:Gc, :], in_=ki[:, 0:Gc, :]), [ki], [kf])
                K.op(DVE, lambda e: e.scalar_tensor_tensor(out=ang[:, 0:Gc, :], in0=kf[:, 0:Gc, :], scalar=-TWO_PI,
                                                           in1=ang[:, 0:Gc, :], op0=ALU.mult, op1=ALU.add),
                     [kf, ang], [ang])
                K.op(DVE, lambda e: e.tensor_scalar(out=ang[:, 0:Gc, :], in0=ang[:, 0:Gc, :], scalar1=-PI_LO,
                                                    scalar2=PI_LO, op0=ALU.max, op1=ALU.min), [ang], [ang])
                K.op(ACT, lambda e: e.activation(out=cs[:, 0:Gc, 0:64], in_=ang[:, 0:Gc, :], func=AF.Sin,
                                                 scale=1.0 - 1e-6), [ang], [cs])
                K.op(ACT, lambda e: e.activation(out=kf[:, 0:Gc, :], in_=ang[:, 0:Gc, :], func=AF.Sin,
                                                 scale=0.5), [ang], [kf])
                K.op(DVE, lambda e: e.scalar_tensor_tensor(out=kf[:, 0:Gc, :], in0=kf[:, 0:Gc, :], scalar=-2.0,
                                                           in1=kf[:, 0:Gc, :], op0=ALU.mult, op1=ALU.mult),
                     [kf], [kf])
                if scale == 1.0:
                    K.op(DVE, lambda e: e.tensor_scalar(out=cs[:, 0:Gc, 64:128], in0=kf[:, 0:Gc, :], scalar1=1.0,
                                                        scalar2=None, op0=ALU.add), [kf], [cs])
                else:
                    K.op(DVE, lambda e: e.tensor_scalar(out=cs[:, 0:Gc, 64:128], in0=kf[:, 0:Gc, :], scalar1=1.0,
                                                        scalar2=scale, op0=ALU.add, op1=ALU.mult), [kf], [cs])
                    K.op(DVE, lambda e: e.tensor_scalar(out=cs[:, 0:Gc, 0:64], in0=cs[:, 0:Gc, 0:64],
                                                        scalar1=scale, scalar2=None, op0=ALU.mult), [cs], [cs])
            return cs, fill

        def rope(E1, E2, xb, x_ap, nh, cs, g, ob, o_ap, tm):
            xv = x_ap.rearrange("p (h i t) -> p h i t", h=nh, t=2)
            ov = o_ap.rearrange("p (h i t) -> p h i t", h=nh, t=2)
            x1, x2 = xv[:, :, :, 0], xv[:, :, :, 1]
            sb_ = cs[:, g, 0:64].unsqueeze(1).to_broadcast([128, nh, 64])
            cb_ = cs[:, g, 64:128].unsqueeze(1).to_broadcast([128, nh, 64])
            t = [tm[i][:, 0:nh * 64].rearrange("p (h i) -> p h i", h=nh) for i in range(4)]
            K.op(E1, lambda e: e.tensor_tensor(out=t[0], in0=x1, in1=cb_, op=ALU.mult), [xb, cs], [tm[0]])
            K.op(E1, lambda e: e.tensor_tensor(out=t[1], in0=x2, in1=sb_, op=ALU.mult), [xb, cs], [tm[1]])
            K.op(E1, lambda e: e.tensor_tensor(out=ov[:, :, :, 0], in0=t[0], in1=t[1], op=ALU.subtract),
                 [tm[0], tm[1]], [ob])
            K.op(E2, lambda e: e.tensor_tensor(out=t[2], in0=x1, in1=sb_, op=ALU.mult), [xb, cs], [tm[2]])
            K.op(E2, lambda e: e.tensor_tensor(out=t[3], in0=x2, in1=cb_, op=ALU.mult), [xb, cs], [tm[3]])
            K.op(E2, lambda e: e.tensor_tensor(out=ov[:, :, :, 1], in0=t[2], in1=t[3], op=ALU.add),
                 [tm[2], tm[3]], [ob])

        def rstd_from(E_sq, ssq_b, ssq_ap, inv_n, tmp_b, tmp_ap, out_b, out_ap):
            rows = ssq_ap.shape[0]
            K.op(ACT, lambda e: e.activation(out=tmp_ap, in_=ssq_ap, func=AF.Sqrt, scale=inv_n,
                                             bias=epsb[0:rows, 0:1]), [ssq_b, epsb], [tmp_b])
            K.op(DVE, lambda e: e.reciprocal(out=out_ap, in_=tmp_ap), [tmp_b], [out_b])

        def norm_block(xTb, sqb, uTb, ps_stat, rt, rstd, ntok):
            K.op(ACT, lambda e: e.activation(out=sqb[:, :, 0:ntok], in_=xTb[:, :, 0:ntok], func=AF.Square),
                 [xTb], [sqb])
            for kc in range(KC):
                K.op(PE, lambda e: e.matmul(ps_stat[:, 0:ntok], lhsT=ones[:], rhs=sqb[:, kc, 0:ntok],
                                            start=(kc == 0), stop=(kc == KC - 1)), [ones, sqb], [ps_stat],
                     selfsync=False)
            rstd_from(None, ps_stat, ps_stat[:, 0:ntok], 1.0 / D, rt, rt[:, 0:ntok], rstd, rstd[:, 0:ntok])
            h = KC // 2
            for (E, a, b) in ((DVE, 0, h), (POOL, h, KC)):
                K.op(E, lambda e: e.tensor_tensor(
                    out=uTb[:, a:b, 0:ntok], in0=xTb[:, a:b, 0:ntok],
                    in1=rstd[:, 0:ntok].unsqueeze(1).to_broadcast([128, b - a, ntok]), op=ALU.mult),
                    [xTb, rstd], [uTb])

        cast_rr = [0]

        cast_engs = [[0, 1, 2]]

        def cast(srcb, src_ap, dstb, dst_ap, gcol=None):
            i = cast_engs[0][cast_rr[0] % len(cast_engs[0])]
            cast_rr[0] += 1
            if i == 0:
                if gcol is None:
                    K.op(DVE, lambda e: e.tensor_copy(out=dst_ap, in_=src_ap), [srcb], [dstb])
                else:
                    K.op(DVE, lambda e: e.tensor_scalar(out=dst_ap, in0=src_ap, scalar1=gcol, scalar2=None,
                                                        op0=ALU.mult), [srcb, cst], [dstb])
            elif i == 1:
                K.op(ACT, lambda e: e.activation(out=dst_ap, in_=src_ap, func=AF.Copy,
                                                 scale=(1.0 if gcol is None else gcol)), [srcb, cst], [dstb])
            else:
                if gcol is None:
                    K.op(POOL, lambda e: e.tensor_copy(out=dst_ap, in_=src_ap), [srcb], [dstb])
                else:
                    K.op(POOL, lambda e: e.tensor_scalar(out=dst_ap, in0=src_ap, scalar1=gcol, scalar2=None,
                                                         op0=ALU.mult), [srcb, cst], [dstb])

        with contextlib.ExitStack() as st:
            NBUF = 3
            stg = [K.sb(st, f"stg{i}", [128, 2048], F32, dma=True) for i in range(NBUF)]
            wbf = [K.sb(st, f"wbf{i}", [128, 2048], BF16) for i in range(NBUF)]
            g1c, g2c = C("g1c"), C("g2c")
            units = []
            for kc in range(KC):
                rows = slice(kc * 128, (kc + 1) * 128)
                units.append((w_in, w_in[rows, 0:2048], 2048, g1c[:, kc:kc + 1],
                              [(Wown, Wown[0:4, :, kc, :].rearrange("g p j -> p g j"), 0, 2048, 4),
                               (W1s, W1s[:, kc, 512:2048], 512, 1536, 0)]))
                units.append((w_in, w_in[rows, 2048:4096], 2048, g1c[:, kc:kc + 1],
                              [(Wown, Wown[4:8, :, kc, :].rearrange("g p j -> p g j"), 0, 2048, 4)]))
                units.append((w_in, w_in[rows, 4096:4608], 512, g1c[:, kc:kc + 1],
                              [(W1s, W1s[:, kc, 0:512], 0, 512, 0)]))
            for kc in range(KC):
                rows = slice(kc * 128, (kc + 1) * 128)
                units.append((w_out, w_out[rows, :], 2048, None,
                              [(Wouts, Wouts[:, :, kc, :].rearrange("g p j -> p g j"), 0, 2048, 4)]))
            for kc in range(KC):
                rows = slice(kc * 128, (kc + 1) * 128)
                for q in range(4):
                    units.append((w_up, w_up[rows, q * 2048:(q + 1) * 2048], 2048, g2c[:, kc:kc + 1],
                                  [(Wups, Wups[4 * q:4 * q + 4, :, kc, :].rearrange("g p j -> p g j"), 0, 2048, 4)]))
            for fc in range(64):
                rows = slice(fc * 128, (fc + 1) * 128)
                units.append((w_down, w_down[rows, :], 2048, None,
                              [(Wdowns, Wdowns[:, :, fc, :].rearrange("g p j -> p g j"), 0, 2048, 16)]))

            def run_units(ulist, stg, wbf, Q):
                nb_ = len(stg)

                def ld(i):
                    src, sap, n, _, _ = ulist[i]
                    K.dma(SP, stg[i % nb_], stg[i % nb_][:, 0:n], src, sap)

                for i in range(min(nb_ - 1, len(ulist))):
                    ld(i)
                for i, (src, sap, n, gcol, stores) in enumerate(ulist):
                    if i + nb_ - 1 < len(ulist):
                        ld(i + nb_ - 1)
                    s_, w_ = stg[i % nb_], wbf[i % nb_]
                    cast(s_, s_[:, 0:n], w_, w_[:, 0:n], gcol)
                    for (dstb, dap, off, nn, ng) in stores:
                        sap2 = w_[:, off:off + nn]
                        if ng:
                            sap2 = sap2.rearrange("p (g j) -> p g j", g=ng)
                        K.dma(Q, dstb, dap, w_, sap2, chk_dst=False)
                    yield i

            n_in = 3 * KC
            for _ in run_units(units[:n_in], stg, wbf, ACT):
                pass
            late_units = units[n_in:]
        K.barrier()

        GT = 4
        with contextlib.ExitStack() as st:
            W1 = K.sb(st, "W1", [128, KC, 2048], BF16, dma=True)
            stg1 = [K.sb(st, f"stgl{i}", [128, 2048], F32, dma=True) for i in range(2)]
            wbf1 = [K.sb(st, f"wbfl{i}", [128, 2048], BF16) for i in range(2)]
            late = run_units(late_units[:KC], stg1, wbf1, SP)
            for q in range(4):
                K.dma(SP, W1, W1[:, 4 * q:4 * q + 4, :], W1s, W1s[:, 4 * q:4 * q + 4, :], chk_dst=(q == 0))
            NX = 3
            xTs = [K.sb(st, f"xT{i}", [128, KC, 128], F32, dma=True) for i in range(NX)]
            sqbs = [K.sb(st, f"sq{i}", [128, KC, 128], BF16) for i in range(2)]
            uTs = [K.sb(st, f"uT{i}", [128, KC, 128], BF16) for i in range(2)]
            rt = K.sb(st, "rt", [128, 128], F32)
            rstd = K.sb(st, "rstd", [128, 128], F32)
            pp = [K.ps(st, f"pp{j}", [128, 512]) for j in range(4)]
            ps_stat = K.ps(st, "pstat", [128, 512])
            ps_tr = K.ps(st, "ptr", [128, 1024], BF16)
            ps_kv = [K.ps(st, f"pkv{j}", [128, 512]) for j in range(2)]
            cst_ = cs_tmps(st, GT)
            cs_r = [build_cs(st, f"csr{i}", cst_, GT, 128.0 ** -0.5) for i in range(2)]
            cs_a = [build_cs(st, f"csa{i}", cst_, GT, 1.0) for i in range(2)]
            rk32 = K.sb(st, "rk32", [128, 512], F32)
            ak32 = K.sb(st, "ak32", [128, 256], F32)
            akn = K.sb(st, "akn", [128, 256], F32)
            tm = [K.sb(st, f"tm{i}", [128, 256], F32) for i in range(4)]
            tm2 = [K.sb(st, f"tn{i}", [128, 256], F32) for i in range(4)]
            rk_r = K.sb(st, "rk_r", [128, 512], BF16)
            ND = 3
            kws = [[K.sb(st, f"kw{j}{i}", [128, 512], BF16) for i in range(2)] for j in range(ND)]
            rv_bfs = [K.sb(st, f"rv_bf{j}", [128, 1024], BF16) for j in range(ND)]
            ak_rs = [K.sb(st, f"ak_r{j}", [128, 256], BF16) for j in range(ND)]
            av_bf = [K.sb(st, f"av_bf{i}", [128, 256], BF16) for i in range(2)]
            ktb = [K.sb(st, f"ktb{i}", [128, 2, 128], BF16) for i in range(2)]
            sk = K.sb(st, "sk", [128, 8], F32)
            junk = K.sb(st, "junk", [128, 128], F32)

            def load_x(i):
                K.dma(SP, xTs[i % NX], xTs[i % NX][:], xb1, xb1[i])

            blocks = []
            for si, s in enumerate(cfg.seqs):
                for bl in range(s["nb"]):
                    blocks.append((s["b0"] + bl, si, bl))

            def stage_sq(i):
                xTb, sqb = xTs[i % NX], sqbs[i % 2]
                K.op(ACT, lambda e: e.activation(out=sqb[:], in_=xTb[:], func=AF.Square), [xTb], [sqb])

            def stage_stat(i):
                sqb = sqbs[i % 2]
                for kc in range(KC):
                    K.op(PE, lambda e: e.matmul(ps_stat[:, 0:128], lhsT=ones[:], rhs=sqb[:, kc, :],
                                                start=(kc == 0), stop=(kc == KC - 1)), [ones, sqb], [ps_stat],
                         selfsync=False)
                K.op(ACT, lambda e: e.activation(out=rt[:], in_=ps_stat[:, 0:128], func=AF.Sqrt, scale=1.0 / D,
                                                 bias=epsb[:, 0:1]), [ps_stat, epsb], [rt])

            def stage_uT(i):
                xTb, uTb = xTs[i % NX], uTs[i % 2]
                K.op(DVE, lambda e: e.reciprocal(out=rstd[:], in_=rt[:]), [rt], [rstd])
                h = KC // 2
                for (E, a_, b_) in ((DVE, 0, h), (POOL, h, KC)):
                    K.op(E, lambda e: e.tensor_tensor(
                        out=uTb[:, a_:b_, :], in0=xTb[:, a_:b_, :],
                        in1=rstd[:].unsqueeze(1).to_broadcast([128, b_ - a_, 128]), op=ALU.mult),
                        [xTb, rstd], [uTb])

            def stage_tables(i):
                g, gi = i % GT, (i // GT) % 2
                if g == 0:
                    Gc = min(GT, NB1 - i)
                    cs_r[gi][1]([(C("rfreq"), m1[:, i:i + Gc, 0], 0)], Gc)
                    cs_a[gi][1]([(C("afreq"), m1[:, i:i + Gc, 1], 0), (C("afreq"), m1[:, i:i + Gc, 2], 32)], Gc)

            def stage_proj(i, banks):
                uTb = uTs[i % 2]
                for j in banks:
                    for kc in range(KC):
                        K.op(PE, lambda e: e.matmul(pp[j][:], lhsT=uTb[:, kc, :], rhs=W1[:, kc, 512 * j:512 * j + 512],
                                                    start=(kc == 0), stop=(kc == KC - 1)), [uTb, W1], [pp[j]],
                             selfsync=False)

            def stage_post(i, si, bl):
                g, gi = i % GT, (i // GT) % 2
                csr, csa = cs_r[gi][0], cs_a[gi][0]
                kw, rv_bf, ak_r = kws[i % ND], rv_bfs[i % ND], ak_rs[i % ND]
                K.op(ACT, lambda e: e.activation(out=ak32[:], in_=pp[0][:, 0:256], func=AF.Copy), [pp[0]], [ak32])
                avb = av_bf[i % 2]
                K.op(ACT, lambda e: e.activation(out=avb[:], in_=pp[0][:, 256:512], func=AF.Copy), [pp[0]], [avb])
                K.dma(SP, Vs[si], Vs[si][:, bl, :], avb, avb[:], chk_dst=False)
                for h in range(2):
                    K.op(ACT, lambda e: e.activation(out=junk[:], in_=ak32[:, 128 * h:128 * h + 128], func=AF.Square,
                                                     accum_out=sk[:, h:h + 1]), [ak32], [junk, sk])
                rstd_from(None, sk, sk[:, 0:2], 1.0 / 128, sk, sk[:, 2:4], sk, sk[:, 4:6])
                for h in range(2):
                    K.op(DVE, lambda e: e.scalar_tensor_tensor(out=akn[:, 128 * h:128 * h + 128],
                                                               in0=ak32[:, 128 * h:128 * h + 128],
                                                               scalar=sk[:, 4 + h:5 + h], in1=C("kg"),
                                                               op0=ALU.mult, op1=ALU.mult), [ak32, sk, cst], [akn])
                rope(POOL, DVE, akn, akn[:], 2, csa, g, ak_r, ak_r[:], tm2)
                K.op(ACT, lambda e: e.activation(out=rk32[:], in_=pp[1][:], func=AF.Copy), [pp[1]], [rk32])
                K.op(ACT, lambda e: e.activation(out=rv_bf[:, 0:512], in_=pp[2][:], func=AF.Copy), [pp[2]], [rv_bf])
                K.op(ACT, lambda e: e.activation(out=rv_bf[:, 512:1024], in_=pp[3][:], func=AF.Copy), [pp[3]], [rv_bf])
                rope(DVE, POOL, rk32, rk32[:], 4, csr, g, rk_r, rk_r[:], tm)
                for d in range(2):
                    K.op(DVE if d == 0 else POOL, lambda e: e.tensor_tensor(
                        out=kw[d][:].rearrange("p (h x) -> p h x", h=4),
                        in0=rk_r[:].rearrange("p (h x) -> p h x", h=4),
                        in1=wfb[:, i, 4 * d:4 * d + 4].unsqueeze(2).to_broadcast([128, 4, 128]), op=ALU.mult),
                        [rk_r, wfb], [kw[d]])

            def stage_def_tr(i, si, bl):
                ak_r = ak_rs[i % ND]
                for h in range(2):
                    K.op(PE, lambda e: e.transpose(out=ps_tr[:, 128 * h:128 * h + 128], in_=ak_r[:, 128 * h:128 * h + 128],
                                                   identity=ident[:]), [ak_r, ident], [ps_tr], selfsync=False)
                kt = ktb[i % 2]
                K.op(DVE, lambda e: e.tensor_copy(out=kt[:].rearrange("p h t -> p (h t)"), in_=ps_tr[:, 0:256]),
                     [ps_tr], [kt])
                K.dma(SP, KTs[si], KTs[si][:, :, bl * 128:(bl + 1) * 128], kt, kt[:], chk_dst=False)

            def stage_def_state(i, si, bl, d):
                Sd = Sst[si][d]
                kw, rv_bf = kws[i % ND], rv_bfs[i % ND]
                for hp in range(2):
                    pk = ps_kv[hp]
                    for hh in range(2):
                        h = 2 * hp + hh
                        K.op(PE, lambda e: e.matmul(pk[:, 256 * hh:256 * hh + 256],
                                                    lhsT=kw[d][:, 128 * h:128 * h + 128],
                                                    rhs=rv_bf[:, 256 * h:256 * h + 256], start=True, stop=True),
                             [kw[d], rv_bf], [pk], selfsync=False)
                for hp in range(2):
                    pk = ps_kv[hp]
                    sv = Sd[:, 2 * hp:2 * hp + 2, :].rearrange("p h v -> p (h v)")
                    K.op(DVE, lambda e: e.tensor_tensor(out=sv, in0=sv, in1=pk[:], op=ALU.add), [Sd, pk], [Sd])

            for i0 in range(min(NX, NB1)):
                load_x(i0)
            stage_sq(0)
            if NB1 > 1:
                stage_sq(1)
            stage_stat(0)
            stage_uT(0)
            for idx, (i, si, bl) in enumerate(blocks):
                prev = blocks[idx - 2] if idx > 1 else None
                if i + 1 < NB1:
                    stage_stat(i + 1)
                if prev:
                    stage_def_tr(*prev)
                    stage_def_state(*prev, 0)
                if i + 2 < NB1:
                    stage_sq(i + 2)
                stage_tables(i)
                stage_proj(i, [0])
                if prev:
                    stage_def_state(*prev, 1)
                if i + 1 < NB1:
                    stage_uT(i + 1)
                if i + 3 < NB1:
                    load_x(i + 3)
                stage_proj(i, [1, 2, 3])
                stage_post(i, si, bl)
                next(late, None)
            for bk in blocks[-2:]:
                stage_def_tr(*bk)
                stage_def_state(*bk, 0)
                stage_def_state(*bk, 1)
            for _ in late:
                pass
            if debug:
                for si in range(2):
                    for d in range(2):
                        K.dma(SP, SDBG, SDBG[2 * si + d], Sst[si][d], Sst[si][d][:].rearrange("p h v -> p (h v)"),
                              chk_dst=False)
        K.barrier()

        with contextlib.ExitStack() as st:
            uTo = [K.sb(st, f"uTo{b}", [128, KC, 128], BF16) for b in range(NOWN)]
            xTs = [K.sb(st, f"xT{i}", [128, KC, 128], F32, dma=True) for i in range(2)]
            sqb = K.sb(st, "sq", [128, KC, 128], BF16)
            rt = K.sb(st, "rt", [128, 128], F32)
            rstd = K.sb(st, "rstd", [128, 128], F32)
            ps_stat = K.ps(st, "pstat", [128, 512])
            pp = [K.ps(st, f"pp{j}", [128, 512]) for j in range(4)]
            wg = [K.sb(st, f"wg{i}", [128, KC, 512], BF16, dma=True) for i in range(2)]
            stg = [K.sb(st, f"pstg{i}", [128, 512], F32) for i in range(4)]
            K.dma(SP, wg[0], wg[0][:], Wown, Wown[0])
            K.dma(SP, xTs[0], xTs[0][:], xown, xown[0])
            def own_norm(b):
                if b + 1 < NOWN:
                    K.dma(SP, xTs[(b + 1) % 2], xTs[(b + 1) % 2][:], xown, xown[b + 1])
                norm_block(xTs[b % 2], sqb, uTo[b], ps_stat, rt, rstd, 128)

            own_norm(0)
            n = 0
            for cg in range(8):
                if cg + 1 < 8:
                    K.dma(SP, wg[(cg + 1) % 2], wg[(cg + 1) % 2][:], Wown, Wown[cg + 1])
                w = wg[cg % 2]
                for b in range(NOWN):
                    if cg == 0 and b + 1 < NOWN:
                        own_norm(b + 1)
                    p_, s_ = pp[n % 4], stg[n % 4]
                    for kc in range(KC):
                        K.op(PE, lambda e: e.matmul(p_[:], lhsT=uTo[b][:, kc, :], rhs=w[:, kc, :], start=(kc == 0),
                                                    stop=(kc == KC - 1)), [uTo[b], w], [p_], selfsync=False)
                    if n % 2 == 0:
                        K.op(ACT, lambda e: e.activation(out=s_[:], in_=p_[:], func=AF.Copy), [p_], [s_])
                    else:
                        K.op(DVE, lambda e: e.tensor_copy(out=s_[:], in_=p_[:]), [p_], [s_])
                    K.dma(SP, PROJ, PROJ[b, :, 512 * cg:512 * cg + 512], s_, s_[:], chk_dst=False)
                    n += 1
        K.barrier()

        for si, s in enumerate(cfg.seqs):
            Sf, Sb = Sst[si]
            own = list(range(s["own0"], s["own0"] + s["nown"]))
            no = len(own)
            with contextlib.ExitStack() as st:
                pr = [K.sb(st, f"pr{i}", [128, 4096], F32, dma=True) for i in range(2)]
                MT, qdf, qdb = make_decay_tables(st)
                cst_ = cs_tmps(st, no)
                csr1 = build_cs(st, "csr1", cst_, no, 1.0)
                csrs = build_cs(st, "csrs", cst_, no, 128.0 ** -0.5)
                csa = build_cs(st, "csa", cst_, no, 128.0 ** -0.5)
                o0 = own[0]
                csr1[1]([(C("rfreq"), mo[:, o0:o0 + no, 0], 0)], no)
                csrs[1]([(C("rfreq"), mo[:, o0:o0 + no, 0], 0)], no)
                csa[1]([(C("afreq"), mo[:, o0:o0 + no, 1], 0), (C("afreq"), mo[:, o0:o0 + no, 2], 32)], no)
                tm = [K.sb(st, f"tm{i}", [128, 256], F32) for i in range(4)]
                tm2 = [K.sb(st, f"tn{i}", [128, 256], F32) for i in range(4)]
                tm3 = [K.sb(st, f"to{i}", [128, 512], F32) for i in range(4)]
                rq_r = K.sb(st, "rq_r", [128, 512], BF16)
                rk_r = K.sb(st, "rk_r", [128, 512], BF16)
                rv_bf = K.sb(st, "rv_bf", [128, 1024], BF16)
                aqsq = K.sb(st, "aqsq", [128, 1024], F32)
                aqn = K.sb(st, "aqn", [128, 1024], F32)
                aq_r = K.sb(st, "aq_r", [128, 1024], BF16)
                sk = K.sb(st, "sk", [128, 24], F32)
                ps_t1 = K.ps(st, "pt1", [128, 1024], BF16)
                ps_t2 = K.ps(st, "pt2", [128, 1024], BF16)
                ps_p = K.ps(st, "ppt", [128, 512])
                ps_o = [K.ps(st, f"po{j}", [128, 512]) for j in range(2)]
                ps_kv = [K.ps(st, f"pkv{j}", [128, 512]) for j in range(2)]
                rqT = K.sb(st, "rqT", [128, 512], BF16)
                rqfT = K.sb(st, "rqfT", [128, 512], BF16)
                rqbT = [K.sb(st, f"rqbT{i}", [128, 512], BF16) for i in range(2)]
                rkT = K.sb(st, "rkT", [128, 512], BF16)
                aqT = [K.sb(st, f"aqT{i}", [128, 1024], BF16) for i in range(2)]
                PT = K.sb(st, "PT", [128, 512], BF16)
                kw = [K.sb(st, f"kw{i}", [128, 512], BF16) for i in range(2)]
                Sf_bf = K.sb(st, "Sf_bf", [128, 1024], BF16)
                ro_s = [K.sb(st, f"ro_s{i}", [128, 1024], F32) for i in range(2)]
                kvb_s = [K.sb(st, f"kvb_s{i}", [128, 1024], F32) for i in range(2)]
                K.op(DVE, lambda e: e.tensor_copy(out=Sf_bf[:], in_=Sf[:].rearrange("p h v -> p (h v)")), [Sf], [Sf_bf])
                K.dma(SP, pr[0], pr[0][:], PROJ, PROJ[own[0]])
                for n, b in enumerate(own):
                    if n + 1 < no:
                        K.dma(SP, pr[(n + 1) % 2], pr[(n + 1) % 2][:], PROJ, PROJ[own[n + 1]])
                    p_ = pr[n % 2]
                    rope(DVE, POOL, p_, p_[:, 0:512], 4, csr1[0], n, rq_r, rq_r[:], tm)
                    rope(DVE, POOL, p_, p_[:, 512:1024], 4, csrs[0], n, rk_r, rk_r[:], tm2)
                    K.op(ACT, lambda e: e.activation(out=rv_bf[:], in_=p_[:, 1024:2048], func=AF.Copy), [p_], [rv_bf])
                    aq = p_[:, 3072:4096]
                    for h in range(8):
                        K.op(ACT, lambda e: e.activation(out=aqsq[:, 128 * h:128 * h + 128], in_=aq[:, 128 * h:128 * h + 128],
                                                         func=AF.Square, accum_out=sk[:, h:h + 1]), [p_], [aqsq, sk])
                    rstd_from(None, sk, sk[:, 0:8], 1.0 / 128, sk, sk[:, 8:16], sk, sk[:, 16:24])
                    for h in range(8):
                        K.op(DVE, lambda e: e.scalar_tensor_tensor(
                            out=aqn[:, 128 * h:128 * h + 128], in0=aq[:, 128 * h:128 * h + 128], scalar=sk[:, 16 + h:17 + h],
                            in1=C("qg"), op0=ALU.mult, op1=ALU.mult), [p_, sk, cst], [aqn])
                    rope(DVE, POOL, aqn, aqn[:], 8, csa[0], n, aq_r, aq_r[:], tm3)
                    for h in range(4):
                        K.op(PE, lambda e: e.transpose(out=ps_t1[:, 128 * h:128 * h + 128], in_=rq_r[:, 128 * h:128 * h + 128],
                                                       identity=ident[:]), [rq_r, ident], [ps_t1], selfsync=False)
                    for h in range(4):
                        K.op(PE, lambda e: e.transpose(out=ps_t1[:, 512 + 128 * h:640 + 128 * h],
                                                       in_=rk_r[:, 128 * h:128 * h + 128], identity=ident[:]),
                             [rk_r, ident], [ps_t1], selfsync=False)
                    for h in range(8):
                        K.op(PE, lambda e: e.transpose(out=ps_t2[:, 128 * h:128 * h + 128], in_=aq_r[:, 128 * h:128 * h + 128],
                                                       identity=ident[:]), [aq_r, ident], [ps_t2], selfsync=False)
                    rb = rqbT[n % 2]
                    K.op(ACT, lambda e: e.activation(out=rqT[:], in_=ps_t1[:, 0:512], func=AF.Copy), [ps_t1], [rqT])
                    K.op(DVE, lambda e: e.tensor_tensor(out=rqfT[:], in0=ps_t1[:, 0:512],
                                                        in1=qdf[:].rearrange("p h c -> p (h c)"), op=ALU.mult),
                         [ps_t1, qdf], [rqfT])
                    K.op(DVE, lambda e: e.tensor_tensor(out=rb[:], in0=ps_t1[:, 0:512],
                                                        in1=qdb[:].rearrange("p h c -> p (h c)"), op=ALU.mult),
                         [ps_t1, qdb], [rb])
                    K.op(ACT, lambda e: e.activation(out=rkT[:], in_=ps_t1[:, 512:1024], func=AF.Copy), [ps_t1], [rkT])
                    at = aqT[n % 2]
                    K.op(ACT, lambda e: e.activation(out=at[:], in_=ps_t2[:], func=AF.Copy), [ps_t2], [at])
                    K.dma(SP, QT, QT[b], at, at[:].rearrange("p (h t) -> p h t", h=8), chk_dst=False)
                    K.dma(SP, RQB, RQB[b], rb, rb[:], chk_dst=False)
                    for h in range(4):
                        K.op(PE, lambda e: e.matmul(ps_p[:, 128 * h:128 * h + 128], lhsT=rkT[:, 128 * h:128 * h + 128],
                                                    rhs=rqT[:, 128 * h:128 * h + 128], start=True, stop=True),
                             [rkT, rqT], [ps_p], selfsync=False)
                    K.op(DVE, lambda e: e.tensor_tensor(out=PT[:], in0=ps_p[:], in1=MT[:].rearrange("p h c -> p (h c)"),
                                                        op=ALU.mult), [ps_p, MT], [PT])
                    for h in range(4):
                        po = ps_o[h // 2]
                        osl = po[:, 256 * (h % 2):256 * (h % 2) + 256]
                        K.op(PE, lambda e: e.matmul(osl, lhsT=PT[:, 128 * h:128 * h + 128], rhs=rv_bf[:, 256 * h:256 * h + 256],
                                                    start=True, stop=False), [PT, rv_bf], [po], selfsync=False)
                        K.op(PE, lambda e: e.matmul(osl, lhsT=rqfT[:, 128 * h:128 * h + 128], rhs=Sf_bf[:, 256 * h:256 * h + 256],
                                                    start=False, stop=True), [rqfT, Sf_bf], [po], selfsync=False)
                    ros = ro_s[n % 2]
                    K.op(ACT, lambda e: e.activation(out=ros[:, 0:512], in_=ps_o[0][:], func=AF.Copy), [ps_o[0]], [ros])
                    K.op(ACT, lambda e: e.activation(out=ros[:, 512:1024], in_=ps_o[1][:], func=AF.Copy), [ps_o[1]], [ros])
                    K.dma(SP, RO, RO[b], ros, ros[:], chk_dst=False)
                    for d, kd in enumerate((kdf, kdb)):
                        K.op(POOL, lambda e: e.tensor_tensor(
                            out=kw[d][:].rearrange("p (h x) -> p h x", h=4), in0=rk_r[:].rearrange("p (h x) -> p h x", h=4),
                            in1=kd[:].unsqueeze(2).to_broadcast([128, 4, 128]), op=ALU.mult), [rk_r, kd], [kw[d]])
                    for hp in range(2):
                        pk = ps_kv[hp]
                        for hh in range(2):
                            h = 2 * hp + hh
                            K.op(PE, lambda e: e.matmul(pk[:, 256 * hh:256 * hh + 256], lhsT=kw[0][:, 128 * h:128 * h + 128],
                                                        rhs=rv_bf[:, 256 * h:256 * h + 256], start=True, stop=True),
                                 [kw[0], rv_bf], [pk], selfsync=False)
                        for hh in range(2):
                            h = 2 * hp + hh
                            K.op(DVE, lambda e: e.scalar_tensor_tensor(out=Sf[:, h, :], in0=Sf[:, h, :], scalar=cdf[:, h:h + 1],
                                                                       in1=pk[:, 256 * hh:256 * hh + 256], op0=ALU.mult,
                                                                       op1=ALU.add), [Sf, cdf, pk], [Sf])
                    K.op(ACT, lambda e: e.activation(out=Sf_bf[:], in_=Sf[:].rearrange("p h v -> p (h v)"), func=AF.Copy),
                         [Sf], [Sf_bf])
                    kvs = kvb_s[n % 2]
                    for hp in range(2):
                        pk = ps_kv[hp]
                        for hh in range(2):
                            h = 2 * hp + hh
                            K.op(PE, lambda e: e.matmul(pk[:, 256 * hh:256 * hh + 256], lhsT=kw[1][:, 128 * h:128 * h + 128],
                                                        rhs=rv_bf[:, 256 * h:256 * h + 256], start=True, stop=True),
                                 [kw[1], rv_bf], [pk], selfsync=False)
                        K.op(ACT, lambda e: e.activation(out=kvs[:, 512 * hp:512 * hp + 512], in_=pk[:], func=AF.Copy),
                             [pk], [kvs])
                    K.dma(SP, KVB, KVB[b], kvs, kvs[:], chk_dst=False)
            K.barrier()
            with contextlib.ExitStack() as st:
                ro_l = [K.sb(st, f"ro_l{i}", [128, 1024], F32, dma=True) for i in range(2)]
                rg_l = [K.sb(st, f"rg_l{i}", [128, 1024], F32, dma=True) for i in range(2)]
                rqb_l = [K.sb(st, f"rqb_l{i}", [128, 512], BF16, dma=True) for i in range(2)]
                kvb_l = [K.sb(st, f"kvb_l{i}", [128, 1024], F32, dma=True) for i in range(2)]
                Sb_bf = K.sb(st, "Sb_bf", [128, 1024], BF16)
                o32 = K.sb(st, "o32", [128, 1024], F32)
                osq = K.sb(st, "osq", [128, 1024], F32)
                sg = K.sb(st, "sg", [128, 1024], F32)
                mixr = K.sb(st, "mixr", [128, 1024], BF16)
                mT = [K.sb(st, f"mT{i}", [128, 1024], BF16) for i in range(2)]
                sk = K.sb(st, "sk", [128, 12], F32)
                ps_o = [K.ps(st, f"po{j}", [128, 512]) for j in range(2)]
                ps_t = K.ps(st, "pt", [128, 1024], BF16)

                def loadB(n):
                    b = own[n]
                    K.dma(SP, ro_l[n % 2], ro_l[n % 2][:], RO, RO[b])
                    K.dma(SP, rg_l[n % 2], rg_l[n % 2][:], PROJ, PROJ[b, :, 2048:3072])
                    K.dma(SP, rqb_l[n % 2], rqb_l[n % 2][:], RQB, RQB[b])
                    if n + 1 < no:
                        K.dma(SP, kvb_l[n % 2], kvb_l[n % 2][:], KVB, KVB[own[n + 1]])

                loadB(no - 1)
                for n in range(no - 1, -1, -1):
                    b = own[n]
                    if n - 1 >= 0:
                        loadB(n - 1)
                    if n + 1 < no:
                        kv = kvb_l[n % 2]
                        for h in range(4):
                            K.op(DVE, lambda e: e.scalar_tensor_tensor(out=Sb[:, h, :], in0=Sb[:, h, :], scalar=cdb[:, h:h + 1],
                                                                       in1=kv[:, 256 * h:256 * h + 256], op0=ALU.mult,
                                                                       op1=ALU.add), [Sb, cdb, kv], [Sb])
                    K.op(ACT, lambda e: e.activation(out=Sb_bf[:], in_=Sb[:].rearrange("p h v -> p (h v)"), func=AF.Copy),
                         [Sb], [Sb_bf])
                    rb, ro, rg = rqb_l[n % 2], ro_l[n % 2], rg_l[n % 2]
                    for h in range(4):
                        po = ps_o[h // 2]
                        K.op(PE, lambda e: e.matmul(po[:, 256 * (h % 2):256 * (h % 2) + 256], lhsT=rb[:, 128 * h:128 * h + 128],
                                                    rhs=Sb_bf[:, 256 * h:256 * h + 256], start=True, stop=True),
                             [rb, Sb_bf], [po], selfsync=False)
                    for hp in range(2):
                        K.op(DVE, lambda e: e.tensor_tensor(out=o32[:, 512 * hp:512 * hp + 512], in0=ps_o[hp][:],
                                                            in1=ro[:, 512 * hp:512 * hp + 512], op=ALU.add),
                             [ps_o[hp], ro], [o32])
                    for h in range(4):
                        K.op(ACT, lambda e: e.activation(out=osq[:, 256 * h:256 * h + 256], in_=o32[:, 256 * h:256 * h + 256],
                                                         func=AF.Square, accum_out=sk[:, h:h + 1]), [o32], [osq, sk])
                    rstd_from(None, sk, sk[:, 0:4], 1.0 / 256, sk, sk[:, 4:8], sk, sk[:, 8:12])
                    K.op(ACT, lambda e: e.activation(out=sg[:], in_=rg[:], func=AF.Silu), [rg], [sg])
                    for h in range(4):
                        K.op(DVE, lambda e: e.scalar_tensor_tensor(
                            out=mixr[:, 256 * h:256 * h + 256], in0=o32[:, 256 * h:256 * h + 256], scalar=sk[:, 8 + h:9 + h],
                            in1=sg[:, 256 * h:256 * h + 256], op0=ALU.mult, op1=ALU.mult), [o32, sk, sg], [mixr])
                    for c in range(8):
                        K.op(PE, lambda e: e.transpose(out=ps_t[:, 128 * c:128 * c + 128], in_=mixr[:, 128 * c:128 * c + 128],
                                                       identity=ident[:]), [mixr, ident], [ps_t], selfsync=False)
                    mt = mT[n % 2]
                    K.op(ACT, lambda e: e.activation(out=mt[:], in_=ps_t[:], func=AF.Copy), [ps_t], [mt])
                    K.dma(SP, MIXT, MIXT[:, 0:8, b * 128:(b + 1) * 128], mt, mt[:].rearrange("p (c t) -> p c t", c=8),
                          chk_dst=False)
            K.barrier()

        for si, s in enumerate(cfg.seqs):
            own = list(range(s["own0"], s["own0"] + s["nown"]))
            nb = s["nb"]
            with contextlib.ExitStack() as st:
                KT = K.sb(st, "KT", [128, 2, nb * 128], BF16, dma=True)
                V = K.sb(st, "V", [128, nb, 256], BF16, dma=True)
                npc = 4 if nb >= 8 else 1
                bounds = [nb * q // npc for q in range(npc + 1)]
                for q in range(npc):
                    a, b_ = bounds[q], bounds[q + 1]
                    K.dma(SP, KT, KT[:, :, a * 128:b_ * 128], KTs[si], KTs[si][:, :, a * 128:b_ * 128], chk_dst=(q == 0))
                    K.dma(SP, V, V[:, a:b_, :], Vs[si], Vs[si][:, a:b_, :], chk_dst=(q == 0))
                qT = [K.sb(st, f"qT{i}", [128, 8, 128], BF16, dma=True) for i in range(2)]
                late2 = None
                if si == 1:
                    stg2 = [K.sb(st, f"stgm{i}", [128, 2048], F32, dma=True) for i in range(2)]
                    wbf2 = [K.sb(st, f"wbfm{i}", [128, 2048], BF16) for i in range(2)]
                    cast_engs[0] = [0, 2]
                    late2 = run_units(late_units[KC:], stg2, wbf2, SP)
                NPS, NPT, LOOK = 3, 4, 2
                ps_s = [K.ps(st, f"pss{j}", [128, 512]) for j in range(NPS)]
                ps_ot = [K.ps(st, f"pso{j}", [128, 512]) for j in range(2)]
                ps_dn = [K.ps(st, f"psd{j}", [128, 512]) for j in range(2)]
                PTs = [K.sb(st, f"PTa{i}", [128, 512], BF16) for i in range(NPT)]
                rden = K.sb(st, "rden", [128, 512], F32)
                ma = [K.sb(st, f"ma{i}", [128, 512], BF16) for i in range(2)]
                K.dma(SP, qT[0], qT[0][:], QT, QT[own[0]])
                pairs = [(n, b, kvh, kb) for n, b in enumerate(own) for kvh in range(2) for kb in range(nb)]

                def emit_S(j):
                    n, b, kvh, kb = pairs[j]
                    if kvh == 0 and kb == 0 and n + 1 < len(own):
                        K.dma(SP, qT[(n + 1) % 2], qT[(n + 1) % 2][:], QT, QT[own[n + 1]])
                    q_ = qT[n % 2]
                    qr = q_[:, 4 * kvh:4 * kvh + 4, :].rearrange("p h t -> p (h t)")
                    nk = 16 if kb == 0 else 128
                    pss, pt = ps_s[j % NPS], PTs[j % NPT]
                    K.op(PE, lambda e: e.matmul(pss[0:nk, :], lhsT=KT[:, kvh, kb * 128:kb * 128 + nk], rhs=qr,
                                                start=True, stop=True), [KT, q_], [pss], selfsync=False)
                    K.op(ACT, lambda e: e.activation(out=pt[0:nk, :], in_=pss[0:nk, :], func=AF.Exp,
                                                     bias=negB[0:nk, 0:1]), [pss, negB], [pt])

                def emit_PV(j):
                    n, b, kvh, kb = pairs[j]
                    nk = 16 if kb == 0 else 128
                    pt = PTs[j % NPT]
                    pot, pdn = ps_ot[kvh], ps_dn[kvh]
                    K.op(PE, lambda e: e.matmul(pot[:], lhsT=V[0:nk, kb, 128 * kvh:128 * kvh + 128], rhs=pt[0:nk, :],
                                                start=(kb == 0), stop=(kb == nb - 1)), [V, pt], [pot], selfsync=False)
                    K.op(PE, lambda e: e.matmul(pdn[:], lhsT=ones[0:nk, :], rhs=pt[0:nk, :],
                                                start=(kb == 0), stop=(kb == nb - 1)), [ones, pt], [pdn], selfsync=False)
                    if kb == nb - 1:
                        K.op(DVE, lambda e: e.reciprocal(out=rden[:], in_=pdn[:]), [pdn], [rden])
                        m_ = ma[kvh]
                        K.op(DVE, lambda e: e.tensor_tensor(out=m_[:], in0=pot[:], in1=rden[:], op=ALU.mult), [pot, rden], [m_])
                        K.dma(SP, MIXT, MIXT[:, 8 + 4 * kvh:12 + 4 * kvh, b * 128:(b + 1) * 128], m_,
                              m_[:].rearrange("p (h t) -> p h t", h=4), chk_dst=False)

                for j in range(min(LOOK, len(pairs))):
                    emit_S(j)
                every = max(1, len(pairs) // (len(late_units) - KC + 1))
                for j in range(len(pairs)):
                    if j + LOOK < len(pairs):
                        emit_S(j + LOOK)
                    emit_PV(j)
                    if late2 is not None and j % every == every - 1:
                        next(late2, None)
                if late2 is not None:
                    for _ in late2:
                        pass
                    cast_engs[0] = [0, 1, 2]
            K.barrier()

        NT = NOWN // 4
        with contextlib.ExitStack() as st:
            hTs = [K.sb(st, f"hT{dc}", [128, 512], F32, dma=True) for dc in range(KC)]
            actT = K.sb(st, "actT", [128, KC, 512], BF16, dma=True)
            aT = K.sb(st, "aT", [128, 64, 512], BF16)
            NWB = 3
            wb = [K.sb(st, f"wb{i}", [128, 8192], BF16, dma=True) for i in range(NWB)]
            rt = K.sb(st, "rt", [128, 512], F32)
            rstd = K.sb(st, "rstd", [128, 512], F32)
            rl = [K.sb(st, f"rl{i}", [128, 512], F32) for i in range(2)]
            pm = [K.ps(st, f"pm{j}", [128, 512]) for j in range(4)]
            ps_stat = K.ps(st, "pstat", [128, 512])
            slabs = []
            for t in range(NT):
                for g in range(4):
                    slabs.append((Wouts, Wouts[g].rearrange("p k j -> p (k j)")))
                for g in range(16):
                    slabs.append((Wups, Wups[g].rearrange("p k j -> p (k j)")))
                for g in range(16):
                    slabs.append((Wdowns, Wdowns[g].rearrange("p f j -> p (f j)")))
            wi = [0]

            def issue_slab():
                i = wi[0]
                if i < len(slabs):
                    K.dma(SP, wb[i % NWB], wb[i % NWB][:], slabs[i][0], slabs[i][1])
                wi[0] += 1

            used = [0]

            def next_slab():
                i = used[0]
                used[0] += 1
                return wb[i % NWB]

            for _ in range(NWB - 1):
                issue_slab()
            cnt = 0
            for t in range(NT):
                tok = slice(t * 512, (t + 1) * 512)
                if t == 0:
                    K.dma(SP, actT, actT[:], MIXT, MIXT[:, :, tok])
                for dc in range(KC):
                    K.dma(SP, hTs[dc], hTs[dc][:].rearrange("p (b t) -> p b t", b=4), xown,
                          xown[4 * t:4 * t + 4, :, dc, :].rearrange("b p t -> p b t"))
                for g in range(4):
                    issue_slab()
                    w = next_slab()
                    wv = w[:].rearrange("p (k j) -> p k j", k=KC)
                    for dd in range(4):
                        dc = 4 * g + dd
                        p_ = pm[cnt % 4]
                        cnt += 1
                        for kc in range(KC):
                            K.op(PE, lambda e: e.matmul(p_[:], lhsT=wv[:, kc, 128 * dd:128 * dd + 128], rhs=actT[:, kc, :],
                                                        start=(kc == 0), stop=(kc == KC - 1)), [w, actT], [p_], selfsync=False)
                        K.op(DVE, lambda e: e.tensor_tensor(out=hTs[dc][:], in0=p_[:], in1=hTs[dc][:], op=ALU.add),
                             [p_, hTs[dc]], [hTs[dc]])
                for dc in range(KC):
                    K.op(ACT, lambda e: e.activation(out=actT[:, dc, :], in_=hTs[dc][:], func=AF.Square), [hTs[dc]], [actT])
                for kc in range(KC):
                    K.op(PE, lambda e: e.matmul(ps_stat[:], lhsT=ones[:], rhs=actT[:, kc, :], start=(kc == 0),
                                                stop=(kc == KC - 1)), [ones, actT], [ps_stat], selfsync=False)
                rstd_from(None, ps_stat, ps_stat[:], 1.0 / D, rt, rt[:], rstd, rstd[:])
                for dc in range(KC):
                    K.op(POOL if dc % 3 == 2 else DVE, lambda e: e.tensor_tensor(out=actT[:, dc, :], in0=hTs[dc][:], in1=rstd[:],
                                                                                  op=ALU.mult), [hTs[dc], rstd], [actT])
                for g in range(16):
                    issue_slab()
                    w = next_slab()
                    wv = w[:].rearrange("p (k j) -> p k j", k=KC)
                    for ff in range(4):
                        fc = 4 * g + ff
                        p_ = pm[cnt % 4]
                        r_ = rl[cnt % 2]
                        cnt += 1
                        for kc in range(KC):
                            K.op(PE, lambda e: e.matmul(p_[:], lhsT=wv[:, kc, 128 * ff:128 * ff + 128], rhs=actT[:, kc, :],
                                                        start=(kc == 0), stop=(kc == KC - 1)), [w, actT], [p_], selfsync=False)
                        K.op(ACT, lambda e: e.activation(out=r_[:], in_=p_[:], func=AF.Relu), [p_], [r_])
                        K.op(POOL if fc % 2 == 0 else DVE, lambda e: e.tensor_tensor(out=aT[:, fc, :], in0=r_[:], in1=r_[:],
                                                                                      op=ALU.mult), [r_], [aT])
                for dc in range(16):
                    issue_slab()
                    w = next_slab()
                    wv = w[:].rearrange("p (f j) -> p f j", f=64)
                    p_ = pm[cnt % 4]
                    cnt += 1
                    for fc in range(64):
                        K.op(PE, lambda e: e.matmul(p_[:], lhsT=wv[:, fc, :], rhs=aT[:, fc, :], start=(fc == 0),
                                                    stop=(fc == 63)), [w, aT], [p_], selfsync=False)
                    K.op(DVE, lambda e: e.tensor_tensor(out=hTs[dc][:], in0=p_[:], in1=hTs[dc][:], op=ALU.add),
                         [p_, hTs[dc]], [hTs[dc]])
                for dc in range(KC):
                    K.op(ACT, lambda e: e.activation(out=actT[:, dc, :], in_=hTs[dc][:], func=AF.Square), [hTs[dc]], [actT])
                for kc in range(KC):
                    K.op(PE, lambda e: e.matmul(ps_stat[:], lhsT=ones[:], rhs=actT[:, kc, :], start=(kc == 0),
                                                stop=(kc == KC - 1)), [ones, actT], [ps_stat], selfsync=False)
                rstd_from(None, ps_stat, ps_stat[:], 1.0 / D, rt, rt[:], rstd, rstd[:])
                if t + 1 < NT:
                    K.dma(SP, actT, actT[:], MIXT, MIXT[:, :, (t + 1) * 512:(t + 2) * 512])
                gfc = C("gfc")
                for dc in range(16):
                    K.op(DVE, lambda e: e.scalar_tensor_tensor(
                        out=hTs[dc][:], in0=hTs[dc][:], scalar=gfc[:, dc:dc + 1], in1=rstd[:], op0=ALU.mult,
                        op1=ALU.mult), [hTs[dc], cst, rstd], [hTs[dc]])
                    K.dma(SP, yT, yT[:, dc, tok], hTs[dc], hTs[dc][:], chk_dst=False)
            K.barrier([SP])
    K.es.close()
    return nc


N_META = 16
GRID_W = 64
ROPE_THETA = 10000.0


def _blocks_T(h):
    nb = h.shape[0] // 128
    return np.ascontiguousarray(h.reshape(nb, 128, KC, 128).transpose(0, 3, 2, 1))


def _consts(q_g, k_g, lgf, lgb, g1, g2, gf):
    c = np.zeros((128, NCST), np.float32)

    def put(name, arr):
        o, w = C_OFF[name]
        c[:, o:o + w] = arr

    idx = np.arange(128, dtype=np.float32)
    dmat = idx[None, :] - idx[:, None]
    put("ident", np.eye(128, dtype=np.float32))
    put("dp", np.maximum(dmat, 0.0))
    put("dn", np.maximum(-dmat, 0.0))
    put("cp1", np.broadcast_to(idx[None, :] + 1.0, (128, 128)))
    put("c128m", np.broadcast_to(128.0 - idx[None, :], (128, 128)))
    rfreq = (np.float32(ROPE_THETA) ** (-np.linspace(0.0, 1.0, 64, dtype=np.float32))).astype(np.float32)
    afreq = (np.float32(ROPE_THETA) ** (-np.arange(32, dtype=np.float32) / np.float32(32))).astype(np.float32)
    put("rfreq", np.broadcast_to(rfreq[None], (128, 64)))
    put("afreq", np.broadcast_to(afreq[None], (128, 32)))
    put("g1c", g1.reshape(KC, 128).T)
    put("g2c", g2.reshape(KC, 128).T)
    put("gfc", gf.reshape(KC, 128).T)
    put("qg", np.broadcast_to(q_g[None], (128, 128)))
    put("kg", np.broadcast_to(k_g[None], (128, 128)))
    put("lgf", np.broadcast_to(lgf[None], (128, 4)))
    put("lgb", np.broadcast_to(lgb[None], (128, 4)))
    put("colc", idx[:, None])
    put("col127", 127.0 - idx[:, None])
    return c


def _seq_meta(nb_real):
    L = (nb_real + 1) * 128
    pos = np.zeros(L, np.float32)
    row = np.zeros(L, np.float32)
    col = np.zeros(L, np.float32)
    valid = np.zeros(L, np.float32)
    pos[0:16] = 112 + np.arange(16)
    valid[0:16] = 1
    j = np.arange(nb_real * 128)
    pos[128:] = 128 + j
    row[128:] = j // GRID_W
    col[128:] = j % GRID_W
    valid[128:] = 1
    return pos, row, col, valid


def prepare(cfg, x_prompt, x_sample, meta_tokens, ln1_g, w_in, q_norm_g, k_norm_g, ret_log_decay_fwd,
            ret_log_decay_bwd, w_out, ln2_g, w_up, w_down, final_norm_g):
    f32 = np.float32
    cst = _consts(np.asarray(q_norm_g[0], f32), np.asarray(k_norm_g[0], f32), np.asarray(ret_log_decay_fwd[0], f32),
                  np.asarray(ret_log_decay_bwd[0], f32), np.asarray(ln1_g[0], f32), np.asarray(ln2_g[0], f32),
                  np.asarray(final_norm_g, f32))
    mblk = np.zeros((128, D), f32)
    mblk[0:16] = meta_tokens
    mblkT = _blocks_T(mblk)
    xs_T = _blocks_T(np.asarray(x_sample[0], f32))
    xp_T = [_blocks_T(np.asarray(x_prompt[b], f32)) for b in range(x_prompt.shape[0])]
    shared = dict(cst=cst, w_in=np.ascontiguousarray(w_in[0], f32), w_out=np.ascontiguousarray(w_out[0], f32),
                  w_up=np.ascontiguousarray(w_up[0], f32), w_down=np.ascontiguousarray(w_down[0], f32))
    in_maps = []
    for c in range(8):
        pb, ph = c // 2, c % 2
        xb1 = np.concatenate([mblkT, xp_T[pb], mblkT, xs_T], axis=0)
        own_p = list(range(ph * cfg.OWN_P, (ph + 1) * cfg.OWN_P))
        own_s = list(range(c * cfg.OWN_S, (c + 1) * cfg.OWN_S))
        xown = np.concatenate([xp_T[pb][own_p], xs_T[own_s]], axis=0)
        metas, metao = [], []
        for (nbr, ownl) in ((cfg.NBP, own_p), (cfg.NBS, own_s)):
            pos, row, col, valid = _seq_meta(nbr)
            start = 128.0 + ownl[0] * 128
            end = 128.0 + (ownl[-1] + 1) * 128
            mf = ((pos < start) & (valid > 0)).astype(f32)
            mb = ((pos >= end) & (valid > 0)).astype(f32)
            df = np.where(mf > 0, start - 1.0 - pos, 0.0).astype(f32)
            db = np.where(mb > 0, pos - end, 0.0).astype(f32)
            m = np.stack([pos, row, col, df, mf, db, mb], axis=-1).reshape(nbr + 1, 128, 7)
            metas.append(m)
            osl = [1 + o for o in ownl]
            metao.append(m[osl][:, :, 0:3])
        meta1 = np.ascontiguousarray(np.concatenate(metas, axis=0).transpose(1, 0, 2))
        metao = np.ascontiguousarray(np.concatenate(metao, axis=0).transpose(1, 0, 2))
        d = dict(xb1=np.ascontiguousarray(xb1), xown=np.ascontiguousarray(xown), meta1=meta1, metao=metao)
        d.update(shared)
        in_maps.append(d)
    return in_maps


def assemble(cfg, results, nbatch):
    yp = np.zeros((nbatch, cfg.NBP * 128, D), np.float32)
    ys = np.zeros((1, cfg.NBS * 128, D), np.float32)
    for c in range(8):
        y = results[c]["yT"]
        y = y.transpose(2, 1, 0).reshape(cfg.NOWN * 128, D)
        pb, ph = c // 2, c % 2
        np_ = cfg.OWN_P * 128
        yp[pb, ph * np_:(ph + 1) * np_] = y[0:np_]
        ns_ = cfg.OWN_S * 128
        ys[0, c * ns_:(c + 1) * ns_] = y[np_:np_ + ns_]
    return yp, ys


_NC_CACHE = {}


def kernel(x_prompt, x_sample, meta_tokens, ln1_g, w_in, q_norm_g, k_norm_g, ret_log_decay_fwd,
           ret_log_decay_bwd, w_out, ln2_g, w_up, w_down, final_norm_g):
    cfg = Cfg(16, 128)
    args = [np.asarray(a) for a in (x_prompt, x_sample, meta_tokens, ln1_g, w_in, q_norm_g, k_norm_g,
                                    ret_log_decay_fwd, ret_log_decay_bwd, w_out, ln2_g, w_up, w_down, final_norm_g)]
    in_maps = prepare(cfg, *args)
    nc = build(cfg)
    res = run_bass_kernel_spmd(nc, in_maps, core_ids=list(range(8)))
    yp, ys = assemble(cfg, res.results, args[0].shape[0])
    return (yp, ys)
```
